# Optimizing a Trainium2 kernel written in Bass

```python
import math
import jax, jax.numpy as jnp
from jax import lax
import numpy as np


D_MODEL = 1024
BATCH = 4
SEQ = 8192
DEPTH = 4

CTX_LEN = 256
GRID_W = 64
D_FF = 2816
MLA_HEADS = 8
MLA_NOPE = 64
MLA_ROPE = 32
MLA_V = 64
MLA_Q_RANK = 384
MLA_KV_RANK = 256
SSM_WIDTH = 512
SSM_GROUP = 16
SSM_GROUPS = SSM_WIDTH // SSM_GROUP
SSM_STATE = 64
DT_MIN = 1e-3
DT_MAX = 1e-1
GQA_HEADS = 8
GQA_KV_HEADS = 2
GQA_HEAD_DIM = 64
WINDOW = 128
BLOCK = 128
N_BRANCH = 3
N_MOD = 9
ROPE_BASE = 10000.0
EPS = 1e-6
NEG_INF = -1e30
IN_SPLITS = (MLA_Q_RANK, MLA_KV_RANK, MLA_ROPE, SSM_WIDTH, GQA_HEADS * GQA_HEAD_DIM, GQA_KV_HEADS * GQA_HEAD_DIM, GQA_KV_HEADS * GQA_HEAD_DIM, N_BRANCH * D_MODEL)
IN_DIM = sum(IN_SPLITS)

kernel_name = 'hybrid_mla_s5_swa_dit_block'


def _offsets():
    return np.cumsum(IN_SPLITS)[:-1].tolist()


def bcast(t):
    return t[..., None, :]


def rmsnorm(x, g):
    x32 = x.astype(jnp.float32)
    y = x32 * lax.rsqrt(jnp.mean(x32 * x32, axis=-1, keepdims=True) + EPS)
    return (y * g.astype(jnp.float32)).astype(x.dtype)


def modulate(x, shift, scale):
    return x * (1 + bcast(scale)) + bcast(shift)


def swiglu(x, w13, w2):
    a, b = jnp.split(x @ w13, 2, axis=-1)
    return (jax.nn.silu(a) * b) @ w2


def rope_1d(x, pos):
    n = x.shape[-1]
    inv = ROPE_BASE ** (-jnp.arange(0, n, 2, dtype=jnp.float32) / n)
    ang = pos.astype(jnp.float32)[:, None, None] * inv
    cos, sin = jnp.cos(ang), jnp.sin(ang)
    x32 = x.astype(jnp.float32)
    x1, x2 = x32[..., : n // 2], x32[..., n // 2:]
    return jnp.concatenate([x1 * cos - x2 * sin, x1 * sin + x2 * cos], axis=-1).astype(x.dtype)


def axial_rope(x, row, col):
    half = x.shape[-1] // 2
    return jnp.concatenate([rope_1d(x[..., :half], row), rope_1d(x[..., half:], col)], axis=-1)


def sink_softmax(score_list, sink_logit):
    m = sink_logit
    for s in score_list:
        m = jnp.maximum(m, s.max(axis=-1, keepdims=True))
    e = [jnp.exp(s - m) for s in score_list]
    denom = jnp.exp(sink_logit - m)
    for t in e:
        denom = denom + t.sum(axis=-1, keepdims=True)
    return [t / denom for t in e]


def dense_attention_blocks(q, k, v):
    B, T, H, dk = q.shape
    dv = v.shape[-1]
    nb = T // BLOCK
    scale = dk ** -0.5
    qb = jnp.moveaxis(q.reshape(B, nb, BLOCK, H, dk), 1, 0)

    def one_block(qi):
        s = jnp.einsum('bqhd,bkhd->bhqk', qi, k, preferred_element_type=jnp.float32) * scale
        p = jax.nn.softmax(s, axis=-1).astype(v.dtype)
        return jnp.einsum('bhqk,bkhd->bqhd', p, v)

    o = lax.map(one_block, qb)
    return jnp.moveaxis(o, 0, 1).reshape(B, T, H * dv)


def mla_queries(cq, q_norm, w_uq, row, col):
    B, T, _ = cq.shape
    q = (rmsnorm(cq, q_norm) @ w_uq).reshape(B, T, MLA_HEADS, MLA_NOPE + MLA_ROPE)
    q_nope, q_rope = q[..., :MLA_NOPE], q[..., MLA_NOPE:]
    if row is not None:
        q_rope = axial_rope(q_rope, row, col)
    return jnp.concatenate([q_nope, q_rope], axis=-1)


def mla_keys_values(ckv, kr, kv_norm, w_ukv, row, col):
    B, T, _ = ckv.shape
    kv = (rmsnorm(ckv, kv_norm) @ w_ukv).reshape(B, T, MLA_HEADS, MLA_NOPE + MLA_V)
    k_nope, v = kv[..., :MLA_NOPE], kv[..., MLA_NOPE:]
    kr = kr[:, :, None, :]
    if row is not None:
        kr = axial_rope(kr, row, col)
    k = jnp.concatenate([k_nope, jnp.broadcast_to(kr, (B, T, MLA_HEADS, MLA_ROPE))], axis=-1)
    return k, v


def window_gqa(q, k, v, kc, vc, sink):
    B, T, H, d = q.shape
    G = H // GQA_KV_HEADS
    nb = T // BLOCK
    scale = d ** -0.5
    qb = q.reshape(B, nb, BLOCK, GQA_KV_HEADS, G, d)

    def band(t):
        tb = t.reshape(B, nb, BLOCK, GQA_KV_HEADS, d)
        tp = jnp.pad(tb, ((0, 0), (1, 1), (0, 0), (0, 0), (0, 0)))
        return jnp.concatenate([tp[:, :-2], tp[:, 1:-1], tp[:, 2:]], axis=2)

    kb, vb = band(k), band(v)
    s_band = jnp.einsum('bnqhgd,bnkhd->bhgnqk', qb, kb, preferred_element_type=jnp.float32) * scale
    blk = jnp.arange(nb)[:, None, None]
    qpos = blk * BLOCK + jnp.arange(BLOCK)[None, :, None]
    kpos = (blk - 1) * BLOCK + jnp.arange(3 * BLOCK)[None, None, :]
    valid = (jnp.abs(qpos - kpos) <= WINDOW) & (kpos >= 0) & (kpos < T)
    s_band = jnp.where(valid, s_band, NEG_INF)
    s_ctx = jnp.einsum('bnqhgd,bchd->bhgnqc', qb, kc, preferred_element_type=jnp.float32) * scale
    sk = sink.astype(jnp.float32).reshape(GQA_KV_HEADS, G)[None, :, :, None, None, None]
    p_band, p_ctx = sink_softmax([s_band, s_ctx], sk)
    o = (jnp.einsum('bhgnqk,bnkhd->bnqhgd', p_band.astype(v.dtype), vb)
         + jnp.einsum('bhgnqc,bchd->bnqhgd', p_ctx.astype(vc.dtype), vc))
    return o.reshape(B, T, H * d)


def context_gqa(qc, kc, vc, sink):
    B, C, H, d = qc.shape
    G = H // GQA_KV_HEADS
    qg = qc.reshape(B, C, GQA_KV_HEADS, G, d)
    s = jnp.einsum('bqhgd,bkhd->bhgqk', qg, kc, preferred_element_type=jnp.float32) * d ** -0.5
    sk = sink.astype(jnp.float32).reshape(GQA_KV_HEADS, G)[None, :, :, None, None]
    (p,) = sink_softmax([s], sk)
    o = jnp.einsum('bhgqk,bkhd->bqhgd', p.astype(vc.dtype), vc)
    return o.reshape(B, C, H * d)


def cmul(ar, ai, br, bi):
    return ar * br - ai * bi, ar * bi + ai * br


def ssm_discretize(lam_re, lam_im, log_dt, b_re, b_im):
    lr, li = lam_re.astype(jnp.float32), lam_im.astype(jnp.float32)
    dt = jnp.exp(log_dt.astype(jnp.float32))[:, None]
    mag = jnp.exp(lr * dt)
    a_re, a_im = mag * jnp.cos(li * dt), mag * jnp.sin(li * dt)
    den = lr * lr + li * li
    w_re = ((a_re - 1) * lr + a_im * li) / den
    w_im = (a_im * lr - (a_re - 1) * li) / den
    bb_re, bb_im = cmul(w_re[..., None], w_im[..., None], b_re.astype(jnp.float32), b_im.astype(jnp.float32))
    return a_re, a_im, bb_re, bb_im


def diag_scan(a_re, a_im, b_re, b_im, h0, reverse):
    if reverse:
        b_re, b_im = jnp.flip(b_re, axis=1), jnp.flip(b_im, axis=1)
    if h0 is not None:
        i_re, i_im = cmul(a_re, a_im, h0[0], h0[1])
        b_re = b_re.at[:, 0].add(i_re)
        b_im = b_im.at[:, 0].add(i_im)
    T = b_re.shape[1]
    ar = jnp.broadcast_to(a_re, (1, T) + a_re.shape)
    ai = jnp.broadcast_to(a_im, (1, T) + a_im.shape)

    def combine(e1, e2):
        a1r, a1i, b1r, b1i = e1
        a2r, a2i, b2r, b2i = e2
        nar, nai = cmul(a2r, a2i, a1r, a1i)
        nbr, nbi = cmul(a2r, a2i, b1r, b1i)
        return nar, nai, nbr + b2r, nbi + b2i

    _, _, s_re, s_im = lax.associative_scan(combine, (ar, ai, b_re, b_im), axis=1)
    if reverse:
        s_re, s_im = jnp.flip(s_re, axis=1), jnp.flip(s_im, axis=1)
    return s_re, s_im


def ssm_branch(u_lat, u_ctx, lam_re, lam_im, log_dt, b_re, b_im, c_re, c_im, d_skip, w_glu, ctx_out):
    dtype = u_lat.dtype

    def to_groups(u):
        return u.astype(jnp.float32).reshape(u.shape[0], u.shape[1], SSM_GROUPS, SSM_GROUP)

    ul, uc = to_groups(u_lat), to_groups(u_ctx)
    d_g = d_skip.astype(jnp.float32).reshape(SSM_GROUPS, SSM_GROUP)
    y_lat = ul * d_g
    y_ctx = uc * d_g if ctx_out else None
    for direction in range(2):
        reverse = direction == 1
        a_re, a_im, bb_re, bb_im = ssm_discretize(lam_re[direction], lam_im[direction], log_dt[direction], b_re[direction], b_im[direction])
        c_r, c_i = c_re[direction].astype(jnp.float32), c_im[direction].astype(jnp.float32)

        def drive(u):
            return jnp.einsum('btgm,gpm->btgp', u, bb_re), jnp.einsum('btgm,gpm->btgp', u, bb_im)

        def readout(sr, si):
            return jnp.einsum('btgp,gmp->btgm', sr, c_r) - jnp.einsum('btgp,gmp->btgm', si, c_i)

        sc_re, sc_im = diag_scan(a_re, a_im, *drive(uc), None, reverse)
        end = 0 if reverse else -1
        sl_re, sl_im = diag_scan(a_re, a_im, *drive(ul), (sc_re[:, end], sc_im[:, end]), reverse)
        y_lat = y_lat + readout(sl_re, sl_im)
        if ctx_out:
            y_ctx = y_ctx + readout(sc_re, sc_im)

    def glu(y):
        y = jax.nn.gelu(y).reshape(y.shape[0], y.shape[1], SSM_WIDTH).astype(dtype)
        a, g = jnp.split(y @ w_glu, 2, axis=-1)
        return a * jax.nn.sigmoid(g)

    return glu(y_lat), (glu(y_ctx) if ctx_out else None)


def mixing_sublayer(xl, xc, row, col, ctx_out, w_in, mla_q_norm, mla_kv_norm, mla_w_uq, mla_w_ukv, mla_w_o,
                    lam_re, lam_im, log_dt, b_re, b_im, c_re, c_im, d_skip, w_glu, sink, gqa_w_o, w_out):
    B, T, _ = xl.shape
    C = xc.shape[1]
    offs = _offsets()
    cq, ckv, kr, u, gq, gk, gv, gates = jnp.split(xl @ w_in, offs, axis=-1)
    w_parts = jnp.split(w_in, offs, axis=1)
    ckv_c, kr_c, u_c, gk_c, gv_c = (xc @ w_parts[i] for i in (1, 2, 3, 5, 6))
    k_c, v_c = mla_keys_values(ckv_c, kr_c, mla_kv_norm, mla_w_ukv, None, None)
    gk_c = gk_c.reshape(B, C, GQA_KV_HEADS, GQA_HEAD_DIM)
    gv_c = gv_c.reshape(B, C, GQA_KV_HEADS, GQA_HEAD_DIM)
    q = mla_queries(cq, mla_q_norm, mla_w_uq, row, col)
    k, v = mla_keys_values(ckv, kr, mla_kv_norm, mla_w_ukv, row, col)
    mla_lat = dense_attention_blocks(q, jnp.concatenate([k_c, k], axis=1), jnp.concatenate([v_c, v], axis=1)) @ mla_w_o
    ssm_lat, ssm_ctx = ssm_branch(u, u_c, lam_re, lam_im, log_dt, b_re, b_im, c_re, c_im, d_skip, w_glu, ctx_out)
    gq = axial_rope(gq.reshape(B, T, GQA_HEADS, GQA_HEAD_DIM), row, col)
    gk = axial_rope(gk.reshape(B, T, GQA_KV_HEADS, GQA_HEAD_DIM), row, col)
    gv = gv.reshape(B, T, GQA_KV_HEADS, GQA_HEAD_DIM)
    gqa_lat = window_gqa(gq, gk, gv, gk_c, gv_c, sink) @ gqa_w_o

    def merge(gate_logits, b0, b1, b2):
        g0, g1, g2 = jnp.split(jax.nn.sigmoid(gate_logits), N_BRANCH, axis=-1)
        return (g0 * b0 + g1 * b1 + g2 * b2) @ w_out

    out_lat = merge(gates, mla_lat, ssm_lat, gqa_lat)
    if not ctx_out:
        return out_lat, None
    cq_c, gq_c, gates_c = (xc @ w_parts[i] for i in (0, 4, 7))
    q_c = mla_queries(cq_c, mla_q_norm, mla_w_uq, None, None)
    mla_ctx = dense_attention_blocks(q_c, k_c, v_c) @ mla_w_o
    gqa_ctx = context_gqa(gq_c.reshape(B, C, GQA_HEADS, GQA_HEAD_DIM), gk_c, gv_c, sink) @ gqa_w_o
    out_ctx = merge(gates_c, mla_ctx, ssm_ctx, gqa_ctx)
    return out_lat, out_ctx


def setup_inputs(seed: int = 0) -> dict:
    key = jax.random.key(seed)
    ks = list(jax.random.split(key, 32))
    f32 = jnp.float32

    def nrm(shape, scale):
        return jax.random.normal(ks.pop(), shape, f32) * scale

    D, F, Lr = D_MODEL, D_FF, DEPTH
    G, P, M = SSM_GROUPS, SSM_STATE, SSM_GROUP
    inp = {}
    inp['x'] = nrm((BATCH, SEQ, D), 1.0)
    inp['c'] = nrm((BATCH, D), 1.0)
    inp['ctx'] = nrm((BATCH, CTX_LEN, D), 1.0)
    inp['c_ctx'] = nrm((D,), 1.0)
    inp['ada_w'] = nrm((Lr, D, N_MOD * D), 0.5 * D ** -0.5)
    inp['ada_b'] = nrm((Lr, N_MOD * D), 0.01)
    inp['norm_ffn1'] = 1.0 + nrm((Lr, D), 0.01)
    inp['norm_mix'] = 1.0 + nrm((Lr, D), 0.01)
    inp['norm_ffn2'] = 1.0 + nrm((Lr, D), 0.01)
    inp['ffn1_w13'] = nrm((Lr, D, 2 * F), D ** -0.5)
    inp['ffn1_w2'] = nrm((Lr, F, D), F ** -0.5)
    inp['ffn2_w13'] = nrm((Lr, D, 2 * F), D ** -0.5)
    inp['ffn2_w2'] = nrm((Lr, F, D), F ** -0.5)
    inp['w_in'] = nrm((Lr, D, IN_DIM), D ** -0.5)
    inp['mla_q_norm'] = 1.0 + nrm((Lr, MLA_Q_RANK), 0.01)
    inp['mla_kv_norm'] = 1.0 + nrm((Lr, MLA_KV_RANK), 0.01)
    inp['mla_w_uq'] = nrm((Lr, MLA_Q_RANK, MLA_HEADS * (MLA_NOPE + MLA_ROPE)), MLA_Q_RANK ** -0.5)
    inp['mla_w_ukv'] = nrm((Lr, MLA_KV_RANK, MLA_HEADS * (MLA_NOPE + MLA_V)), MLA_KV_RANK ** -0.5)
    inp['mla_w_o'] = nrm((Lr, MLA_HEADS * MLA_V, D), (MLA_HEADS * MLA_V) ** -0.5)
    inp['ssm_lambda_re'] = -0.5 + nrm((Lr, 2, G, P), 0.01)
    inp['ssm_lambda_im'] = jnp.pi * jnp.arange(P, dtype=f32) + nrm((Lr, 2, G, P), 0.01)
    inp['ssm_log_dt'] = jax.random.uniform(ks.pop(), (Lr, 2, G), f32, minval=math.log(DT_MIN), maxval=math.log(DT_MAX))
    inp['ssm_b_re'] = nrm((Lr, 2, G, P, M), (2 * M) ** -0.5)
    inp['ssm_b_im'] = nrm((Lr, 2, G, P, M), (2 * M) ** -0.5)
    inp['ssm_c_re'] = nrm((Lr, 2, G, M, P), 0.5)
    inp['ssm_c_im'] = nrm((Lr, 2, G, M, P), 0.5)
    inp['ssm_d'] = nrm((Lr, SSM_WIDTH), 1.0)
    inp['ssm_w_glu'] = nrm((Lr, SSM_WIDTH, 2 * D), SSM_WIDTH ** -0.5)
    inp['gqa_sink'] = nrm((Lr, GQA_HEADS), 0.5)
    inp['gqa_w_o'] = nrm((Lr, GQA_HEADS * GQA_HEAD_DIM, D), (GQA_HEADS * GQA_HEAD_DIM) ** -0.5)
    inp['w_out'] = nrm((Lr, D, D), D ** -0.5)
    inp['final_norm'] = 1.0 + nrm((D,), 0.01)
    return inp


def reference(x, c, ctx, c_ctx, ada_w, ada_b, norm_ffn1, norm_mix, norm_ffn2, ffn1_w13, ffn1_w2, ffn2_w13, ffn2_w2,
              w_in, mla_q_norm, mla_kv_norm, mla_w_uq, mla_w_ukv, mla_w_o, ssm_lambda_re, ssm_lambda_im, ssm_log_dt,
              ssm_b_re, ssm_b_im, ssm_c_re, ssm_c_im, ssm_d, ssm_w_glu, gqa_sink, gqa_w_o, w_out, final_norm):
    L = x.shape[1]
    ROWS = L // GRID_W
    row = jnp.repeat(jnp.arange(ROWS, dtype=jnp.int32), GRID_W)
    col = jnp.tile(jnp.arange(GRID_W, dtype=jnp.int32), ROWS)
    h, hc = x, ctx
    for layer in range(DEPTH):
        ctx_out = layer < DEPTH - 1
        mod = jnp.split(jax.nn.silu(c) @ ada_w[layer] + ada_b[layer], N_MOD, axis=-1)
        mod_c = jnp.split(jax.nn.silu(c_ctx) @ ada_w[layer] + ada_b[layer], N_MOD, axis=-1)
        h = h + 0.5 * bcast(mod[2]) * swiglu(modulate(rmsnorm(h, norm_ffn1[layer]), mod[0], mod[1]), ffn1_w13[layer], ffn1_w2[layer])
        hc = hc + 0.5 * bcast(mod_c[2]) * swiglu(modulate(rmsnorm(hc, norm_ffn1[layer]), mod_c[0], mod_c[1]), ffn1_w13[layer], ffn1_w2[layer])
        mix_lat, mix_ctx = mixing_sublayer(
            modulate(rmsnorm(h, norm_mix[layer]), mod[3], mod[4]),
            modulate(rmsnorm(hc, norm_mix[layer]), mod_c[3], mod_c[4]),
            row, col, ctx_out, w_in[layer], mla_q_norm[layer], mla_kv_norm[layer], mla_w_uq[layer], mla_w_ukv[layer],
            mla_w_o[layer], ssm_lambda_re[layer], ssm_lambda_im[layer], ssm_log_dt[layer], ssm_b_re[layer], ssm_b_im[layer],
            ssm_c_re[layer], ssm_c_im[layer], ssm_d[layer], ssm_w_glu[layer], gqa_sink[layer], gqa_w_o[layer], w_out[layer])
        h = h + bcast(mod[5]) * mix_lat
        h = h + 0.5 * bcast(mod[8]) * swiglu(modulate(rmsnorm(h, norm_ffn2[layer]), mod[6], mod[7]), ffn2_w13[layer], ffn2_w2[layer])
        if ctx_out:
            hc = hc + bcast(mod_c[5]) * mix_ctx
            hc = hc + 0.5 * bcast(mod_c[8]) * swiglu(modulate(rmsnorm(hc, norm_ffn2[layer]), mod_c[6], mod_c[7]), ffn2_w13[layer], ffn2_w2[layer])
    return rmsnorm(h, final_norm)
```

```python
import contextlib
import math
import numpy as np
import concourse.bass as bass
import concourse.mybir as mybir
from concourse.bass_utils import run_bass_kernel_spmd

F32 = mybir.dt.float32
BF16 = mybir.dt.bfloat16
AF = mybir.ActivationFunctionType
ALU = mybir.AluOpType

D = 1024
CTX = 256
FF = 2816
NFS = FF // 128
EPS = 1e-6


class Buf:
    __slots__ = ("name", "writers", "readers")

    ALL = []

    def __init__(self, name):
        self.name = name
        self.writers = []
        self.readers = []
        Buf.ALL.append(self)

    @staticmethod
    def reset_all():
        for b in Buf.ALL:
            b.writers = []
            b.readers = []


class Op:
    __slots__ = ("eng", "fn", "deps", "is_dma", "sem", "val", "signal", "idx", "strict")

    def __init__(self, eng, fn, is_dma):
        self.strict = ()
        self.eng = eng
        self.fn = fn
        self.deps = []
        self.is_dma = is_dma
        self.sem = None
        self.val = 0
        self.signal = is_dma
        self.idx = 0


class Sched:
    ENGS = ("pe", "dve", "act", "pool", "sp")

    def __init__(self, nc):
        self.nc = nc
        self.stack = contextlib.ExitStack()
        self.ops = []
        self.dma_sems = {}
        self.last = {}
        self.phase_dep = None

    def sbuf(self, name, shape, dt):
        return self.stack.enter_context(self.nc.sbuf_tensor(name, list(shape), dt))

    def psum(self, name, shape, dt=F32):
        return self.stack.enter_context(self.nc.psum_tensor(name, list(shape), dt))

    def sem(self, name):
        if not hasattr(self, "all_sems"):
            self.all_sems = []
        Sched._uid = getattr(Sched, "_uid", 0) + 1
        h = self.stack.enter_context(self.nc.semaphore("%s_%d" % (name, Sched._uid)))
        self.all_sems.append(h)
        return h

    def op(self, eng, fn, reads=(), writes=(), dma_key=None, extra=(), sreads=()):
        is_dma = dma_key is not None
        o = Op(eng, fn, is_dma)
        o.idx = len(self.ops)
        deps = set(extra)
        if sreads:
            st_ = set()
            for b in sreads:
                st_.update(b.writers)
            o.strict = st_
            deps.update(st_)
        if self.phase_dep is not None:
            deps.add(self.phase_dep)
        for b in reads:
            deps.update(b.writers)
        for b in writes:
            deps.update(b.writers)
            deps.update(b.readers)
        o.deps = sorted(deps)
        if is_dma:
            if dma_key not in self.dma_sems:
                self.dma_sems[dma_key] = [self.sem("d_" + str(dma_key)), 0]
            ent = self.dma_sems[dma_key]
            ent[1] += 16
            o.sem = ent[0]
            o.val = ent[1]
            self.last[("dma", dma_key)] = o.idx
        else:
            self.last[eng] = o.idx
        for b in reads:
            b.readers.append(o.idx)
        for b in writes:
            b.writers = [o.idx]
            b.readers = []
        self.ops.append(o)
        return o

    def barrier(self):
        deps = list(self.last.values())
        self.phase_dep = None
        o = self.op("sp", lambda e: e.nop(), extra=deps)
        self.phase_dep = o.idx
        self.last = {}

    def emit(self):
        nc = self.nc
        ops = self.ops
        esem = {e: self.sem("e_" + e) for e in self.ENGS}
        for o in ops:
            for d in o.deps:
                p = ops[d]
                if p.is_dma or (p.eng == o.eng and o.eng in ("pe", "sp")):
                    continue
                p.signal = True
        cnt = {e: 0 for e in self.ENGS}
        for o in ops:
            if not o.is_dma and o.signal:
                cnt[o.eng] += 1
                o.sem = esem[o.eng]
                o.val = cnt[o.eng]
        dma_issued = {}
        per_eng = {e: [] for e in self.ENGS}
        waited = {e: {} for e in self.ENGS}
        for o in ops:
            waits = {}
            for d in o.deps:
                p = ops[d]
                if p.is_dma:
                    s, v = dma_issued[id(p.sem)]
                else:
                    if p.eng == o.eng and o.eng in ("pe", "sp"):
                        continue
                    s, v = p.sem, p.val
                k = id(s)
                if waited[o.eng].get(k, 0) >= v:
                    continue
                if k not in waits or waits[k][1] < v:
                    waits[k] = (s, v)
            for k, (s, v) in waits.items():
                waited[o.eng][k] = v
            if o.is_dma:
                dma_issued[id(o.sem)] = (o.sem, o.val)
            per_eng[o.eng].append((o, list(waits.values())))

        def run(engname, e):
            for o, waits in per_eng[engname]:
                for s, v in waits:
                    e.wait_ge(s, v)
                if o.fn is None:
                    continue
                ins = o.fn(e)
                if o.signal:
                    ins.then_inc(o.sem, 16 if o.is_dma else 1)

        with nc.Block() as block:
            @block.tensor
            def _(e):
                run("pe", e)

            @block.vector
            def _(e):
                run("dve", e)

            @block.scalar
            def _(e):
                run("act", e)

            @block.gpsimd
            def _(e):
                run("pool", e)

            @block.sync
            def _(e):
                run("sp", e)

    def reset_sems(self):
        nc = self.nc
        for s in self.all_sems:
            nc.gpsimd.sem_clear(s)
        nc.all_engine_barrier()

    def close(self):
        self.stack.close()


class Tl:
    __slots__ = ("t", "b")

    def __init__(self, t, name):
        self.t = t
        self.b = Buf(name)


class Rot:
    def __init__(self, tiles):
        self.tiles = tiles
        self.i = 0

    def next(self):
        t = self.tiles[self.i % len(self.tiles)]
        self.i += 1
        return t


class Arena:
    def __init__(self, S, nbytes):
        self.t = S.sbuf("arena", [128, nbytes // 4], F32)
        self.cap = nbytes
        self.off = 0
        self.n = 0

    def alloc(self, free_shape, dt, name=None):
        esz = 4 if dt == F32 else 2
        n = int(np.prod(free_shape))
        nb = (n * esz + 63) // 64 * 64
        assert self.off + nb <= self.cap, ("arena overflow", name, self.off, nb, self.cap)
        v = self.t[:, self.off // 4:(self.off + nb) // 4]
        if dt != F32:
            v = v.bitcast(dt)
        v = v[:, 0:n]
        if len(free_shape) == 2:
            v = v.rearrange("p (a b) -> p a b", a=free_shape[0])
        elif len(free_shape) == 3:
            v = v.rearrange("p (a b c) -> p a b c", a=free_shape[0], b=free_shape[1])
        elif len(free_shape) == 4:
            v = v.rearrange("p (a b c d) -> p a b c d", a=free_shape[0], b=free_shape[1], c=free_shape[2])
        self.off += nb
        self.n += 1
        return Tl(v, name or ("a%d" % self.n))

    def rot(self, k, free_shape, dt, name):
        return Rot([self.alloc(free_shape, dt, "%s%d" % (name, i)) for i in range(k)])

    def mark(self):
        return self.off

    def release(self, m):
        self.off = m


def _swap_perm(n_axial):
    half = n_axial // 2
    q = half // 2
    perm = np.zeros(n_axial, np.int64)
    for i in range(n_axial):
        base = (i // half) * half
        j = i - base
        perm[i] = base + (j + q if j < q else j - q)
    return perm


def _rope_tables(n_axial, T, grid_w=64):
    half = n_axial // 2
    q = half // 2
    t = np.arange(T)
    row = (t // grid_w).astype(np.float32)
    col = (t % grid_w).astype(np.float32)
    inv = (10000.0 ** (-np.arange(0, half, 2, dtype=np.float32) / np.float32(half))).astype(np.float32)
    cos = np.zeros((n_axial, T), np.float32)
    sin = np.zeros((n_axial, T), np.float32)
    for i in range(n_axial):
        pos = row if i < half else col
        j = (i % half)
        ang = (pos * inv[j % q]).astype(np.float32)
        cos[i] = np.cos(ang)
        sn = np.sin(ang)
        sin[i] = -sn if j < q else sn
    return cos, sin


def fm(v, nch):
    v = np.asarray(v, np.float32)
    lead = v.shape[:-1]
    v = v.reshape(lead + (nch, 128))
    v = np.moveaxis(v, -1, 0)
    return np.ascontiguousarray(v)


def host_prep(inp, T, L):
    f32 = np.float32
    sh = {}
    sh["ada_w"] = np.ascontiguousarray(inp["ada_w"][:L], f32)
    sh["ada_bT"] = fm(inp["ada_b"][:L].reshape(L, 9, D), 8)
    norms = np.stack([inp["norm_ffn1"][:L], inp["norm_mix"][:L], inp["norm_ffn2"][:L]], 1)
    sh["normsT"] = fm(norms, 8)
    sh["fnT"] = fm(inp["final_norm"], 8)
    for k in ("ffn1_w13", "ffn1_w2", "ffn2_w13", "ffn2_w2", "mla_w_o", "ssm_w_glu", "gqa_w_o", "w_out", "mla_w_uq"):
        sh[k] = np.ascontiguousarray(inp[k][:L], f32)
    w_in = np.asarray(inp["w_in"][:L], f32)
    sh["w_in"] = np.ascontiguousarray(w_in)
    pm = _swap_perm(32)
    pg = _swap_perm(64)
    krp = np.zeros((L, D, 96), f32)
    krs = np.zeros((L, D, 96), f32)
    krp[:, :, 64:96] = w_in[:, :, 640:672]
    krs[:, :, 64:96] = w_in[:, :, 640 + pm]
    sh["w_krp"] = krp
    sh["w_krs"] = krs
    gq = w_in[:, :, 1184:1696].reshape(L, D, 8, 64)
    sh["w_gqs"] = np.ascontiguousarray(gq[:, :, :, pg].reshape(L, D, 512))
    gk = w_in[:, :, 1696:1824].reshape(L, D, 2, 64)
    sh["w_gks"] = np.ascontiguousarray(gk[:, :, :, pg].reshape(L, D, 128))
    uq = np.asarray(inp["mla_w_uq"][:L], f32).reshape(L, 384, 8, 96)
    uqs = uq.copy()
    uqs[:, :, :, 64:96] = uq[:, :, :, 64 + pm]
    sh["w_uqs"] = np.ascontiguousarray(uqs.reshape(L, 384, 768))
    ukv = np.asarray(inp["mla_w_ukv"][:L], f32).reshape(L, 256, 8, 128)
    sh["w_uk"] = np.ascontiguousarray(ukv[:, :, :, :64].reshape(L, 256, 512))
    sh["w_uv"] = np.ascontiguousarray(ukv[:, :, :, 64:].reshape(L, 256, 512))
    sh["qnT"] = fm(inp["mla_q_norm"][:L], 3)
    sh["kvnT"] = fm(inp["mla_kv_norm"][:L], 2)
    sh["ssm_dT"] = fm(inp["ssm_d"][:L], 4)

    def st_layout(a):
        a = np.asarray(a, f32)
        rest = a.shape[4:]
        a = a.reshape((L, 2, 16, 2, 64) + rest)
        a = np.moveaxis(a, (3, 4), (0, 1))
        return np.ascontiguousarray(a.reshape((128, L, 2, 16) + rest))
    sh["lamre"] = st_layout(inp["ssm_lambda_re"][:L])
    sh["lamim"] = st_layout(inp["ssm_lambda_im"][:L])
    ldt = np.broadcast_to(np.asarray(inp["ssm_log_dt"][:L], f32)[:, :, :, None], (L, 2, 32, 64))
    sh["logdt"] = st_layout(ldt)
    sh["Bre"] = st_layout(inp["ssm_b_re"][:L])
    sh["Bim"] = st_layout(inp["ssm_b_im"][:L])
    for nm, key in (("Cxre", "ssm_c_re"), ("Cxim", "ssm_c_im")):
        c = np.asarray(inp[key][:L], f32)
        cx = np.zeros((128, L, 2, 16, 128), f32)
        for st in range(16):
            for gi in range(2):
                g = 2 * st + gi
                c0 = (st % 4) * 32 + gi * 16
                cx[gi * 64:(gi + 1) * 64, :, :, st, c0:c0 + 16] = np.transpose(c[:, :, g, :, :], (3, 0, 1, 2))
        sh[nm] = cx
    sk = np.asarray(inp["gqa_sink"][:L], f32)
    sh["sinkb"] = np.ascontiguousarray(np.broadcast_to(sk[None, :, :, None], (1, L, 8, 128)))
    cm, sm = _rope_tables(32, T)
    cg, sg = _rope_tables(64, T)
    sh["cosM"] = cm
    sh["sinM"] = sm
    sh["cosG"] = np.ascontiguousarray(np.concatenate([cg, cg], 0))
    sh["sinG"] = np.ascontiguousarray(np.concatenate([sg, sg], 0))
    kk = np.arange(128)[:, None]
    qq = np.arange(128)[None, :]
    sh["maskL"] = np.ascontiguousarray(np.tile((kk >= qq).astype(f32), (1, 4)))
    sh["maskU"] = np.ascontiguousarray(np.tile((kk <= qq).astype(f32), (1, 4)))
    sh["ident"] = np.eye(128, dtype=f32)
    per_core = []
    for b in range(inp["x"].shape[0]):
        pc = {}
        pc["xT"] = np.ascontiguousarray(np.asarray(inp["x"][b], f32).T)
        pc["ctxT"] = np.ascontiguousarray(np.asarray(inp["ctx"][b], f32).T)
        cc = np.stack([np.asarray(inp["c"][b], f32), np.asarray(inp["c_ctx"], f32)], 0)
        pc["cc"] = np.ascontiguousarray(np.transpose(fm(cc, 8), (0, 2, 1)))
        per_core.append(pc)
    return sh, per_core


def build(T, L, shapes, debug=False, mode="full"):
    import os
    S_ = T + CTX
    NKT = S_ // 128
    nc = bass.Bass("TRN2", target_bir_lowering=False)
    S = Sched(nc)
    dr = {}
    for k, (shp, _) in shapes.items():
        dr[k] = nc.dram_tensor(k, list(shp), F32, kind="ExternalInput").ap()
    LB = L if mode == "full" else 1
    NOTLAYERED = ("xT", "ctxT", "cc", "fnT", "cosM", "sinM", "cosG", "sinG", "maskL", "maskU", "ident")
    LEAD = ("ada_w", "ffn1_w13", "ffn1_w2", "ffn2_w13", "ffn2_w2", "mla_w_o", "ssm_w_glu", "gqa_w_o", "w_out", "mla_w_uq",
            "w_in", "w_krp", "w_krs", "w_gqs", "w_gks", "w_uqs", "w_uk", "w_uv")
    BF16SET = ("ffn1_w13", "ffn1_w2", "ffn2_w13", "ffn2_w2", "w_in", "w_krp", "w_krs", "w_gqs", "w_gks", "mla_w_uq", "w_uqs",
               "w_uk", "w_uv", "mla_w_o", "ssm_w_glu", "gqa_w_o", "w_out", "Cxre", "Cxim")
    ext = dict(dr)
    layered = []
    WENG = "pool"
    if mode == "loop" and os.environ.get("LOOPDBG") != "1":
        WENG = "sp"
    if mode == "loop" and os.environ.get("LOOPDBG") != "1":
        layered = [k for k in shapes if k not in NOTLAYERED]
        for k in layered:
            dr[k] = nc.dram_tensor("wc_" + k, list(shapes[k][0]), BF16 if k in BF16SET else F32, kind="Internal").ap()
        for k in ("maskL", "maskU"):
            dr[k] = nc.dram_tensor("wc_" + k, list(shapes[k][0]), BF16, kind="Internal").ap()

    def flat(ap):
        n = len(ap.shape)
        if n <= 2:
            return ap
        pat = {3: "p a b -> p (a b)", 4: "p a b c -> p (a b c)", 5: "p a b c d -> p (a b c d)"}[n]
        return ap.rearrange(pat)

    def lslice(ap, k, l):
        return flat(ap[l]) if k in LEAD else flat(ap[:, l])
    if mode != "layer":
        outT = nc.dram_tensor("outT", [D, T], F32, kind="ExternalOutput").ap()
    sk = "ExternalOutput" if debug else "Internal"

    def scratch(name, shape, dt):
        return nc.dram_tensor(name, list(shape), dt, kind=sk).ap()
    if mode == "layer":
        hT = nc.dram_tensor("hT", [D, S_], F32, kind="ExternalOutput").ap()
    else:
        hT = scratch("hT", [D, S_], F32)
    XMT = scratch("XMT", [D, S_], BF16)
    QT = scratch("QT", [8, 96, S_], BF16)
    KNT = scratch("KNT", [512, S_], BF16)
    KRT = scratch("KRT", [32, S_], BF16)
    VM = scratch("VM", [S_, 520], BF16)
    UT = scratch("UT", [512, S_], F32)
    GQT = scratch("GQT", [512, S_], BF16)
    GKT = scratch("GKT", [128, S_], BF16)
    GVM = scratch("GVM", [S_, 130], BF16)
    OMT = scratch("OMT", [512, S_], BF16)
    YT = scratch("YT", [512, S_], F32)
    GYT = scratch("GYT", [512, S_], BF16)
    OGT = scratch("OGT", [512, S_], BF16)
    dbuf = {}

    def DB(name, i=0):
        k = (name, i)
        if k not in dbuf:
            dbuf[k] = Buf("%s_%s" % (name, i))
        return dbuf[k]

    def DBall(name):
        return [b for (n, _), b in dbuf.items() if n == name]

    A = Arena(S, 200 * 1024)
    _banks = [Tl(S.psum("ps%d" % i, [128, 512]), "ps%d" % i) for i in range(8)]
    PS = Rot(_banks[0:6])
    PSA = Rot(_banks[6:8])
    dbg_outs = {}

    def dbg_dump(name, tl, dt=F32):
        if not debug:
            return
        shp = list(tl.t.shape)
        d_ = nc.dram_tensor("dbg_" + name, shp, dt, kind="ExternalOutput").ap()
        dma("dbg_" + name, d_, tl.t, [tl.b], [DB("dbg_" + name, 0)])

    def mm(out, lhsT, rhs, start, stop, reads, writes):
        S.op("pe", lambda e: e.matmul(out, lhsT=lhsT, rhs=rhs, start=start, stop=stop), reads, writes)

    def act(out, in_, func, reads, writes, **kw):
        S.op("act", lambda e: e.activation(out=out, in_=in_, func=func, **kw), reads, writes)

    def tt(out, in0, in1, op, reads, writes, eng="dve"):
        S.op(eng, lambda e: e.tensor_tensor(out=out, in0=in0, in1=in1, op=op), reads, writes)

    def ts(out, in0, s1, s2, op0, op1, reads, writes, eng="dve", sreads=()):
        if op1 is None:
            S.op(eng, lambda e: e.tensor_scalar(out=out, in0=in0, scalar1=s1, scalar2=None, op0=op0), reads, writes, sreads=sreads)
        else:
            S.op(eng, lambda e: e.tensor_scalar(out=out, in0=in0, scalar1=s1, scalar2=s2, op0=op0, op1=op1), reads, writes, sreads=sreads)

    def stt(out, in0, scalar, in1, op0, op1, reads, writes, sreads=()):
        S.op("dve", lambda e: e.scalar_tensor_tensor(out=out, in0=in0, scalar=scalar, in1=in1, op0=op0, op1=op1), reads, writes, sreads=sreads)

    def scan(out, d0, d1, init, reads, writes):
        S.op("dve", lambda e: e.tensor_tensor_scan(out=out, data0=d0, data1=d1, initial=init, op0=ALU.mult, op1=ALU.add), reads, writes)

    def cp(out, in_, reads, writes, eng="dve"):
        S.op(eng, lambda e: e.tensor_copy(out=out, in_=in_), reads, writes)

    def recip(out, in_, reads, writes):
        S.op("dve", lambda e: e.reciprocal(out=out, in_=in_), reads, writes)

    def memset(ap, val, writes, eng="dve"):
        S.op(eng, lambda e: e.memset(ap, val), (), writes)

    def dma(key, out, in_, reads, writes, eng="sp", **kw):
        S.op(eng, lambda e: e.dma_start(out=out, in_=in_, **kw), reads, writes, dma_key=key)

    def wload(tl, src, kc_list=None):
        nkc = tl.t.shape[1]
        v = src.rearrange("(kc p) f -> p kc f", p=128)
        for kc in range(nkc):
            if WENG == "pool":
                dma(tl.b.name, tl.t[:, kc, :], v[:, kc, :], (), [tl.b], eng="pool", max_dma_last_dim=4096)
            else:
                dma(tl.b.name, tl.t[:, kc, :], v[:, kc, :], (), [tl.b], eng="sp")

    ones_bf = A.alloc([128], BF16, "ones_bf")
    ones_f = A.alloc([128], F32, "ones_f")
    ident = A.alloc([128], F32, "ident")
    memset(ones_bf.t, 1.0, [ones_bf.b])
    memset(ones_f.t, 1.0, [ones_f.b])
    if "ident" in dr:
        dma("ident", ident.t, dr["ident"], (), [ident.b])
    modT = A.alloc([LB, 9, 8, 2], F32, "modT")
    gsT = A.alloc([LB, 3, 8, 2], F32, "gsT")
    hgT = A.alloc([LB, 3, 8, 2], F32, "hgT")
    normsT = A.alloc([LB, 3, 8], F32, "normsT")
    fnT = A.alloc([8], F32, "fnT")
    qnT = A.alloc([LB, 3], F32, "qnT")
    kvnT = A.alloc([LB, 2], F32, "kvnT")
    sdT = A.alloc([LB, 4], F32, "sdT")
    if "fnT" in dr:
        dma("fnT", fnT.t, dr["fnT"], (), [fnT.b])

    def setup_layers():
        for tl, nm in ((normsT, "normsT"), (qnT, "qnT"), (kvnT, "kvnT"), (sdT, "ssm_dT")):
            dma(tl.b.name, tl.t, dr[nm][:, 0:LB], (), [tl.b])
        m0 = A.mark()
        cc = A.alloc([8, 2], F32, "cc")
        scT = A.alloc([8, 2], F32, "scT")
        adab = A.alloc([LB, 9, 8], F32, "adab")
        dma("cc", cc.t, dr["cc"], (), [cc.b])
        dma("adab", adab.t, dr["ada_bT"][:, 0:LB], (), [adab.b])
        act(scT.t, cc.t, AF.Silu, [cc.b], [scT.b])
        awr = A.rot(2, [8, 1024], F32, "adaw")
        for l in range(LB):
            pst = PSA.next()
            for i in range(9):
                aw = awr.next()
                src = dr["ada_w"][l, :, i * 1024:(i + 1) * 1024].rearrange("(kc p) f -> p kc f", p=128)
                for half in range(2):
                    dma(aw.b.name, aw.t[:, half * 4:(half + 1) * 4, :], src[:, half * 4:(half + 1) * 4, :], (), [aw.b])
                for j in range(8):
                    c0 = (i * 8 + j) * 2
                    for kc in range(8):
                        mm(pst.t[:, c0:c0 + 2], aw.t[:, kc, j * 128:(j + 1) * 128], scT.t[:, kc, :], kc == 0, kc == 7,
                           [aw.b, scT.b], [pst.b])
            pv = pst.t[:, 0:144].rearrange("p (i j k) -> p i j k", i=9, j=8)
            for k in range(2):
                tt(modT.t[:, l, :, :, k], pv[:, :, :, k], adab.t[:, l, :, :], ALU.add, [pst.b, adab.b], [modT.b])
            for n in range(3):
                for k in range(2):
                    stt(gsT.t[:, l, n, :, k], modT.t[:, l, 3 * n + 1, :, k], 1.0, normsT.t[:, l, n, :], ALU.add, ALU.mult,
                        [modT.b, normsT.b], [gsT.b])
                ts(hgT.t[:, l, n, :, :], modT.t[:, l, 3 * n + 2, :, :], 1.0 if n == 1 else 0.5, None, ALU.mult, None,
                   [modT.b], [hgT.b])
        S.barrier()
        A.release(m0)

    NTL = T // 512
    if mode in ("full", "loop"):
        dma("h0", hT[:, 0:CTX], dr["ctxT"], (), [DB("hT", 0)])
        for i in range(NTL):
            dma("h0", hT[:, CTX + i * 512:CTX + (i + 1) * 512], dr["xT"][:, i * 512:(i + 1) * 512], (), [DB("hT", 1 + i)])
    else:
        dma("h0", hT[:, 0:CTX], dr["h_in"][:, 0:CTX], (), [DB("hT", 0)])
        for i in range(NTL):
            dma("h0", hT[:, CTX + i * 512:CTX + (i + 1) * 512], dr["h_in"][:, CTX + i * 512:CTX + (i + 1) * 512], (), [DB("hT", 1 + i)])
    if mode == "loop" and layered:
        for k in layered:
            for l in range(L):
                if k in BF16SET:
                    dma("cpyc", lslice(dr[k], k, l), lslice(ext[k], k, l), (), [DB("wc_" + k, l)], eng="pool", max_dma_last_dim=4096)
                else:
                    dma("cpy", lslice(dr[k], k, l), lslice(ext[k], k, l), (), [DB("wc_" + k, l)])
        for k in ("maskL", "maskU"):
            dma("cpyc", dr[k], ext[k], (), [DB("wc_" + k, 0)], eng="pool", max_dma_last_dim=4096)
    if mode in ("full", "layer"):
        setup_layers()
    else:
        S.barrier()

    tiles512 = [(0, CTX, True, 0)] + [(CTX + i * 512, 512, False, 1 + i) for i in range(NTL)]
    tiles256 = [(0, CTX, True, [0])] + [(CTX + i * 256, 256, False, [1 + i // 2]) for i in range(T // 256)]

    def norm_mod(Ht, N, nchunks, gs_ap, sh_ap, out_t, sq, rstd, tmpr, dim, extra_reads=()):
        act(sq.t[:, 0:nchunks, :N], Ht.t[:, 0:nchunks, :N], AF.Square, [Ht.b], [sq.b])
        pss = PS.next()
        for j in range(nchunks):
            mm(pss.t[:, :N], ones_bf.t, sq.t[:, j, :N], j == 0, j == nchunks - 1, [sq.b, ones_bf.b], [pss.b])
        ts(rstd.t[:, :N], pss.t[:, :N], 1.0 / dim, EPS, ALU.mult, ALU.add, [pss.b], [rstd.b])
        act(rstd.t[:, :N], rstd.t[:, :N], AF.Sqrt, [rstd.b], [rstd.b])
        recip(rstd.t[:, :N], rstd.t[:, :N], [rstd.b], [rstd.b])
        for j in range(nchunks):
            if sh_ap is None:
                stt(out_t.t[:, j, :N], Ht.t[:, j, :N], gs_ap(j), rstd.t[:, :N], ALU.mult, ALU.mult,
                    [Ht.b, rstd.b] + list(extra_reads), [out_t.b])
            else:
                tm = tmpr.next()
                stt(tm.t[:, :N], Ht.t[:, j, :N], gs_ap(j), rstd.t[:, :N], ALU.mult, ALU.mult,
                    [Ht.b, rstd.b] + list(extra_reads), [tm.b])
                act(out_t.t[:, j, :N], tm.t[:, :N], AF.Identity, [tm.b] + list(extra_reads), [out_t.b], bias=sh_ap(j))

    def ffn_phase(l, nidx, w13_dr, w2_dr, skip_ctx, final):
        m = A.mark()
        W13 = A.alloc([8, 2 * FF], BF16, "W13")
        W2 = A.alloc([NFS, D], BF16, "W2")
        wload(W13, w13_dr)
        wload(W2, w2_dr)
        Hr = A.rot(2, [8, 256], F32, "H")
        xn = A.alloc([8, 256], BF16, "xn")
        sq = A.alloc([8, 256], BF16, "sq")
        g = A.alloc([NFS, 256], BF16, "g")
        rstd = A.alloc([256], F32, "rstd")
        tmpr = A.rot(2, [256], F32, "tmp")
        sir = A.rot(2, [256], BF16, "si")
        if final:
            ot = A.alloc([8, 256], F32, "ot")
        for (c0, N, isctx, hb) in tiles256:
            if isctx and skip_ctx:
                continue
            k = 1 if isctx else 0
            H = Hr.next()
            dma(H.b.name, H.t[:, :, :N], hT[:, c0:c0 + N].rearrange("(j p) n -> p j n", p=128),
                [DB("hT", i) for i in hb], [H.b])
            norm_mod(H, N, 8, lambda j: gsT.t[:, l, nidx, j, k:k + 1], lambda j: modT.t[:, l, 3 * nidx, j, k:k + 1],
                     xn, sq, rstd, tmpr, D, extra_reads=[gsT.b, modT.b])
            for s in range(NFS):
                pa = PS.next()
                pb = PS.next()
                for kc in range(8):
                    mm(pa.t[:, :N], W13.t[:, kc, s * 128:(s + 1) * 128], xn.t[:, kc, :N], kc == 0, kc == 7, [W13.b, xn.b], [pa.b])
                for kc in range(8):
                    mm(pb.t[:, :N], W13.t[:, kc, FF + s * 128:FF + (s + 1) * 128], xn.t[:, kc, :N], kc == 0, kc == 7, [W13.b, xn.b], [pb.b])
                si = sir.next()
                act(si.t[:, :N], pa.t[:, :N], AF.Silu, [pa.b], [si.b])
                tt(g.t[:, s, :N], si.t[:, :N], pb.t[:, :N], ALU.mult, [si.b, pb.b], [g.b])
            for d in range(8):
                po = PS.next()
                for fc in range(NFS):
                    mm(po.t[:, :N], W2.t[:, fc, d * 128:(d + 1) * 128], g.t[:, fc, :N], fc == 0, fc == NFS - 1, [W2.b, g.b], [po.b])
                stt(H.t[:, d, :N], po.t[:, :N], hgT.t[:, l, nidx, d, k:k + 1], H.t[:, d, :N], ALU.mult, ALU.add,
                    [po.b, hgT.b, H.b], [H.b])
            if final:
                if not isctx:
                    norm_mod(H, N, 8, lambda j: fnT.t[:, j:j + 1], None, ot, sq, rstd, tmpr, D, extra_reads=[fnT.b])
                    dma(ot.b.name, outT[:, c0 - CTX:c0 - CTX + N].rearrange("(j p) n -> p j n", p=128), ot.t[:, :, :N],
                        [ot.b], [DB("outT", 0)])
            else:
                dma(H.b.name, hT[:, c0:c0 + N].rearrange("(j p) n -> p j n", p=128), H.t[:, :, :N], [H.b],
                    [DB("hT", i) for i in hb])
        S.barrier()
        A.release(m)

    def proj_phase(l):
        m = A.mark()
        win = dr["w_in"][l]
        Wq1 = A.alloc([8, 384], BF16, "Wq1"); wload(Wq1, win[:, 0:384])
        Wkv1 = A.alloc([8, 256], BF16, "Wkv1"); wload(Wkv1, win[:, 384:640])
        Wkrp = A.alloc([8, 96], BF16, "Wkrp"); wload(Wkrp, dr["w_krp"][l])
        Wkrs = A.alloc([8, 96], BF16, "Wkrs"); wload(Wkrs, dr["w_krs"][l])
        Wu = A.alloc([8, 512], BF16, "Wu"); wload(Wu, win[:, 672:1184])
        Wgq = A.alloc([8, 512], BF16, "Wgq"); wload(Wgq, win[:, 1184:1696])
        Wgqs = A.alloc([8, 512], BF16, "Wgqs"); wload(Wgqs, dr["w_gqs"][l])
        Wgk = A.alloc([8, 128], BF16, "Wgk"); wload(Wgk, win[:, 1696:1824])
        Wgks = A.alloc([8, 128], BF16, "Wgks"); wload(Wgks, dr["w_gks"][l])
        Wgv = A.alloc([8, 128], BF16, "Wgv"); wload(Wgv, win[:, 1824:1952])
        Wuq = A.alloc([3, 768], BF16, "Wuq"); wload(Wuq, dr["mla_w_uq"][l])
        Wuqs = A.alloc([3, 768], BF16, "Wuqs"); wload(Wuqs, dr["w_uqs"][l])
        Wuk = A.alloc([2, 512], BF16, "Wuk"); wload(Wuk, dr["w_uk"][l])
        Wuv = A.alloc([2, 512], BF16, "Wuv"); wload(Wuv, dr["w_uv"][l])
        Hr = A.rot(2, [8, 512], F32, "H")
        xm = A.alloc([8, 512], BF16, "xm")
        sq = A.alloc([8, 512], BF16, "sq")
        rstd = A.alloc([512], F32, "rstd")
        tmpr = A.rot(2, [512], F32, "tmp")
        cq = A.alloc([3, 512], F32, "cq")
        cqn = A.alloc([3, 512], BF16, "cqn")
        ckvn = A.alloc([2, 512], BF16, "ckvn")
        rt = A.rot(2, [4, 512], F32, "ropet")
        qtr = A.rot(3, [512], BF16, "qt")
        t1r = A.rot(2, [512], F32, "t1")
        t2r = A.rot(2, [512], F32, "t2")
        utr = A.rot(2, [512], F32, "ut")
        vst = A.rot(2, [8, 65], BF16, "vst")
        vst2 = A.rot(2, [2, 65], BF16, "vst2")
        for v in vst.tiles + vst2.tiles:
            memset(v.t, 1.0, [v.b])

        def rope_out(dst, p, psw, lo, hi, cosr, sinr, rtb, N, isctx):
            if isctx:
                cp(dst.t[lo:hi, :N], p.t[lo:hi, :N], [p.b], [dst.b])
                return
            t1 = t1r.next()
            t2 = t2r.next()
            tt(t1.t[lo:hi, :N], p.t[lo:hi, :N], cosr[lo:hi, :N], ALU.mult, [p.b, rtb], [t1.b])
            tt(t2.t[lo:hi, :N], psw.t[lo:hi, :N], sinr[lo:hi, :N], ALU.mult, [psw.b, rtb], [t2.b])
            tt(dst.t[lo:hi, :N], t1.t[lo:hi, :N], t2.t[lo:hi, :N], ALU.add, [t1.b, t2.b], [dst.b], eng="pool")

        for (c0, N, isctx, hb) in tiles512:
            k = 1 if isctx else 0
            H = Hr.next()
            dma(H.b.name, H.t[:, :, :N], hT[:, c0:c0 + N].rearrange("(j p) n -> p j n", p=128), [DB("hT", hb)], [H.b])
            R = rt.next()
            if not isctx:
                t0 = c0 - CTX
                dma(R.b.name, R.t[64:96, 0, :N], dr["cosM"][:, t0:t0 + N], (), [R.b])
                dma(R.b.name, R.t[64:96, 1, :N], dr["sinM"][:, t0:t0 + N], (), [R.b])
                dma(R.b.name, R.t[:, 2, :N], dr["cosG"][:, t0:t0 + N], (), [R.b])
                dma(R.b.name, R.t[:, 3, :N], dr["sinG"][:, t0:t0 + N], (), [R.b])
            norm_mod(H, N, 8, lambda j: gsT.t[:, l, 1, j, k:k + 1], lambda j: modT.t[:, l, 3, j, k:k + 1],
                     xm, sq, rstd, tmpr, D, extra_reads=[gsT.b, modT.b])
            dma(xm.b.name, XMT[:, c0:c0 + N].rearrange("(j p) n -> p j n", p=128), xm.t[:, :, :N], [xm.b], [DB("XMT", hb)])
            for j in range(3):
                p = PS.next()
                for kc in range(8):
                    mm(p.t[:, :N], Wq1.t[:, kc, j * 128:(j + 1) * 128], xm.t[:, kc, :N], kc == 0, kc == 7, [Wq1.b, xm.b], [p.b])
                act(cq.t[:, j, :N], p.t[:, :N], AF.Identity, [p.b], [cq.b])
            norm_mod(cq, N, 3, lambda j: qnT.t[:, l, j:j + 1], None, cqn, sq, rstd, tmpr, 384, extra_reads=[qnT.b])
            for h in range(8):
                p = PS.next()
                psw = PS.next()
                for kc in range(3):
                    mm(p.t[0:96, :N], Wuq.t[:, kc, h * 96:(h + 1) * 96], cqn.t[:, kc, :N], kc == 0, kc == 2, [Wuq.b, cqn.b], [p.b])
                if not isctx:
                    for kc in range(3):
                        mm(psw.t[0:96, :N], Wuqs.t[:, kc, h * 96:(h + 1) * 96], cqn.t[:, kc, :N], kc == 0, kc == 2, [Wuqs.b, cqn.b], [psw.b])
                qt = qtr.next()
                act(qt.t[0:64, :N], p.t[0:64, :N], AF.Identity, [p.b], [qt.b])
                rope_out(qt, p, psw, 64, 96, R.t[:, 0, :], R.t[:, 1, :], R.b, N, isctx)
                dma(qt.b.name, QT[h, :, c0:c0 + N], qt.t[0:96, :N], [qt.b], [DB("QT", hb)])
            for j in range(2):
                p = PS.next()
                for kc in range(8):
                    mm(p.t[:, :N], Wkv1.t[:, kc, j * 128:(j + 1) * 128], xm.t[:, kc, :N], kc == 0, kc == 7, [Wkv1.b, xm.b], [p.b])
                act(cq.t[:, j, :N], p.t[:, :N], AF.Identity, [p.b], [cq.b])
            norm_mod(cq, N, 2, lambda j: kvnT.t[:, l, j:j + 1], None, ckvn, sq, rstd, tmpr, 256, extra_reads=[kvnT.b])
            for j in range(4):
                p = PS.next()
                for kc in range(2):
                    mm(p.t[:, :N], Wuk.t[:, kc, j * 128:(j + 1) * 128], ckvn.t[:, kc, :N], kc == 0, kc == 1, [Wuk.b, ckvn.b], [p.b])
                qt = qtr.next()
                act(qt.t[:, :N], p.t[:, :N], AF.Identity, [p.b], [qt.b])
                dma(qt.b.name, KNT[j * 128:(j + 1) * 128, c0:c0 + N], qt.t[:, :N], [qt.b], [DB("KNT", hb)])
            for tb in range(N // 128):
                p = PS.next()
                for kc in range(2):
                    mm(p.t[:, :], ckvn.t[:, kc, tb * 128:(tb + 1) * 128], Wuv.t[:, kc, :], kc == 0, kc == 1, [Wuv.b, ckvn.b], [p.b])
                v = vst.next()
                cp(v.t[:, :, 0:64], p.t[:, :].rearrange("p (h d) -> p h d", h=8), [p.b], [v.b])
                dma(v.b.name, VM[c0 + tb * 128:c0 + (tb + 1) * 128, :], v.t.rearrange("p h d -> p (h d)"), [v.b], [DB("VM", hb)])
            p = PS.next()
            psw = PS.next()
            for kc in range(8):
                mm(p.t[0:96, :N], Wkrp.t[:, kc, :], xm.t[:, kc, :N], kc == 0, kc == 7, [Wkrp.b, xm.b], [p.b])
            if not isctx:
                for kc in range(8):
                    mm(psw.t[0:96, :N], Wkrs.t[:, kc, :], xm.t[:, kc, :N], kc == 0, kc == 7, [Wkrs.b, xm.b], [psw.b])
            qt = qtr.next()
            rope_out(qt, p, psw, 64, 96, R.t[:, 0, :], R.t[:, 1, :], R.b, N, isctx)
            dma(qt.b.name, KRT[:, c0:c0 + N], qt.t[64:96, :N], [qt.b], [DB("KRT", hb)])
            for j in range(4):
                p = PS.next()
                for kc in range(8):
                    mm(p.t[:, :N], Wu.t[:, kc, j * 128:(j + 1) * 128], xm.t[:, kc, :N], kc == 0, kc == 7, [Wu.b, xm.b], [p.b])
                ut = utr.next()
                act(ut.t[:, :N], p.t[:, :N], AF.Identity, [p.b], [ut.b])
                dma(ut.b.name, UT[j * 128:(j + 1) * 128, c0:c0 + N], ut.t[:, :N], [ut.b], [DB("UT", hb)])
            for j in range(5):
                Wa, Ws, cs = (Wgq, Wgqs, slice(j * 128, (j + 1) * 128)) if j < 4 else (Wgk, Wgks, slice(0, 128))
                p = PS.next()
                psw = PS.next()
                for kc in range(8):
                    mm(p.t[:, :N], Wa.t[:, kc, cs], xm.t[:, kc, :N], kc == 0, kc == 7, [Wa.b, xm.b], [p.b])
                if not isctx:
                    for kc in range(8):
                        mm(psw.t[:, :N], Ws.t[:, kc, cs], xm.t[:, kc, :N], kc == 0, kc == 7, [Ws.b, xm.b], [psw.b])
                qt = qtr.next()
                rope_out(qt, p, psw, 0, 128, R.t[:, 2, :], R.t[:, 3, :], R.b, N, isctx)
                if j < 4:
                    dma(qt.b.name, GQT[j * 128:(j + 1) * 128, c0:c0 + N], qt.t[:, :N], [qt.b], [DB("GQT", hb)])
                else:
                    dma(qt.b.name, GKT[:, c0:c0 + N], qt.t[:, :N], [qt.b], [DB("GKT", hb)])
            for tb in range(N // 128):
                p = PS.next()
                for kc in range(8):
                    mm(p.t[:, 0:128], xm.t[:, kc, tb * 128:(tb + 1) * 128], Wgv.t[:, kc, :], kc == 0, kc == 7, [Wgv.b, xm.b], [p.b])
                v = vst2.next()
                cp(v.t[:, :, 0:64], p.t[:, 0:128].rearrange("p (h d) -> p h d", h=2), [p.b], [v.b])
                dma(v.b.name, GVM[c0 + tb * 128:c0 + (tb + 1) * 128, :], v.t.rearrange("p h d -> p (h d)"), [v.b], [DB("GVM", hb)])
        S.barrier()
        A.release(m)

    def finalize_attn(po, N, osb, rec, extra_den, dst_dram_ap, dst_key, otile):
        act(osb.t[0:65, :N], po.t[0:65, :N], AF.Identity, [po.b], [osb.b])
        if extra_den is not None:
            ap_, b_ = extra_den
            tt(osb.t[64:65, :N], osb.t[64:65, :N], ap_, ALU.add, [osb.b, b_], [osb.b])
        recip(rec.t[64:65, :N], osb.t[64:65, :N], [osb.b], [rec.b])
        pb = PS.next()
        mm(pb.t[0:64, :N], ones_f.t[64:65, 0:64], rec.t[64:65, :N], True, True, [ones_f.b, rec.b], [pb.b])
        tt(otile.t[0:64, :N], osb.t[0:64, :N], pb.t[0:64, :N], ALU.mult, [osb.b, pb.b], [otile.b])
        dma(otile.b.name, dst_dram_ap, otile.t[0:64, :N], [otile.b], [dst_key])

    def mla_phase(l, ctx_out):
        m = A.mark()
        Vall = A.alloc([NKT, 520], BF16, "Vall")
        dma("Vall", Vall.t, VM.rearrange("(kt p) c -> p kt c", p=128), DBall("VM"), [Vall.b])
        Kr = A.rot(2, [S_], BF16, "Kh")
        Qr = A.rot(2, [S_], BF16, "Qh")
        pTr = A.rot(4, [512], BF16, "pT")
        osbr = A.rot(2, [512], F32, "osb")
        recr = A.rot(2, [512], F32, "rec")
        otr = A.rot(2, [512], BF16, "ot")
        scale = 96 ** -0.5
        qblocks = ([(0, CTX, 2)] if ctx_out else []) + [(CTX + i * 512, 512, NKT) for i in range(NTL)]
        for h in range(8):
            Kh = Kr.next()
            Qh = Qr.next()
            dma(Kh.b.name, Kh.t[0:64, :], KNT[h * 64:(h + 1) * 64, :], DBall("KNT"), [Kh.b])
            dma(Kh.b.name, Kh.t[64:96, :], KRT[:, :], DBall("KRT"), [Kh.b])
            dma(Qh.b.name, Qh.t[0:96, :], QT[h, :, :], DBall("QT"), [Qh.b])
            for (c0, N, nk) in qblocks:
                po = PSA.next()
                for kt in range(nk):
                    ps = PS.next()
                    mm(ps.t[:, :N], Kh.t[0:96, kt * 128:(kt + 1) * 128], Qh.t[0:96, c0:c0 + N], True, True, [Kh.b, Qh.b], [ps.b])
                    pT = pTr.next()
                    act(pT.t[:, :N], ps.t[:, :N], AF.Exp, [ps.b], [pT.b], scale=scale)
                    mm(po.t[0:65, :N], Vall.t[:, kt, h * 65:(h + 1) * 65], pT.t[:, :N], kt == 0, kt == nk - 1, [Vall.b, pT.b], [po.b])
                finalize_attn(po, N, osbr.next(), recr.next(), None, OMT[h * 64:(h + 1) * 64, c0:c0 + N],
                              DB("OMT", (h, c0)), otr.next())
        S.barrier()
        A.release(m)

    def gqa_phase(l, ctx_out):
        m = A.mark()
        V2 = A.alloc([NKT, 130], BF16, "V2")
        dma("V2", V2.t, GVM.rearrange("(kt p) c -> p kt c", p=128), DBall("GVM"), [V2.b])
        mL = A.alloc([512], BF16, "mL")
        mU = A.alloc([512], BF16, "mU")
        dma("mL", mL.t, dr["maskL"], (), [mL.b], eng=WENG)
        dma("mU", mU.t, dr["maskU"], (), [mU.b], eng=WENG)
        skx = A.alloc([8, 128], F32, "skx")
        dma("skx", skx.t[64:65, :, :], dr["sinkb"][:, l, :, :], (), [skx.b])
        act(skx.t[64:65, :, :], skx.t[64:65, :, :], AF.Exp, [skx.b], [skx.b])
        K2r = A.rot(1, [S_], BF16, "K2")
        Q2r = A.rot(1, [4, S_], BF16, "Q2")
        pTr = A.rot(4, [512], BF16, "pT")
        osbr = A.rot(2, [512], F32, "osb")
        recr = A.rot(2, [512], F32, "rec")
        otr = A.rot(2, [4, 128], BF16, "ot")
        nb = T // 128
        for kvh in range(2):
            K2 = K2r.next()
            Q2 = Q2r.next()
            dma(K2.b.name, K2.t[0:64, :], GKT[kvh * 64:(kvh + 1) * 64, :], DBall("GKT"), [K2.b])
            dma(Q2.b.name, Q2.t[0:64, :, :], GQT[kvh * 256:(kvh + 1) * 256, :].rearrange("(g d) s -> d g s", g=4),
                DBall("GQT"), [Q2.b])
            blocks = []
            if ctx_out:
                for cb in range(2):
                    blocks.append((cb * 128, [(0, None), (1, None)]))
            for n in range(nb):
                kts = [(0, None), (1, None)]
                if n > 0:
                    kts.append((2 + n - 1, mL))
                kts.append((2 + n, None))
                if n < nb - 1:
                    kts.append((2 + n + 1, mU))
                blocks.append((CTX + n * 128, kts))
            for (c0, kts) in blocks:
                po = PSA.next()
                for i, (kt, msk) in enumerate(kts):
                    ps = PS.next()
                    mm(ps.t[:, :], K2.t[0:64, kt * 128:(kt + 1) * 128], Q2.t[0:64, :, c0:c0 + 128], True, True, [K2.b, Q2.b], [ps.b])
                    pT = pTr.next()
                    act(pT.t[:, :], ps.t[:, :], AF.Exp, [ps.b], [pT.b], scale=0.125)
                    if msk is not None:
                        tt(pT.t[:, :], pT.t[:, :], msk.t, ALU.mult, [pT.b, msk.b], [pT.b])
                    mm(po.t[0:65, :], V2.t[:, kt, kvh * 65:(kvh + 1) * 65], pT.t[:, :], i == 0, i == len(kts) - 1, [V2.b, pT.b], [po.b])
                ot = otr.next()
                osb = osbr.next()
                rec = recr.next()
                act(osb.t[0:65, :], po.t[0:65, :], AF.Identity, [po.b], [osb.b])
                tt(osb.t[64:65, :], osb.t[64:65, :], skx.t[64:65, kvh * 4:(kvh + 1) * 4, :].rearrange("p g q -> p (g q)"),
                   ALU.add, [osb.b, skx.b], [osb.b])
                recip(rec.t[64:65, :], osb.t[64:65, :], [osb.b], [rec.b])
                pb = PS.next()
                mm(pb.t[0:64, :], ones_f.t[64:65, 0:64], rec.t[64:65, :], True, True, [ones_f.b, rec.b], [pb.b])
                tt(ot.t[0:64, :, :].rearrange("p g q -> p (g q)"), osb.t[0:64, :], pb.t[0:64, :], ALU.mult, [osb.b, pb.b], [ot.b])
                dma(ot.b.name, OGT[kvh * 256:(kvh + 1) * 256, c0:c0 + 128].rearrange("(g d) q -> d g q", g=4), ot.t[0:64, :, :],
                    [ot.b], [DB("OGT", (kvh, c0))])
        S.barrier()
        A.release(m)

    def ssm_phase(l, ctx_out):
        m = A.mark()
        LC = 512
        par = {}
        for nm in ("lamre", "lamim", "logdt"):
            par[nm] = A.alloc([2, 16], F32, nm)
            dma(nm, par[nm].t, dr[nm][:, l, :, :], (), [par[nm].b])
        Bre = A.alloc([2, 16, 16], F32, "Bre")
        Bim = A.alloc([2, 16, 16], F32, "Bim")
        dma("Bre", Bre.t, dr["Bre"][:, l], (), [Bre.b])
        dma("Bim", Bim.t, dr["Bim"][:, l], (), [Bim.b])
        Cre = A.alloc([2, 16, 128], BF16, "Cre")
        Cim = A.alloc([2, 16, 128], BF16, "Cim")
        for d in range(2):
            if WENG == "pool":
                dma("Cre", Cre.t[:, d], dr["Cxre"][:, l, d], (), [Cre.b], eng="pool", max_dma_last_dim=4096)
                dma("Cim", Cim.t[:, d], dr["Cxim"][:, l, d], (), [Cim.b], eng="pool", max_dma_last_dim=4096)
            else:
                dma("Cre", Cre.t[:, d], dr["Cxre"][:, l, d], (), [Cre.b])
                dma("Cim", Cim.t[:, d], dr["Cxim"][:, l, d], (), [Cim.b])
        sm = {}
        for nm in ("dt", "lrdt", "lidt", "mag", "cs", "sn", "are", "aim", "den", "wre", "wim", "x1", "x2", "x3"):
            sm[nm] = A.alloc([2, 16], F32, "s_" + nm)
        halfpi = A.alloc([1], F32, "halfpi")
        memset(halfpi.t, math.pi / 2, [halfpi.b])
        P = par
        act(sm["dt"].t, P["logdt"].t, AF.Exp, [P["logdt"].b], [sm["dt"].b])
        tt(sm["lrdt"].t, P["lamre"].t, sm["dt"].t, ALU.mult, [P["lamre"].b, sm["dt"].b], [sm["lrdt"].b])
        tt(sm["lidt"].t, P["lamim"].t, sm["dt"].t, ALU.mult, [P["lamim"].b, sm["dt"].b], [sm["lidt"].b])
        act(sm["mag"].t, sm["lrdt"].t, AF.Exp, [sm["lrdt"].b], [sm["mag"].b])
        act(sm["sn"].t, sm["lidt"].t, AF.Sin, [sm["lidt"].b], [sm["sn"].b], scale=1.0 / 16)
        act(sm["cs"].t, sm["lidt"].t, AF.Sin, [sm["lidt"].b, halfpi.b], [sm["cs"].b], scale=1.0 / 16, bias=halfpi.t[:, 0:1])
        for _ in range(4):
            tt(sm["x1"].t, sm["cs"].t, sm["cs"].t, ALU.mult, [sm["cs"].b], [sm["x1"].b])
            tt(sm["x2"].t, sm["sn"].t, sm["sn"].t, ALU.mult, [sm["sn"].b], [sm["x2"].b])
            tt(sm["x3"].t, sm["cs"].t, sm["sn"].t, ALU.mult, [sm["cs"].b, sm["sn"].b], [sm["x3"].b])
            tt(sm["cs"].t, sm["x1"].t, sm["x2"].t, ALU.subtract, [sm["x1"].b, sm["x2"].b], [sm["cs"].b])
            ts(sm["sn"].t, sm["x3"].t, 2.0, None, ALU.mult, None, [sm["x3"].b], [sm["sn"].b])
        tt(sm["are"].t, sm["mag"].t, sm["cs"].t, ALU.mult, [sm["mag"].b, sm["cs"].b], [sm["are"].b])
        tt(sm["aim"].t, sm["mag"].t, sm["sn"].t, ALU.mult, [sm["mag"].b, sm["sn"].b], [sm["aim"].b])
        tt(sm["x1"].t, P["lamre"].t, P["lamre"].t, ALU.mult, [P["lamre"].b], [sm["x1"].b])
        tt(sm["x2"].t, P["lamim"].t, P["lamim"].t, ALU.mult, [P["lamim"].b], [sm["x2"].b])
        tt(sm["den"].t, sm["x1"].t, sm["x2"].t, ALU.add, [sm["x1"].b, sm["x2"].b], [sm["den"].b])
        recip(sm["den"].t, sm["den"].t, [sm["den"].b], [sm["den"].b])
        ts(sm["x3"].t, sm["are"].t, -1.0, None, ALU.add, None, [sm["are"].b], [sm["x3"].b])
        tt(sm["x1"].t, sm["x3"].t, P["lamre"].t, ALU.mult, [sm["x3"].b, P["lamre"].b], [sm["x1"].b])
        tt(sm["x2"].t, sm["aim"].t, P["lamim"].t, ALU.mult, [sm["aim"].b, P["lamim"].b], [sm["x2"].b])
        tt(sm["wre"].t, sm["x1"].t, sm["x2"].t, ALU.add, [sm["x1"].b, sm["x2"].b], [sm["wre"].b])
        tt(sm["wre"].t, sm["wre"].t, sm["den"].t, ALU.mult, [sm["wre"].b, sm["den"].b], [sm["wre"].b])
        tt(sm["x1"].t, sm["aim"].t, P["lamre"].t, ALU.mult, [sm["aim"].b, P["lamre"].b], [sm["x1"].b])
        tt(sm["x2"].t, sm["x3"].t, P["lamim"].t, ALU.mult, [sm["x3"].b, P["lamim"].b], [sm["x2"].b])
        tt(sm["wim"].t, sm["x1"].t, sm["x2"].t, ALU.subtract, [sm["x1"].b, sm["x2"].b], [sm["wim"].b])
        tt(sm["wim"].t, sm["wim"].t, sm["den"].t, ALU.mult, [sm["wim"].b, sm["den"].b], [sm["wim"].b])
        bbx = {}
        for ri in ("re", "im"):
            bbx[ri] = A.alloc([2, 16, 2, 16], F32, "bbx" + ri)
            memset(bbx[ri].t, 0.0, [bbx[ri].b])
        tmpb = A.alloc([16], F32, "tmpb")
        for d in range(2):
            for st in range(16):
                wre = sm["wre"].t[:, d, st:st + 1]
                wim = sm["wim"].t[:, d, st:st + 1]
                for gi in range(2):
                    lo, hi = gi * 64, gi * 64 + 64
                    rd = [sm["wre"].b, sm["wim"].b, Bre.b, Bim.b, tmpb.b]
                    sr_ = [sm["wre"].b, sm["wim"].b]
                    ts(tmpb.t[lo:hi, :], Bim.t[lo:hi, d, st, :], wim[lo:hi], None, ALU.mult, None, rd, [tmpb.b], sreads=sr_)
                    stt(bbx["re"].t[lo:hi, d, st, gi, :], Bre.t[lo:hi, d, st, :], wre[lo:hi], tmpb.t[lo:hi, :], ALU.mult, ALU.subtract,
                        rd, [bbx["re"].b], sreads=sr_)
                    ts(tmpb.t[lo:hi, :], Bre.t[lo:hi, d, st, :], wim[lo:hi], None, ALU.mult, None, rd, [tmpb.b], sreads=sr_)
                    stt(bbx["im"].t[lo:hi, d, st, gi, :], Bim.t[lo:hi, d, st, :], wre[lo:hi], tmpb.t[lo:hi, :], ALU.mult, ALU.add,
                        rd, [bbx["im"].b], sreads=sr_)
        BT = {}
        for ri in ("re", "im"):
            BT[ri] = A.alloc([2, 4, 128], BF16, "BT" + ri)
            for d in range(2):
                for q in range(4):
                    p = PS.next()
                    src = bbx[ri].t[:, d, q * 4:(q + 1) * 4, :, :].rearrange("p a b c -> p (a b c)")
                    S.op("pe", lambda e, p=p, src=src: e.transpose(out=p.t[:, 0:128], in_=src, identity=ident.t),
                         [bbx[ri].b, ident.b], [p.b])
                    cp(BT[ri].t[:, d, q, :], p.t[:, 0:128], [p.b], [BT[ri].b])
        BT3 = {}
        bz = A.alloc([4, 2, 16], F32, "bz")
        memset(bz.t, 0.0, [bz.b])
        for ri in ("re", "im"):
            BT3[ri] = A.alloc([2, 4, 128], BF16, "BT3" + ri)
            for d in range(2):
                for q in range(4):
                    cp(bz.t[:, 3, :, :], bbx[ri].t[:, d, q * 4 + 3, :, :], [bbx[ri].b], [bz.b])
                    p = PS.next()
                    src = bz.t.rearrange("p a b c -> p (a b c)")
                    S.op("pe", lambda e, p=p, src=src: e.transpose(out=p.t[:, 0:128], in_=src, identity=ident.t),
                         [bz.b, ident.b], [p.b])
                    cp(BT3[ri].t[:, d, q, :], p.t[:, 0:128], [p.b], [BT3[ri].b])
        ECOS = A.alloc([16, LC], F32, "ECOS")
        ESIN = A.alloc([16, LC], F32, "ESIN")
        carry = A.alloc([16, 2], F32, "carry")
        uf = A.rot(2, [4, LC], F32, "uf")
        ub = A.rot(2, [4, LC], BF16, "ub")
        zr = A.rot(2, [LC], F32, "zr")
        zi = A.rot(2, [LC], F32, "zi")
        t1r = A.rot(3, [LC], F32, "st1")
        t2r = A.rot(3, [LC], F32, "st2")
        srr = A.rot(2, [LC], F32, "sr")
        sir = A.rot(2, [LC], F32, "si")
        srb = A.rot(8, [LC], BF16, "srb")
        sib = A.rot(8, [LC], BF16, "sib")
        yt = A.rot(2, [LC], F32, "yt")
        yo = A.rot(2, [LC], F32, "yo")
        g1 = A.rot(2, [LC], F32, "g1")
        gyb = A.rot(2, [LC], BF16, "gyb")
        chunks_f = [(0, CTX, 0)] + [(CTX + i * LC, LC, 1 + i) for i in range(T // LC)]
        for d in range(2):
            rd0 = [sm["cs"].b, sm["sn"].b]
            cp(ECOS.t[:, :, 0], sm["cs"].t[:, d, :], rd0, [ECOS.b])
            cp(ESIN.t[:, :, 0], sm["sn"].t[:, d, :], rd0, [ESIN.b])
            w = 1
            while w < LC:
                for st in range(16):
                    c_ = ECOS.t[:, st, w - 1:w]
                    s_ = ESIN.t[:, st, w - 1:w]
                    a1 = t1r.next()
                    a2 = t2r.next()
                    ts(a1.t[:, :w], ESIN.t[:, st, 0:w], s_, None, ALU.mult, None, [ESIN.b], [a1.b], sreads=[ESIN.b, ECOS.b])
                    ts(a2.t[:, :w], ECOS.t[:, st, 0:w], s_, None, ALU.mult, None, [ECOS.b, ESIN.b], [a2.b], sreads=[ESIN.b, ECOS.b])
                    stt(ECOS.t[:, st, w:2 * w], ECOS.t[:, st, 0:w], c_, a1.t[:, :w], ALU.mult, ALU.subtract, [ECOS.b, a1.b], [ECOS.b], sreads=[ESIN.b, ECOS.b])
                    stt(ESIN.t[:, st, w:2 * w], ESIN.t[:, st, 0:w], c_, a2.t[:, :w], ALU.mult, ALU.add, [ESIN.b, ECOS.b, a2.b], [ESIN.b], sreads=[ESIN.b, ECOS.b])
                w *= 2
            if d == 0:
                order = [(c, False) for c in chunks_f]
            else:
                order = [(chunks_f[0], True)] + [(c, True) for c in reversed(chunks_f[1:])]
            memset(carry.t, 0.0, [carry.b])
            for ((c0, N, hb), rev) in order:
                isctx = c0 == 0
                U = uf.next()
                Ub = ub.next()
                dma(U.b.name, U.t[:, :, :N], UT[:, c0:c0 + N].rearrange("(q p) n -> p q n", p=128), DBall("UT"), [U.b])
                cp(Ub.t[:, :, :N], U.t[:, :, :N], [U.b], [Ub.b], eng="pool")

                def rv(ap):
                    return ap[:, N - 1::-1] if rev else ap[:, 0:N]
                for q in range(4):
                    sbs = []
                    for sti in range(4):
                        st = q * 4 + sti
                        lo = sti * 32
                        pr = PS.next()
                        pi = PS.next()
                        if sti < 3:
                            urhs = rv(Ub.t[lo:lo + 32, q, :N]) if rev else Ub.t[lo:lo + 32, q, :N]
                            lre = BT["re"].t[lo:lo + 32, d, q, :]
                            lim = BT["im"].t[lo:lo + 32, d, q, :]
                        else:
                            urhs = rv(Ub.t[64:128, q, :N]) if rev else Ub.t[64:128, q, :N]
                            lre = BT3["re"].t[64:128, d, q, :]
                            lim = BT3["im"].t[64:128, d, q, :]
                        mm(pr.t[:, :N], lre, urhs, True, True, [BT["re"].b, BT3["re"].b, Ub.b], [pr.b])
                        mm(pi.t[:, :N], lim, urhs, True, True, [BT["im"].b, BT3["im"].b, Ub.b], [pi.b])
                        ec = ECOS.t[:, st, :N]
                        es = ESIN.t[:, st, :N]
                        a1 = t1r.next(); a2 = t2r.next(); ZR = zr.next(); ZI = zi.next()
                        tt(a1.t[:, :N], pr.t[:, :N], ec, ALU.mult, [pr.b, ECOS.b], [a1.b])
                        tt(a2.t[:, :N], pi.t[:, :N], es, ALU.mult, [pi.b, ESIN.b], [a2.b])
                        tt(ZR.t[:, :N], a1.t[:, :N], a2.t[:, :N], ALU.add, [a1.b, a2.b], [ZR.b], eng="pool")
                        a1 = t1r.next(); a2 = t2r.next()
                        tt(a1.t[:, :N], pi.t[:, :N], ec, ALU.mult, [pi.b, ECOS.b], [a1.b])
                        tt(a2.t[:, :N], pr.t[:, :N], es, ALU.mult, [pr.b, ESIN.b], [a2.b])
                        tt(ZI.t[:, :N], a1.t[:, :N], a2.t[:, :N], ALU.subtract, [a1.b, a2.b], [ZI.b], eng="pool")
                        rbc = sm["mag"].t[:, d, st:st + 1].to_broadcast([128, N])
                        scan(ZR.t[:, :N], rbc, ZR.t[:, :N], carry.t[:, st, 0:1], [ZR.b, carry.b, sm["mag"].b], [ZR.b])
                        scan(ZI.t[:, :N], rbc, ZI.t[:, :N], carry.t[:, st, 1:2], [ZI.b, carry.b, sm["mag"].b], [ZI.b])
                        SR = srr.next(); SI = sir.next()
                        a1 = t1r.next(); a2 = t2r.next()
                        tt(a1.t[:, :N], ZR.t[:, :N], ec, ALU.mult, [ZR.b, ECOS.b], [a1.b])
                        tt(a2.t[:, :N], ZI.t[:, :N], es, ALU.mult, [ZI.b, ESIN.b], [a2.b], eng="pool")
                        tt(SR.t[:, :N], a1.t[:, :N], a2.t[:, :N], ALU.subtract, [a1.b, a2.b], [SR.b])
                        a1 = t1r.next(); a2 = t2r.next()
                        tt(a1.t[:, :N], ZR.t[:, :N], es, ALU.mult, [ZR.b, ESIN.b], [a1.b])
                        tt(a2.t[:, :N], ZI.t[:, :N], ec, ALU.mult, [ZI.b, ECOS.b], [a2.b], eng="pool")
                        tt(SI.t[:, :N], a1.t[:, :N], a2.t[:, :N], ALU.add, [a1.b, a2.b], [SI.b])
                        act(carry.t[:, st, 0:1], SR.t[:, N - 1:N], AF.Identity, [SR.b], [carry.b])
                        act(carry.t[:, st, 1:2], SI.t[:, N - 1:N], AF.Identity, [SI.b], [carry.b])
                        SRb = srb.next(); SIb = sib.next()
                        act(SRb.t[:, :N], SR.t[:, :N], AF.Identity, [SR.b], [SRb.b])
                        act(SIb.t[:, :N], SI.t[:, :N], AF.Identity, [SI.b], [SIb.b], scale=-1.0)
                        sbs.append((st, SRb, SIb))
                    if isctx and not ctx_out:
                        continue
                    py = PSA.next()
                    for i, (st, SRb, SIb) in enumerate(sbs):
                        mm(py.t[:, :N], Cre.t[:, d, st, :], rv(SRb.t[:, :N]) if rev else SRb.t[:, :N], i == 0, False, [Cre.b, SRb.b], [py.b])
                        mm(py.t[:, :N], Cim.t[:, d, st, :], rv(SIb.t[:, :N]) if rev else SIb.t[:, :N], False, i == 3, [Cim.b, SIb.b], [py.b])
                    Y = yt.next()
                    if d == 0:
                        stt(Y.t[:, :N], U.t[:, q, :N], sdT.t[:, l, q:q + 1], py.t[:, :N], ALU.mult, ALU.add, [U.b, sdT.b, py.b], [Y.b])
                        dma(Y.b.name, YT[q * 128:(q + 1) * 128, c0:c0 + N], Y.t[:, :N], [Y.b], [DB("YT", (q, c0))])
                    else:
                        Yo = yo.next()
                        dma(Yo.b.name, Yo.t[:, :N], YT[q * 128:(q + 1) * 128, c0:c0 + N], [DB("YT", (q, c0))], [Yo.b])
                        tt(Y.t[:, :N], Yo.t[:, :N], py.t[:, :N], ALU.add, [Yo.b, py.b], [Y.b])
                        G = g1.next()
                        tt(G.t[:, :N], Y.t[:, :N], Y.t[:, :N], ALU.mult, [Y.b], [G.b], eng="pool")
                        ts(G.t[:, :N], G.t[:, :N], 0.044715, 1.0, ALU.mult, ALU.add, [G.b], [G.b], eng="pool")
                        tt(G.t[:, :N], G.t[:, :N], Y.t[:, :N], ALU.mult, [G.b, Y.b], [G.b], eng="pool")
                        act(G.t[:, :N], G.t[:, :N], AF.Sigmoid, [G.b], [G.b], scale=1.5957691216057308)
                        Gb = gyb.next()
                        tt(Gb.t[:, :N], G.t[:, :N], Y.t[:, :N], ALU.mult, [G.b, Y.b], [Gb.b], eng="pool")
                        dma(Gb.b.name, GYT[q * 128:(q + 1) * 128, c0:c0 + N], Gb.t[:, :N], [Gb.b], [DB("GYT", (q, c0))])
        for nm in ("are", "aim", "wre", "wim", "mag", "cs", "sn"):
            dbg_dump(nm, sm[nm])
        dbg_dump("ECOS", ECOS)
        dbg_dump("ESIN", ESIN)
        dbg_dump("carry", carry)
        dbg_dump("BTre", BT["re"], BF16)
        dbg_dump("bbxre", bbx["re"])
        dbg_dump("Bre", Bre)
        S.barrier()
        A.release(m)

    def merge_phase(l, ctx_out):
        m = A.mark()
        Wg = A.alloc([8, 3072], BF16, "Wg"); wload(Wg, dr["w_in"][l][:, 1952:5024])
        Wmo = A.alloc([4, D], BF16, "Wmo"); wload(Wmo, dr["mla_w_o"][l])
        Wgl = A.alloc([4, 2 * D], BF16, "Wgl"); wload(Wgl, dr["ssm_w_glu"][l])
        Wgo = A.alloc([4, D], BF16, "Wgo"); wload(Wgo, dr["gqa_w_o"][l])
        Wo = A.alloc([8, D], BF16, "Wo"); wload(Wo, dr["w_out"][l])
        Hr = A.rot(2, [8, 512], F32, "H")
        xmr = A.rot(2, [8, 512], BF16, "xm")
        inr = A.rot(2, [3, 4, 512], BF16, "ins")
        mg = A.alloc([8, 512], BF16, "mg")
        sgr = A.rot(4, [512], BF16, "sg")
        b1r = A.rot(2, [512], F32, "b1")
        accr = A.rot(2, [512], F32, "acc")
        t3r = A.rot(2, [512], F32, "t3")
        for (c0, N, isctx, hb) in tiles512:
            if isctx and not ctx_out:
                continue
            k = 1 if isctx else 0
            H = Hr.next(); xm = xmr.next(); I = inr.next()
            dma(H.b.name, H.t[:, :, :N], hT[:, c0:c0 + N].rearrange("(j p) n -> p j n", p=128), [DB("hT", hb)], [H.b])
            dma(xm.b.name, xm.t[:, :, :N], XMT[:, c0:c0 + N].rearrange("(j p) n -> p j n", p=128), DBall("XMT"), [xm.b])
            for i, (src, nm) in enumerate(((OMT, "OMT"), (GYT, "GYT"), (OGT, "OGT"))):
                dma(I.b.name, I.t[:, i, :, :N], src[:, c0:c0 + N].rearrange("(j p) n -> p j n", p=128), DBall(nm), [I.b])
            for dd in range(8):
                cs = slice(dd * 128, (dd + 1) * 128)
                sg = []
                for gi in range(3):
                    p = PS.next()
                    for kc in range(8):
                        mm(p.t[:, :N], Wg.t[:, kc, gi * D + dd * 128:gi * D + (dd + 1) * 128], xm.t[:, kc, :N], kc == 0, kc == 7, [Wg.b, xm.b], [p.b])
                    s_ = sgr.next()
                    act(s_.t[:, :N], p.t[:, :N], AF.Sigmoid, [p.b], [s_.b])
                    sg.append(s_)
                p0 = PS.next()
                for kc in range(4):
                    mm(p0.t[:, :N], Wmo.t[:, kc, cs], I.t[:, 0, kc, :N], kc == 0, kc == 3, [Wmo.b, I.b], [p0.b])
                acc = accr.next()
                tt(acc.t[:, :N], sg[0].t[:, :N], p0.t[:, :N], ALU.mult, [sg[0].b, p0.b], [acc.b])
                pa = PS.next(); pg = PS.next()
                for kc in range(4):
                    mm(pa.t[:, :N], Wgl.t[:, kc, cs], I.t[:, 1, kc, :N], kc == 0, kc == 3, [Wgl.b, I.b], [pa.b])
                for kc in range(4):
                    mm(pg.t[:, :N], Wgl.t[:, kc, D + dd * 128:D + (dd + 1) * 128], I.t[:, 1, kc, :N], kc == 0, kc == 3, [Wgl.b, I.b], [pg.b])
                s3 = sgr.next()
                act(s3.t[:, :N], pg.t[:, :N], AF.Sigmoid, [pg.b], [s3.b])
                b1 = b1r.next()
                tt(b1.t[:, :N], s3.t[:, :N], pa.t[:, :N], ALU.mult, [s3.b, pa.b], [b1.b])
                t3 = t3r.next()
                tt(t3.t[:, :N], b1.t[:, :N], sg[1].t[:, :N], ALU.mult, [b1.b, sg[1].b], [t3.b], eng="pool")
                p2 = PS.next()
                for kc in range(4):
                    mm(p2.t[:, :N], Wgo.t[:, kc, cs], I.t[:, 2, kc, :N], kc == 0, kc == 3, [Wgo.b, I.b], [p2.b])
                b2 = b1r.next()
                tt(b2.t[:, :N], sg[2].t[:, :N], p2.t[:, :N], ALU.mult, [sg[2].b, p2.b], [b2.b])
                tt(acc.t[:, :N], acc.t[:, :N], t3.t[:, :N], ALU.add, [acc.b, t3.b], [acc.b], eng="pool")
                tt(mg.t[:, dd, :N], acc.t[:, :N], b2.t[:, :N], ALU.add, [acc.b, b2.b], [mg.b], eng="pool")
            for dd in range(8):
                po = PS.next()
                for kc in range(8):
                    mm(po.t[:, :N], Wo.t[:, kc, dd * 128:(dd + 1) * 128], mg.t[:, kc, :N], kc == 0, kc == 7, [Wo.b, mg.b], [po.b])
                stt(H.t[:, dd, :N], po.t[:, :N], hgT.t[:, l, 1, dd, k:k + 1], H.t[:, dd, :N], ALU.mult, ALU.add, [po.b, hgT.b, H.b], [H.b])
            dma(H.b.name, hT[:, c0:c0 + N].rearrange("(j p) n -> p j n", p=128), H.t[:, :, :N], [H.b], [DB("hT", hb)])
        S.barrier()
        A.release(m)

    def final_phase():
        m = A.mark()
        Hr = A.rot(2, [8, 512], F32, "H")
        ot = A.alloc([8, 512], F32, "ot")
        sq = A.alloc([8, 512], BF16, "sq")
        rstd = A.alloc([512], F32, "rstd")
        for (c0, N, isctx, hb) in tiles512:
            if isctx:
                continue
            H = Hr.next()
            dma(H.b.name, H.t[:, :, :N], hT[:, c0:c0 + N].rearrange("(j p) n -> p j n", p=128), [DB("hT", hb)], [H.b])
            norm_mod(H, N, 8, lambda j: fnT.t[:, j:j + 1], None, ot, sq, rstd, None, D, extra_reads=[fnT.b])
            dma(ot.b.name, outT[:, c0 - CTX:c0 - CTX + N].rearrange("(j p) n -> p j n", p=128), ot.t[:, :, :N],
                [ot.b], [DB("outT", 0)])
        S.op("sp", None, reads=DBall("outT"))
        A.release(m)

    PH = os.environ.get("PHASES", "fpmsgeF")

    def layer_body():
        if "f" in PH:
            ffn_phase(0, 0, dr["ffn1_w13"][0], dr["ffn1_w2"][0], False, False)
        if "p" in PH:
            proj_phase(0)
        if "m" in PH:
            mla_phase(0, True)
        if "s" in PH:
            ssm_phase(0, True)
        if "g" in PH:
            gqa_phase(0, True)
        if "e" in PH:
            merge_phase(0, True)
        if "F" in PH:
            ffn_phase(0, 2, dr["ffn2_w13"][0], dr["ffn2_w2"][0], False, False)

    scheds = [S]
    if mode == "final":
        final_phase()
        S.emit()
    elif mode == "layer":
        layer_body()
        S.op("sp", None, reads=DBall("hT"))
        S.emit()
    elif mode == "loop":
        S.emit()
        Buf.reset_all()
        S = Sched(nc)
        scheds.append(S)
        setup_layers()
        layer_body()
        for k in layered:
            for l in range(L - 1):
                dma("shift", lslice(dr[k], k, l), lslice(dr[k], k, l + 1), [DB("wc_" + k, 0)], [DB("wc_" + k, 0)])
        S.barrier()
        with nc.Fori(0, L):
            S.emit()
            S.reset_sems()
        Buf.reset_all()
        S = Sched(nc)
        scheds.append(S)
        final_phase()
        S.emit()
    else:
        for l in range(L):
            ctx_out = l < L - 1
            ffn_phase(l, 0, dr["ffn1_w13"][l], dr["ffn1_w2"][l], False, False)
            proj_phase(l)
            mla_phase(l, ctx_out)
            ssm_phase(l, ctx_out)
            gqa_phase(l, ctx_out)
            merge_phase(l, ctx_out)
            ffn_phase(l, 2, dr["ffn2_w13"][l], dr["ffn2_w2"][l], not ctx_out, l == L - 1)
        S.op("sp", None, reads=DBall("outT"))
        S.emit()
    nc._keep = scheds
    return nc


_CACHE = {}


def run(inputs, T=None, L=None, debug=False, mode="loop"):
    x = np.asarray(inputs["x"])
    B = x.shape[0]
    T = T or x.shape[1]
    L = L or np.asarray(inputs["ada_w"]).shape[0]
    sh, pcs = host_prep(inputs, T, L)
    shapes = {k: (v.shape, v.dtype) for k, v in list(sh.items()) + list(pcs[0].items())}
    key = (T, L, debug, mode)
    if key not in _CACHE:
        _CACHE[key] = build(T, L, shapes, debug, mode=mode)
    nc = _CACHE[key]
    in_maps = []
    for c in range(8):
        mp = dict(sh)
        mp.update(pcs[c % B])
        in_maps.append(mp)
    res = run_bass_kernel_spmd(nc, in_maps, core_ids=list(range(8)))
    out = np.stack([np.ascontiguousarray(res.results[b]["outT"].T) for b in range(B)], 0)
    return out.astype(np.float32), res


def run_layers(inputs):
    x = np.asarray(inputs["x"])
    B, T = x.shape[0], x.shape[1]
    L = np.asarray(inputs["ada_w"]).shape[0]
    h = [np.ascontiguousarray(np.concatenate([np.asarray(inputs["ctx"][b], np.float32).T, np.asarray(x[b], np.float32).T], 1))
         for b in range(B)]
    per_layer_keys = [k for k, v in inputs.items() if k not in ("x", "c", "ctx", "c_ctx", "final_norm")]
    nc_layer = None
    for l in range(L):
        inp_l = dict(inputs)
        for k in per_layer_keys:
            inp_l[k] = np.asarray(inputs[k])[l:l + 1]
        sh, pcs = host_prep(inp_l, T, 1)
        for pc in pcs:
            del pc["xT"], pc["ctxT"]
        del sh["fnT"]
        if nc_layer is None:
            shapes = {k: (v.shape, v.dtype) for k, v in list(sh.items()) + list(pcs[0].items())}
            shapes["h_in"] = ((D, T + CTX), np.float32)
            nc_layer = build(T, 1, shapes, False, mode="layer")
        in_maps = []
        for c in range(8):
            mp = dict(sh)
            mp.update(pcs[c % B])
            mp["h_in"] = h[c % B]
            in_maps.append(mp)
        res = run_bass_kernel_spmd(nc_layer, in_maps, core_ids=list(range(8)))
        h = [np.ascontiguousarray(res.results[b]["hT"]) for b in range(B)]
    shapes = {"fnT": ((128, 8), np.float32), "h_in": ((D, T + CTX), np.float32)}
    nc_fin = build(T, 1, shapes, False, mode="final")
    fn = fm(inputs["final_norm"], 8)
    res = run_bass_kernel_spmd(nc_fin, [{"fnT": fn, "h_in": h[c % B]} for c in range(8)], core_ids=list(range(8)))
    out = np.stack([np.ascontiguousarray(res.results[b]["outT"].T) for b in range(B)], 0)
    return out.astype(np.float32)


def kernel(**inputs):
    out, _ = run(inputs, mode="loop")
    return out
```

```python
import contextlib
import math
import numpy as np
import concourse.bass as bass
import concourse.mybir as mybir
from concourse.bass_utils import run_bass_kernel_spmd

F32 = mybir.dt.float32
BF16 = mybir.dt.bfloat16
AF = mybir.ActivationFunctionType
ALU = mybir.AluOpType

D = 1024
CTX = 256
FF = 2816
NFS = FF // 128
EPS = 1e-6


class Buf:
    __slots__ = ("name", "writers", "readers")

    ALL = []

    def __init__(self, name):
        self.name = name
        self.writers = []
        self.readers = []
        Buf.ALL.append(self)

    @staticmethod
    def reset_all():
        for b in Buf.ALL:
            b.writers = []
            b.readers = []


class Op:
    __slots__ = ("eng", "fn", "deps", "is_dma", "sem", "val", "signal", "idx", "strict")

    def __init__(self, eng, fn, is_dma):
        self.strict = ()
        self.eng = eng
        self.fn = fn
        self.deps = []
        self.is_dma = is_dma
        self.sem = None
        self.val = 0
        self.signal = is_dma
        self.idx = 0


class Sched:
    ENGS = ("pe", "dve", "act", "pool", "sp")

    def __init__(self, nc):
        self.nc = nc
        self.stack = contextlib.ExitStack()
        self.ops = []
        self.dma_sems = {}
        self.last = {}
        self.phase_dep = None

    def sbuf(self, name, shape, dt):
        return self.stack.enter_context(self.nc.sbuf_tensor(name, list(shape), dt))

    def psum(self, name, shape, dt=F32):
        return self.stack.enter_context(self.nc.psum_tensor(name, list(shape), dt))

    def sem(self, name):
        if not hasattr(self, "all_sems"):
            self.all_sems = []
        Sched._uid = getattr(Sched, "_uid", 0) + 1
        h = self.stack.enter_context(self.nc.semaphore("%s_%d" % (name, Sched._uid)))
        self.all_sems.append(h)
        return h

    def op(self, eng, fn, reads=(), writes=(), dma_key=None, extra=(), sreads=()):
        is_dma = dma_key is not None
        o = Op(eng, fn, is_dma)
        o.idx = len(self.ops)
        deps = set(extra)
        if sreads:
            st_ = set()
            for b in sreads:
                st_.update(b.writers)
            o.strict = st_
            deps.update(st_)
        if self.phase_dep is not None:
            deps.add(self.phase_dep)
        for b in reads:
            deps.update(b.writers)
        for b in writes:
            deps.update(b.writers)
            deps.update(b.readers)
        o.deps = sorted(deps)
        if is_dma:
            if dma_key not in self.dma_sems:
                self.dma_sems[dma_key] = [self.sem("d_" + str(dma_key)), 0]
            ent = self.dma_sems[dma_key]
            ent[1] += 16
            o.sem = ent[0]
            o.val = ent[1]
            self.last[("dma", dma_key)] = o.idx
        else:
            self.last[eng] = o.idx
        for b in reads:
            b.readers.append(o.idx)
        for b in writes:
            b.writers = [o.idx]
            b.readers = []
        self.ops.append(o)
        return o

    def barrier(self):
        deps = list(self.last.values())
        self.phase_dep = None
        o = self.op("sp", lambda e: e.nop(), extra=deps)
        self.phase_dep = o.idx
        self.last = {}

    def emit(self):
        nc = self.nc
        ops = self.ops
        esem = {e: self.sem("e_" + e) for e in self.ENGS}
        for o in ops:
            for d in o.deps:
                p = ops[d]
                if p.is_dma or (p.eng == o.eng and o.eng in ("pe", "sp")):
                    continue
                p.signal = True
        cnt = {e: 0 for e in self.ENGS}
        for o in ops:
            if not o.is_dma and o.signal:
                cnt[o.eng] += 1
                o.sem = esem[o.eng]
                o.val = cnt[o.eng]
        dma_issued = {}
        per_eng = {e: [] for e in self.ENGS}
        waited = {e: {} for e in self.ENGS}
        for o in ops:
            waits = {}
            for d in o.deps:
                p = ops[d]
                if p.is_dma:
                    s, v = dma_issued[id(p.sem)]
                else:
                    if p.eng == o.eng and o.eng in ("pe", "sp"):
                        continue
                    s, v = p.sem, p.val
                k = id(s)
                if waited[o.eng].get(k, 0) >= v:
                    continue
                if k not in waits or waits[k][1] < v:
                    waits[k] = (s, v)
            for k, (s, v) in waits.items():
                waited[o.eng][k] = v
            if o.is_dma:
                dma_issued[id(o.sem)] = (o.sem, o.val)
            per_eng[o.eng].append((o, list(waits.values())))

        def run(engname, e):
            for o, waits in per_eng[engname]:
                for s, v in waits:
                    e.wait_ge(s, v)
                if o.fn is None:
                    continue
                ins = o.fn(e)
                if o.signal:
                    ins.then_inc(o.sem, 16 if o.is_dma else 1)

        with nc.Block() as block:
            @block.tensor
            def _(e):
                run("pe", e)

            @block.vector
            def _(e):
                run("dve", e)

            @block.scalar
            def _(e):
                run("act", e)

            @block.gpsimd
            def _(e):
                run("pool", e)

            @block.sync
            def _(e):
                run("sp", e)

    def reset_sems(self):
        nc = self.nc
        for s in self.all_sems:
            nc.gpsimd.sem_clear(s)
        nc.all_engine_barrier()

    def close(self):
        self.stack.close()


class Tl:
    __slots__ = ("t", "b")

    def __init__(self, t, name):
        self.t = t
        self.b = Buf(name)


class Rot:
    def __init__(self, tiles):
        self.tiles = tiles
        self.i = 0

    def next(self):
        t = self.tiles[self.i % len(self.tiles)]
        self.i += 1
        return t


class Arena:
    def __init__(self, S, nbytes):
        self.t = S.sbuf("arena", [128, nbytes // 4], F32)
        self.cap = nbytes
        self.off = 0
        self.n = 0

    def alloc(self, free_shape, dt, name=None):
        esz = 4 if dt == F32 else 2
        n = int(np.prod(free_shape))
        nb = (n * esz + 63) // 64 * 64
        assert self.off + nb <= self.cap, ("arena overflow", name, self.off, nb, self.cap)
        v = self.t[:, self.off // 4:(self.off + nb) // 4]
        if dt != F32:
            v = v.bitcast(dt)
        v = v[:, 0:n]
        if len(free_shape) == 2:
            v = v.rearrange("p (a b) -> p a b", a=free_shape[0])
        elif len(free_shape) == 3:
            v = v.rearrange("p (a b c) -> p a b c", a=free_shape[0], b=free_shape[1])
        elif len(free_shape) == 4:
            v = v.rearrange("p (a b c d) -> p a b c d", a=free_shape[0], b=free_shape[1], c=free_shape[2])
        self.off += nb
        self.n += 1
        return Tl(v, name or ("a%d" % self.n))

    def rot(self, k, free_shape, dt, name):
        return Rot([self.alloc(free_shape, dt, "%s%d" % (name, i)) for i in range(k)])

    def mark(self):
        return self.off

    def release(self, m):
        self.off = m


def _swap_perm(n_axial):
    half = n_axial // 2
    q = half // 2
    perm = np.zeros(n_axial, np.int64)
    for i in range(n_axial):
        base = (i // half) * half
        j = i - base
        perm[i] = base + (j + q if j < q else j - q)
    return perm


def _rope_tables(n_axial, T, grid_w=64):
    half = n_axial // 2
    q = half // 2
    t = np.arange(T)
    row = (t // grid_w).astype(np.float32)
    col = (t % grid_w).astype(np.float32)
    inv = (10000.0 ** (-np.arange(0, half, 2, dtype=np.float32) / np.float32(half))).astype(np.float32)
    cos = np.zeros((n_axial, T), np.float32)
    sin = np.zeros((n_axial, T), np.float32)
    for i in range(n_axial):
        pos = row if i < half else col
        j = (i % half)
        ang = (pos * inv[j % q]).astype(np.float32)
        cos[i] = np.cos(ang)
        sn = np.sin(ang)
        sin[i] = -sn if j < q else sn
    return cos, sin


def fm(v, nch):
    v = np.asarray(v, np.float32)
    lead = v.shape[:-1]
    v = v.reshape(lead + (nch, 128))
    v = np.moveaxis(v, -1, 0)
    return np.ascontiguousarray(v)


def host_prep(inp, T, L):
    f32 = np.float32
    sh = {}
    sh["ada_w"] = np.ascontiguousarray(inp["ada_w"][:L], f32)
    sh["ada_bT"] = fm(inp["ada_b"][:L].reshape(L, 9, D), 8)
    norms = np.stack([inp["norm_ffn1"][:L], inp["norm_mix"][:L], inp["norm_ffn2"][:L]], 1)
    sh["normsT"] = fm(norms, 8)
    sh["fnT"] = fm(inp["final_norm"], 8)
    for k in ("ffn1_w13", "ffn1_w2", "ffn2_w13", "ffn2_w2", "mla_w_o", "ssm_w_glu", "gqa_w_o", "w_out", "mla_w_uq"):
        sh[k] = np.ascontiguousarray(inp[k][:L], f32)
    w_in = np.asarray(inp["w_in"][:L], f32)
    sh["w_in"] = np.ascontiguousarray(w_in)
    pm = _swap_perm(32)
    pg = _swap_perm(64)
    krp = np.zeros((L, D, 96), f32)
    krs = np.zeros((L, D, 96), f32)
    krp[:, :, 64:96] = w_in[:, :, 640:672]
    krs[:, :, 64:96] = w_in[:, :, 640 + pm]
    sh["w_krp"] = krp
    sh["w_krs"] = krs
    gq = w_in[:, :, 1184:1696].reshape(L, D, 8, 64)
    sh["w_gqs"] = np.ascontiguousarray(gq[:, :, :, pg].reshape(L, D, 512))
    gk = w_in[:, :, 1696:1824].reshape(L, D, 2, 64)
    sh["w_gks"] = np.ascontiguousarray(gk[:, :, :, pg].reshape(L, D, 128))
    uq = np.asarray(inp["mla_w_uq"][:L], f32).reshape(L, 384, 8, 96)
    uqs = uq.copy()
    uqs[:, :, :, 64:96] = uq[:, :, :, 64 + pm]
    sh["w_uqs"] = np.ascontiguousarray(uqs.reshape(L, 384, 768))
    ukv = np.asarray(inp["mla_w_ukv"][:L], f32).reshape(L, 256, 8, 128)
    sh["w_uk"] = np.ascontiguousarray(ukv[:, :, :, :64].reshape(L, 256, 512))
    sh["w_uv"] = np.ascontiguousarray(ukv[:, :, :, 64:].reshape(L, 256, 512))
    sh["qnT"] = fm(inp["mla_q_norm"][:L], 3)
    sh["kvnT"] = fm(inp["mla_kv_norm"][:L], 2)
    sh["ssm_dT"] = fm(inp["ssm_d"][:L], 4)

    def st_layout(a):
        a = np.asarray(a, f32)
        rest = a.shape[4:]
        a = a.reshape((L, 2, 16, 2, 64) + rest)
        a = np.moveaxis(a, (3, 4), (0, 1))
        return np.ascontiguousarray(a.reshape((128, L, 2, 16) + rest))
    sh["lamre"] = st_layout(inp["ssm_lambda_re"][:L])
    sh["lamim"] = st_layout(inp["ssm_lambda_im"][:L])
    ldt = np.broadcast_to(np.asarray(inp["ssm_log_dt"][:L], f32)[:, :, :, None], (L, 2, 32, 64))
    sh["logdt"] = st_layout(ldt)
    sh["Bre"] = st_layout(inp["ssm_b_re"][:L])
    sh["Bim"] = st_layout(inp["ssm_b_im"][:L])
    for nm, key in (("Cxre", "ssm_c_re"), ("Cxim", "ssm_c_im")):
        c = np.asarray(inp[key][:L], f32)
        cx = np.zeros((128, L, 2, 16, 128), f32)
        for st in range(16):
            for gi in range(2):
                g = 2 * st + gi
                c0 = (st % 4) * 32 + gi * 16
                cx[gi * 64:(gi + 1) * 64, :, :, st, c0:c0 + 16] = np.transpose(c[:, :, g, :, :], (3, 0, 1, 2))
        sh[nm] = cx
    sk = np.asarray(inp["gqa_sink"][:L], f32)
    sh["sinkb"] = np.ascontiguousarray(np.broadcast_to(sk[None, :, :, None], (1, L, 8, 128)))
    cm, sm = _rope_tables(32, T)
    cg, sg = _rope_tables(64, T)
    sh["cosM"] = cm
    sh["sinM"] = sm
    sh["cosG"] = np.ascontiguousarray(np.concatenate([cg, cg], 0))
    sh["sinG"] = np.ascontiguousarray(np.concatenate([sg, sg], 0))
    kk = np.arange(128)[:, None]
    qq = np.arange(128)[None, :]
    sh["maskL"] = np.ascontiguousarray(np.tile((kk >= qq).astype(f32), (1, 4)))
    sh["maskU"] = np.ascontiguousarray(np.tile((kk <= qq).astype(f32), (1, 4)))
    sh["ident"] = np.eye(128, dtype=f32)
    per_core = []
    for b in range(inp["x"].shape[0]):
        pc = {}
        pc["xT"] = np.ascontiguousarray(np.asarray(inp["x"][b], f32).T)
        pc["ctxT"] = np.ascontiguousarray(np.asarray(inp["ctx"][b], f32).T)
        cc = np.stack([np.asarray(inp["c"][b], f32), np.asarray(inp["c_ctx"], f32)], 0)
        pc["cc"] = np.ascontiguousarray(np.transpose(fm(cc, 8), (0, 2, 1)))
        per_core.append(pc)
    return sh, per_core


def build(T, L, shapes, debug=False, mode="full"):
    import os
    S_ = T + CTX
    NKT = S_ // 128
    nc = bass.Bass("TRN2", target_bir_lowering=False)
    S = Sched(nc)
    dr = {}
    for k, (shp, _) in shapes.items():
        dr[k] = nc.dram_tensor(k, list(shp), F32, kind="ExternalInput").ap()
    LB = L if mode == "full" else 1
    NOTLAYERED = ("xT", "ctxT", "cc", "fnT", "cosM", "sinM", "cosG", "sinG", "maskL", "maskU", "ident")
    LEAD = ("ada_w", "ffn1_w13", "ffn1_w2", "ffn2_w13", "ffn2_w2", "mla_w_o", "ssm_w_glu", "gqa_w_o", "w_out", "mla_w_uq",
            "w_in", "w_krp", "w_krs", "w_gqs", "w_gks", "w_uqs", "w_uk", "w_uv")
    BF16SET = ("ffn1_w13", "ffn1_w2", "ffn2_w13", "ffn2_w2", "w_in", "w_krp", "w_krs", "w_gqs", "w_gks", "mla_w_uq", "w_uqs",
               "w_uk", "w_uv", "mla_w_o", "ssm_w_glu", "gqa_w_o", "w_out", "Cxre", "Cxim")
    ext = dict(dr)
    layered = []
    WENG = "pool"
    if mode == "loop" and os.environ.get("LOOPDBG") != "1":
        WENG = "sp"
    if mode == "loop" and os.environ.get("LOOPDBG") != "1":
        layered = [k for k in shapes if k not in NOTLAYERED]
        for k in layered:
            dr[k] = nc.dram_tensor("wc_" + k, list(shapes[k][0]), BF16 if k in BF16SET else F32, kind="Internal").ap()
        for k in ("maskL", "maskU"):
            dr[k] = nc.dram_tensor("wc_" + k, list(shapes[k][0]), BF16, kind="Internal").ap()

    def flat(ap):
        n = len(ap.shape)
        if n <= 2:
            return ap
        pat = {3: "p a b -> p (a b)", 4: "p a b c -> p (a b c)", 5: "p a b c d -> p (a b c d)"}[n]
        return ap.rearrange(pat)

    def lslice(ap, k, l):
        return flat(ap[l]) if k in LEAD else flat(ap[:, l])
    if mode != "layer":
        outT = nc.dram_tensor("outT", [D, T], F32, kind="ExternalOutput").ap()
    sk = "ExternalOutput" if debug else "Internal"

    def scratch(name, shape, dt):
        return nc.dram_tensor(name, list(shape), dt, kind=sk).ap()
    if mode == "layer":
        hT = nc.dram_tensor("hT", [D, S_], F32, kind="ExternalOutput").ap()
    else:
        hT = scratch("hT", [D, S_], F32)
    XMT = scratch("XMT", [D, S_], BF16)
    QT = scratch("QT", [8, 96, S_], BF16)
    KNT = scratch("KNT", [512, S_], BF16)
    KRT = scratch("KRT", [32, S_], BF16)
    VM = scratch("VM", [S_, 520], BF16)
    UT = scratch("UT", [512, S_], F32)
    GQT = scratch("GQT", [512, S_], BF16)
    GKT = scratch("GKT", [128, S_], BF16)
    GVM = scratch("GVM", [S_, 130], BF16)
    OMT = scratch("OMT", [512, S_], BF16)
    YT = scratch("YT", [512, S_], F32)
    GYT = scratch("GYT", [512, S_], BF16)
    OGT = scratch("OGT", [512, S_], BF16)
    dbuf = {}

    def DB(name, i=0):
        k = (name, i)
        if k not in dbuf:
            dbuf[k] = Buf("%s_%s" % (name, i))
        return dbuf[k]

    def DBall(name):
        return [b for (n, _), b in dbuf.items() if n == name]

    A = Arena(S, 200 * 1024)
    _banks = [Tl(S.psum("ps%d" % i, [128, 512]), "ps%d" % i) for i in range(8)]
    PS = Rot(_banks[0:6])
    PSA = Rot(_banks[6:8])
    dbg_outs = {}

    def dbg_dump(name, tl, dt=F32):
        if not debug:
            return
        shp = list(tl.t.shape)
        d_ = nc.dram_tensor("dbg_" + name, shp, dt, kind="ExternalOutput").ap()
        dma("dbg_" + name, d_, tl.t, [tl.b], [DB("dbg_" + name, 0)])

    def mm(out, lhsT, rhs, start, stop, reads, writes):
        S.op("pe", lambda e: e.matmul(out, lhsT=lhsT, rhs=rhs, start=start, stop=stop), reads, writes)

    def act(out, in_, func, reads, writes, **kw):
        S.op("act", lambda e: e.activation(out=out, in_=in_, func=func, **kw), reads, writes)

    def tt(out, in0, in1, op, reads, writes, eng="dve"):
        S.op(eng, lambda e: e.tensor_tensor(out=out, in0=in0, in1=in1, op=op), reads, writes)

    def ts(out, in0, s1, s2, op0, op1, reads, writes, eng="dve", sreads=()):
        if op1 is None:
            S.op(eng, lambda e: e.tensor_scalar(out=out, in0=in0, scalar1=s1, scalar2=None, op0=op0), reads, writes, sreads=sreads)
        else:
            S.op(eng, lambda e: e.tensor_scalar(out=out, in0=in0, scalar1=s1, scalar2=s2, op0=op0, op1=op1), reads, writes, sreads=sreads)

    def stt(out, in0, scalar, in1, op0, op1, reads, writes, sreads=()):
        S.op("dve", lambda e: e.scalar_tensor_tensor(out=out, in0=in0, scalar=scalar, in1=in1, op0=op0, op1=op1), reads, writes, sreads=sreads)

    def scan(out, d0, d1, init, reads, writes):
        S.op("dve", lambda e: e.tensor_tensor_scan(out=out, data0=d0, data1=d1, initial=init, op0=ALU.mult, op1=ALU.add), reads, writes)

    def cp(out, in_, reads, writes, eng="dve"):
        S.op(eng, lambda e: e.tensor_copy(out=out, in_=in_), reads, writes)

    def recip(out, in_, reads, writes):
        S.op("dve", lambda e: e.reciprocal(out=out, in_=in_), reads, writes)

    def memset(ap, val, writes, eng="dve"):
        S.op(eng, lambda e: e.memset(ap, val), (), writes)

    def dma(key, out, in_, reads, writes, eng="sp", **kw):
        S.op(eng, lambda e: e.dma_start(out=out, in_=in_, **kw), reads, writes, dma_key=key)

    def wload(tl, src, kc_list=None):
        nkc = tl.t.shape[1]
        v = src.rearrange("(kc p) f -> p kc f", p=128)
        for kc in range(nkc):
            if WENG == "pool":
                dma(tl.b.name, tl.t[:, kc, :], v[:, kc, :], (), [tl.b], eng="pool", max_dma_last_dim=4096)
            else:
                dma(tl.b.name, tl.t[:, kc, :], v[:, kc, :], (), [tl.b], eng="sp")

    ones_bf = A.alloc([128], BF16, "ones_bf")
    ones_f = A.alloc([128], F32, "ones_f")
    ident = A.alloc([128], F32, "ident")
    memset(ones_bf.t, 1.0, [ones_bf.b])
    memset(ones_f.t, 1.0, [ones_f.b])
    if "ident" in dr:
        dma("ident", ident.t, dr["ident"], (), [ident.b])
    modT = A.alloc([LB, 9, 8, 2], F32, "modT")
    gsT = A.alloc([LB, 3, 8, 2], F32, "gsT")
    hgT = A.alloc([LB, 3, 8, 2], F32, "hgT")
    normsT = A.alloc([LB, 3, 8], F32, "normsT")
    fnT = A.alloc([8], F32, "fnT")
    qnT = A.alloc([LB, 3], F32, "qnT")
    kvnT = A.alloc([LB, 2], F32, "kvnT")
    sdT = A.alloc([LB, 4], F32, "sdT")
    if "fnT" in dr:
        dma("fnT", fnT.t, dr["fnT"], (), [fnT.b])

    def setup_layers():
        for tl, nm in ((normsT, "normsT"), (qnT, "qnT"), (kvnT, "kvnT"), (sdT, "ssm_dT")):
            dma(tl.b.name, tl.t, dr[nm][:, 0:LB], (), [tl.b])
        m0 = A.mark()
        cc = A.alloc([8, 2], F32, "cc")
        scT = A.alloc([8, 2], F32, "scT")
        adab = A.alloc([LB, 9, 8], F32, "adab")
        dma("cc", cc.t, dr["cc"], (), [cc.b])
        dma("adab", adab.t, dr["ada_bT"][:, 0:LB], (), [adab.b])
        act(scT.t, cc.t, AF.Silu, [cc.b], [scT.b])
        awr = A.rot(2, [8, 1024], F32, "adaw")
        for l in range(LB):
            pst = PSA.next()
            for i in range(9):
                aw = awr.next()
                src = dr["ada_w"][l, :, i * 1024:(i + 1) * 1024].rearrange("(kc p) f -> p kc f", p=128)
                for half in range(2):
                    dma(aw.b.name, aw.t[:, half * 4:(half + 1) * 4, :], src[:, half * 4:(half + 1) * 4, :], (), [aw.b])
                for j in range(8):
                    c0 = (i * 8 + j) * 2
                    for kc in range(8):
                        mm(pst.t[:, c0:c0 + 2], aw.t[:, kc, j * 128:(j + 1) * 128], scT.t[:, kc, :], kc == 0, kc == 7,
                           [aw.b, scT.b], [pst.b])
            pv = pst.t[:, 0:144].rearrange("p (i j k) -> p i j k", i=9, j=8)
            for k in range(2):
                tt(modT.t[:, l, :, :, k], pv[:, :, :, k], adab.t[:, l, :, :], ALU.add, [pst.b, adab.b], [modT.b])
            for n in range(3):
                for k in range(2):
                    stt(gsT.t[:, l, n, :, k], modT.t[:, l, 3 * n + 1, :, k], 1.0, normsT.t[:, l, n, :], ALU.add, ALU.mult,
                        [modT.b, normsT.b], [gsT.b])
                ts(hgT.t[:, l, n, :, :], modT.t[:, l, 3 * n + 2, :, :], 1.0 if n == 1 else 0.5, None, ALU.mult, None,
                   [modT.b], [hgT.b])
        S.barrier()
        A.release(m0)

    NTL = T // 512
    if mode in ("full", "loop"):
        dma("h0", hT[:, 0:CTX], dr["ctxT"], (), [DB("hT", 0)])
        for i in range(NTL):
            dma("h0", hT[:, CTX + i * 512:CTX + (i + 1) * 512], dr["xT"][:, i * 512:(i + 1) * 512], (), [DB("hT", 1 + i)])
    else:
        dma("h0", hT[:, 0:CTX], dr["h_in"][:, 0:CTX], (), [DB("hT", 0)])
        for i in range(NTL):
            dma("h0", hT[:, CTX + i * 512:CTX + (i + 1) * 512], dr["h_in"][:, CTX + i * 512:CTX + (i + 1) * 512], (), [DB("hT", 1 + i)])
    if mode == "loop" and layered:
        for k in layered:
            for l in range(L):
                if k in BF16SET:
                    dma("cpyc", lslice(dr[k], k, l), lslice(ext[k], k, l), (), [DB("wc_" + k, l)], eng="pool", max_dma_last_dim=4096)
                else:
                    dma("cpy", lslice(dr[k], k, l), lslice(ext[k], k, l), (), [DB("wc_" + k, l)])
        for k in ("maskL", "maskU"):
            dma("cpyc", dr[k], ext[k], (), [DB("wc_" + k, 0)], eng="pool", max_dma_last_dim=4096)
    if mode in ("full", "layer"):
        setup_layers()
    else:
        S.barrier()

    tiles512 = [(0, CTX, True, 0)] + [(CTX + i * 512, 512, False, 1 + i) for i in range(NTL)]
    tiles256 = [(0, CTX, True, [0])] + [(CTX + i * 256, 256, False, [1 + i // 2]) for i in range(T // 256)]

    def norm_mod(Ht, N, nchunks, gs_ap, sh_ap, out_t, sq, rstd, tmpr, dim, extra_reads=()):
        act(sq.t[:, 0:nchunks, :N], Ht.t[:, 0:nchunks, :N], AF.Square, [Ht.b], [sq.b])
        pss = PS.next()
        for j in range(nchunks):
            mm(pss.t[:, :N], ones_bf.t, sq.t[:, j, :N], j == 0, j == nchunks - 1, [sq.b, ones_bf.b], [pss.b])
        ts(rstd.t[:, :N], pss.t[:, :N], 1.0 / dim, EPS, ALU.mult, ALU.add, [pss.b], [rstd.b])
        act(rstd.t[:, :N], rstd.t[:, :N], AF.Sqrt, [rstd.b], [rstd.b])
        recip(rstd.t[:, :N], rstd.t[:, :N], [rstd.b], [rstd.b])
        for j in range(nchunks):
            if sh_ap is None:
                stt(out_t.t[:, j, :N], Ht.t[:, j, :N], gs_ap(j), rstd.t[:, :N], ALU.mult, ALU.mult,
                    [Ht.b, rstd.b] + list(extra_reads), [out_t.b])
            else:
                tm = tmpr.next()
                stt(tm.t[:, :N], Ht.t[:, j, :N], gs_ap(j), rstd.t[:, :N], ALU.mult, ALU.mult,
                    [Ht.b, rstd.b] + list(extra_reads), [tm.b])
                act(out_t.t[:, j, :N], tm.t[:, :N], AF.Identity, [tm.b] + list(extra_reads), [out_t.b], bias=sh_ap(j))

    def ffn_phase(l, nidx, w13_dr, w2_dr, skip_ctx, final):
        m = A.mark()
        W13 = A.alloc([8, 2 * FF], BF16, "W13")
        W2 = A.alloc([NFS, D], BF16, "W2")
        wload(W13, w13_dr)
        wload(W2, w2_dr)
        Hr = A.rot(2, [8, 256], F32, "H")
        xnr = A.rot(2, [8, 256], BF16, "xn")
        sq = A.alloc([8, 256], BF16, "sq")
        g = A.alloc([NFS, 256], BF16, "g")
        rstd = A.alloc([256], F32, "rstd")
        tmpr = A.rot(2, [256], F32, "tmp")
        sir = A.rot(2, [256], BF16, "si")
        if final:
            ot = A.alloc([8, 256], F32, "ot")
        tl_ = [t_ for t_ in tiles256 if not (t_[2] and skip_ctx)]

        def stage_a(t_):
            (c0, N, isctx, hb) = t_
            k = 1 if isctx else 0
            H = Hr.next()
            xn = xnr.next()
            dma(H.b.name, H.t[:, :, :N], hT[:, c0:c0 + N].rearrange("(j p) n -> p j n", p=128),
                [DB("hT", i) for i in hb], [H.b])
            norm_mod(H, N, 8, lambda j: gsT.t[:, l, nidx, j, k:k + 1], lambda j: modT.t[:, l, 3 * nidx, j, k:k + 1],
                     xn, sq, rstd, tmpr, D, extra_reads=[gsT.b, modT.b])
            return H, xn
        cur = stage_a(tl_[0])
        for ti_, (c0, N, isctx, hb) in enumerate(tl_):
            k = 1 if isctx else 0
            H, xn = cur
            for s in range(NFS):
                pa = PS.next()
                pb = PS.next()
                for kc in range(8):
                    mm(pa.t[:, :N], W13.t[:, kc, s * 128:(s + 1) * 128], xn.t[:, kc, :N], kc == 0, kc == 7, [W13.b, xn.b], [pa.b])
                for kc in range(8):
                    mm(pb.t[:, :N], W13.t[:, kc, FF + s * 128:FF + (s + 1) * 128], xn.t[:, kc, :N], kc == 0, kc == 7, [W13.b, xn.b], [pb.b])
                si = sir.next()
                act(si.t[:, :N], pa.t[:, :N], AF.Silu, [pa.b], [si.b])
                tt(g.t[:, s, :N], si.t[:, :N], pb.t[:, :N], ALU.mult, [si.b, pb.b], [g.b])
            if ti_ + 1 < len(tl_):
                cur = stage_a(tl_[ti_ + 1])
            for d in range(8):
                po = PS.next()
                for fc in range(NFS):
                    mm(po.t[:, :N], W2.t[:, fc, d * 128:(d + 1) * 128], g.t[:, fc, :N], fc == 0, fc == NFS - 1, [W2.b, g.b], [po.b])
                stt(H.t[:, d, :N], po.t[:, :N], hgT.t[:, l, nidx, d, k:k + 1], H.t[:, d, :N], ALU.mult, ALU.add,
                    [po.b, hgT.b, H.b], [H.b])
            if final:
                if not isctx:
                    norm_mod(H, N, 8, lambda j: fnT.t[:, j:j + 1], None, ot, sq, rstd, tmpr, D, extra_reads=[fnT.b])
                    dma(ot.b.name, outT[:, c0 - CTX:c0 - CTX + N].rearrange("(j p) n -> p j n", p=128), ot.t[:, :, :N],
                        [ot.b], [DB("outT", 0)])
            else:
                dma(H.b.name, hT[:, c0:c0 + N].rearrange("(j p) n -> p j n", p=128), H.t[:, :, :N], [H.b],
                    [DB("hT", i) for i in hb])
        S.barrier()
        A.release(m)

    def proj_phase(l):
        m = A.mark()
        win = dr["w_in"][l]
        Wq1 = A.alloc([8, 384], BF16, "Wq1"); wload(Wq1, win[:, 0:384])
        Wkv1 = A.alloc([8, 256], BF16, "Wkv1"); wload(Wkv1, win[:, 384:640])
        Wkrp = A.alloc([8, 96], BF16, "Wkrp"); wload(Wkrp, dr["w_krp"][l])
        Wkrs = A.alloc([8, 96], BF16, "Wkrs"); wload(Wkrs, dr["w_krs"][l])
        Wu = A.alloc([8, 512], BF16, "Wu"); wload(Wu, win[:, 672:1184])
        Wgq = A.alloc([8, 512], BF16, "Wgq"); wload(Wgq, win[:, 1184:1696])
        Wgqs = A.alloc([8, 512], BF16, "Wgqs"); wload(Wgqs, dr["w_gqs"][l])
        Wgk = A.alloc([8, 128], BF16, "Wgk"); wload(Wgk, win[:, 1696:1824])
        Wgks = A.alloc([8, 128], BF16, "Wgks"); wload(Wgks, dr["w_gks"][l])
        Wgv = A.alloc([8, 128], BF16, "Wgv"); wload(Wgv, win[:, 1824:1952])
        Wuq = A.alloc([3, 768], BF16, "Wuq"); wload(Wuq, dr["mla_w_uq"][l])
        Wuqs = A.alloc([3, 768], BF16, "Wuqs"); wload(Wuqs, dr["w_uqs"][l])
        Wuk = A.alloc([2, 512], BF16, "Wuk"); wload(Wuk, dr["w_uk"][l])
        Wuv = A.alloc([2, 512], BF16, "Wuv"); wload(Wuv, dr["w_uv"][l])
        Hr = A.rot(2, [8, 512], F32, "H")
        xm = A.alloc([8, 512], BF16, "xm")
        sq = A.alloc([8, 512], BF16, "sq")
        rstd = A.alloc([512], F32, "rstd")
        tmpr = A.rot(2, [512], F32, "tmp")
        cq = A.alloc([3, 512], F32, "cq")
        cqn = A.alloc([3, 512], BF16, "cqn")
        ckvn = A.alloc([2, 512], BF16, "ckvn")
        rt = A.rot(2, [4, 512], F32, "ropet")
        qtr = A.rot(3, [512], BF16, "qt")
        t1r = A.rot(2, [512], F32, "t1")
        t2r = A.rot(2, [512], F32, "t2")
        utr = A.rot(2, [512], F32, "ut")
        vst = A.rot(2, [8, 65], BF16, "vst")
        vst2 = A.rot(2, [2, 65], BF16, "vst2")
        for v in vst.tiles + vst2.tiles:
            memset(v.t, 1.0, [v.b])

        def rope_out(dst, p, psw, lo, hi, cosr, sinr, rtb, N, isctx):
            if isctx:
                cp(dst.t[lo:hi, :N], p.t[lo:hi, :N], [p.b], [dst.b])
                return
            t1 = t1r.next()
            t2 = t2r.next()
            tt(t1.t[lo:hi, :N], p.t[lo:hi, :N], cosr[lo:hi, :N], ALU.mult, [p.b, rtb], [t1.b])
            tt(t2.t[lo:hi, :N], psw.t[lo:hi, :N], sinr[lo:hi, :N], ALU.mult, [psw.b, rtb], [t2.b])
            tt(dst.t[lo:hi, :N], t1.t[lo:hi, :N], t2.t[lo:hi, :N], ALU.add, [t1.b, t2.b], [dst.b], eng="pool")

        for (c0, N, isctx, hb) in tiles512:
            k = 1 if isctx else 0
            H = Hr.next()
            dma(H.b.name, H.t[:, :, :N], hT[:, c0:c0 + N].rearrange("(j p) n -> p j n", p=128), [DB("hT", hb)], [H.b])
            R = rt.next()
            if not isctx:
                t0 = c0 - CTX
                dma(R.b.name, R.t[64:96, 0, :N], dr["cosM"][:, t0:t0 + N], (), [R.b])
                dma(R.b.name, R.t[64:96, 1, :N], dr["sinM"][:, t0:t0 + N], (), [R.b])
                dma(R.b.name, R.t[:, 2, :N], dr["cosG"][:, t0:t0 + N], (), [R.b])
                dma(R.b.name, R.t[:, 3, :N], dr["sinG"][:, t0:t0 + N], (), [R.b])
            norm_mod(H, N, 8, lambda j: gsT.t[:, l, 1, j, k:k + 1], lambda j: modT.t[:, l, 3, j, k:k + 1],
                     xm, sq, rstd, tmpr, D, extra_reads=[gsT.b, modT.b])
            dma(xm.b.name, XMT[:, c0:c0 + N].rearrange("(j p) n -> p j n", p=128), xm.t[:, :, :N], [xm.b], [DB("XMT", hb)])
            for j in range(3):
                p = PS.next()
                for kc in range(8):
                    mm(p.t[:, :N], Wq1.t[:, kc, j * 128:(j + 1) * 128], xm.t[:, kc, :N], kc == 0, kc == 7, [Wq1.b, xm.b], [p.b])
                act(cq.t[:, j, :N], p.t[:, :N], AF.Identity, [p.b], [cq.b])
            norm_mod(cq, N, 3, lambda j: qnT.t[:, l, j:j + 1], None, cqn, sq, rstd, tmpr, 384, extra_reads=[qnT.b])
            for h in range(8):
                p = PS.next()
                psw = PS.next()
                for kc in range(3):
                    mm(p.t[0:96, :N], Wuq.t[:, kc, h * 96:(h + 1) * 96], cqn.t[:, kc, :N], kc == 0, kc == 2, [Wuq.b, cqn.b], [p.b])
                if not isctx:
                    for kc in range(3):
                        mm(psw.t[0:96, :N], Wuqs.t[:, kc, h * 96:(h + 1) * 96], cqn.t[:, kc, :N], kc == 0, kc == 2, [Wuqs.b, cqn.b], [psw.b])
                qt = qtr.next()
                act(qt.t[0:64, :N], p.t[0:64, :N], AF.Identity, [p.b], [qt.b])
                rope_out(qt, p, psw, 64, 96, R.t[:, 0, :], R.t[:, 1, :], R.b, N, isctx)
                dma(qt.b.name, QT[h, :, c0:c0 + N], qt.t[0:96, :N], [qt.b], [DB("QT", hb)])
            for j in range(2):
                p = PS.next()
                for kc in range(8):
                    mm(p.t[:, :N], Wkv1.t[:, kc, j * 128:(j + 1) * 128], xm.t[:, kc, :N], kc == 0, kc == 7, [Wkv1.b, xm.b], [p.b])
                act(cq.t[:, j, :N], p.t[:, :N], AF.Identity, [p.b], [cq.b])
            norm_mod(cq, N, 2, lambda j: kvnT.t[:, l, j:j + 1], None, ckvn, sq, rstd, tmpr, 256, extra_reads=[kvnT.b])
            for j in range(4):
                p = PS.next()
                for kc in range(2):
                    mm(p.t[:, :N], Wuk.t[:, kc, j * 128:(j + 1) * 128], ckvn.t[:, kc, :N], kc == 0, kc == 1, [Wuk.b, ckvn.b], [p.b])
                qt = qtr.next()
                act(qt.t[:, :N], p.t[:, :N], AF.Identity, [p.b], [qt.b])
                dma(qt.b.name, KNT[j * 128:(j + 1) * 128, c0:c0 + N], qt.t[:, :N], [qt.b], [DB("KNT", hb)])
            for tb in range(N // 128):
                p = PS.next()
                for kc in range(2):
                    mm(p.t[:, :], ckvn.t[:, kc, tb * 128:(tb + 1) * 128], Wuv.t[:, kc, :], kc == 0, kc == 1, [Wuv.b, ckvn.b], [p.b])
                v = vst.next()
                cp(v.t[:, :, 0:64], p.t[:, :].rearrange("p (h d) -> p h d", h=8), [p.b], [v.b])
                dma(v.b.name, VM[c0 + tb * 128:c0 + (tb + 1) * 128, :], v.t.rearrange("p h d -> p (h d)"), [v.b], [DB("VM", hb)])
            p = PS.next()
            psw = PS.next()
            for kc in range(8):
                mm(p.t[0:96, :N], Wkrp.t[:, kc, :], xm.t[:, kc, :N], kc == 0, kc == 7, [Wkrp.b, xm.b], [p.b])
            if not isctx:
                for kc in range(8):
                    mm(psw.t[0:96, :N], Wkrs.t[:, kc, :], xm.t[:, kc, :N], kc == 0, kc == 7, [Wkrs.b, xm.b], [psw.b])
            qt = qtr.next()
            rope_out(qt, p, psw, 64, 96, R.t[:, 0, :], R.t[:, 1, :], R.b, N, isctx)
            dma(qt.b.name, KRT[:, c0:c0 + N], qt.t[64:96, :N], [qt.b], [DB("KRT", hb)])
            for j in range(4):
                p = PS.next()
                for kc in range(8):
                    mm(p.t[:, :N], Wu.t[:, kc, j * 128:(j + 1) * 128], xm.t[:, kc, :N], kc == 0, kc == 7, [Wu.b, xm.b], [p.b])
                ut = utr.next()
                act(ut.t[:, :N], p.t[:, :N], AF.Identity, [p.b], [ut.b])
                dma(ut.b.name, UT[j * 128:(j + 1) * 128, c0:c0 + N], ut.t[:, :N], [ut.b], [DB("UT", hb)])
            for j in range(5):
                Wa, Ws, cs = (Wgq, Wgqs, slice(j * 128, (j + 1) * 128)) if j < 4 else (Wgk, Wgks, slice(0, 128))
                p = PS.next()
                psw = PS.next()
                for kc in range(8):
                    mm(p.t[:, :N], Wa.t[:, kc, cs], xm.t[:, kc, :N], kc == 0, kc == 7, [Wa.b, xm.b], [p.b])
                if not isctx:
                    for kc in range(8):
                        mm(psw.t[:, :N], Ws.t[:, kc, cs], xm.t[:, kc, :N], kc == 0, kc == 7, [Ws.b, xm.b], [psw.b])
                qt = qtr.next()
                rope_out(qt, p, psw, 0, 128, R.t[:, 2, :], R.t[:, 3, :], R.b, N, isctx)
                if j < 4:
                    dma(qt.b.name, GQT[j * 128:(j + 1) * 128, c0:c0 + N], qt.t[:, :N], [qt.b], [DB("GQT", hb)])
                else:
                    dma(qt.b.name, GKT[:, c0:c0 + N], qt.t[:, :N], [qt.b], [DB("GKT", hb)])
            for tb in range(N // 128):
                p = PS.next()
                for kc in range(8):
                    mm(p.t[:, 0:128], xm.t[:, kc, tb * 128:(tb + 1) * 128], Wgv.t[:, kc, :], kc == 0, kc == 7, [Wgv.b, xm.b], [p.b])
                v = vst2.next()
                cp(v.t[:, :, 0:64], p.t[:, 0:128].rearrange("p (h d) -> p h d", h=2), [p.b], [v.b])
                dma(v.b.name, GVM[c0 + tb * 128:c0 + (tb + 1) * 128, :], v.t.rearrange("p h d -> p (h d)"), [v.b], [DB("GVM", hb)])
        S.barrier()
        A.release(m)

    def finalize_attn(po, N, osb, rec, extra_den, dst_dram_ap, dst_key, otile):
        act(osb.t[0:65, :N], po.t[0:65, :N], AF.Identity, [po.b], [osb.b])
        if extra_den is not None:
            ap_, b_ = extra_den
            tt(osb.t[64:65, :N], osb.t[64:65, :N], ap_, ALU.add, [osb.b, b_], [osb.b])
        recip(rec.t[64:65, :N], osb.t[64:65, :N], [osb.b], [rec.b])
        pb = PS.next()
        mm(pb.t[0:64, :N], ones_f.t[64:65, 0:64], rec.t[64:65, :N], True, True, [ones_f.b, rec.b], [pb.b])
        tt(otile.t[0:64, :N], osb.t[0:64, :N], pb.t[0:64, :N], ALU.mult, [osb.b, pb.b], [otile.b])
        dma(otile.b.name, dst_dram_ap, otile.t[0:64, :N], [otile.b], [dst_key])

    def mla_phase(l, ctx_out):
        m = A.mark()
        Vall = A.alloc([NKT, 520], BF16, "Vall")
        dma("Vall", Vall.t, VM.rearrange("(kt p) c -> p kt c", p=128), DBall("VM"), [Vall.b])
        Kr = A.rot(2, [S_], BF16, "Kh")
        Qr = A.rot(2, [S_], BF16, "Qh")
        pTr = A.rot(4, [512], BF16, "pT")
        osbr = A.rot(2, [512], F32, "osb")
        recr = A.rot(2, [512], F32, "rec")
        otr = A.rot(2, [512], BF16, "ot")
        scale = 96 ** -0.5
        qblocks = ([(0, CTX, 2)] if ctx_out else []) + [(CTX + i * 512, 512, NKT) for i in range(NTL)]
        for h in range(8):
            Kh = Kr.next()
            Qh = Qr.next()
            dma(Kh.b.name, Kh.t[0:64, :], KNT[h * 64:(h + 1) * 64, :], DBall("KNT"), [Kh.b])
            dma(Kh.b.name, Kh.t[64:96, :], KRT[:, :], DBall("KRT"), [Kh.b])
            dma(Qh.b.name, Qh.t[0:96, :], QT[h, :, :], DBall("QT"), [Qh.b])
            for (c0, N, nk) in qblocks:
                po = PSA.next()
                for kt in range(nk):
                    ps = PS.next()
                    mm(ps.t[:, :N], Kh.t[0:96, kt * 128:(kt + 1) * 128], Qh.t[0:96, c0:c0 + N], True, True, [Kh.b, Qh.b], [ps.b])
                    pT = pTr.next()
                    act(pT.t[:, :N], ps.t[:, :N], AF.Exp, [ps.b], [pT.b], scale=scale)
                    mm(po.t[0:65, :N], Vall.t[:, kt, h * 65:(h + 1) * 65], pT.t[:, :N], kt == 0, kt == nk - 1, [Vall.b, pT.b], [po.b])
                finalize_attn(po, N, osbr.next(), recr.next(), None, OMT[h * 64:(h + 1) * 64, c0:c0 + N],
                              DB("OMT", (h, c0)), otr.next())
        S.barrier()
        A.release(m)

    def gqa_phase(l, ctx_out):
        m = A.mark()
        V2 = A.alloc([NKT, 130], BF16, "V2")
        dma("V2", V2.t, GVM.rearrange("(kt p) c -> p kt c", p=128), DBall("GVM"), [V2.b])
        mL = A.alloc([512], BF16, "mL")
        mU = A.alloc([512], BF16, "mU")
        dma("mL", mL.t, dr["maskL"], (), [mL.b], eng=WENG)
        dma("mU", mU.t, dr["maskU"], (), [mU.b], eng=WENG)
        skx = A.alloc([8, 128], F32, "skx")
        dma("skx", skx.t[64:65, :, :], dr["sinkb"][:, l, :, :], (), [skx.b])
        act(skx.t[64:65, :, :], skx.t[64:65, :, :], AF.Exp, [skx.b], [skx.b])
        K2r = A.rot(1, [S_], BF16, "K2")
        Q2r = A.rot(1, [4, S_], BF16, "Q2")
        pTr = A.rot(4, [512], BF16, "pT")
        osbr = A.rot(2, [512], F32, "osb")
        recr = A.rot(2, [512], F32, "rec")
        otr = A.rot(2, [4, 128], BF16, "ot")
        nb = T // 128
        for kvh in range(2):
            K2 = K2r.next()
            Q2 = Q2r.next()
            dma(K2.b.name, K2.t[0:64, :], GKT[kvh * 64:(kvh + 1) * 64, :], DBall("GKT"), [K2.b])
            dma(Q2.b.name, Q2.t[0:64, :, :], GQT[kvh * 256:(kvh + 1) * 256, :].rearrange("(g d) s -> d g s", g=4),
                DBall("GQT"), [Q2.b])
            blocks = []
            if ctx_out:
                for cb in range(2):
                    blocks.append((cb * 128, [(0, None), (1, None)]))
            for n in range(nb):
                kts = [(0, None), (1, None)]
                if n > 0:
                    kts.append((2 + n - 1, mL))
                kts.append((2 + n, None))
                if n < nb - 1:
                    kts.append((2 + n + 1, mU))
                blocks.append((CTX + n * 128, kts))
            for (c0, kts) in blocks:
                po = PSA.next()
                for i, (kt, msk) in enumerate(kts):
                    ps = PS.next()
                    mm(ps.t[:, :], K2.t[0:64, kt * 128:(kt + 1) * 128], Q2.t[0:64, :, c0:c0 + 128], True, True, [K2.b, Q2.b], [ps.b])
                    pT = pTr.next()
                    act(pT.t[:, :], ps.t[:, :], AF.Exp, [ps.b], [pT.b], scale=0.125)
                    if msk is not None:
                        tt(pT.t[:, :], pT.t[:, :], msk.t, ALU.mult, [pT.b, msk.b], [pT.b])
                    mm(po.t[0:65, :], V2.t[:, kt, kvh * 65:(kvh + 1) * 65], pT.t[:, :], i == 0, i == len(kts) - 1, [V2.b, pT.b], [po.b])
                ot = otr.next()
                osb = osbr.next()
                rec = recr.next()
                act(osb.t[0:65, :], po.t[0:65, :], AF.Identity, [po.b], [osb.b])
                tt(osb.t[64:65, :], osb.t[64:65, :], skx.t[64:65, kvh * 4:(kvh + 1) * 4, :].rearrange("p g q -> p (g q)"),
                   ALU.add, [osb.b, skx.b], [osb.b])
                recip(rec.t[64:65, :], osb.t[64:65, :], [osb.b], [rec.b])
                pb = PS.next()
                mm(pb.t[0:64, :], ones_f.t[64:65, 0:64], rec.t[64:65, :], True, True, [ones_f.b, rec.b], [pb.b])
                tt(ot.t[0:64, :, :].rearrange("p g q -> p (g q)"), osb.t[0:64, :], pb.t[0:64, :], ALU.mult, [osb.b, pb.b], [ot.b])
                dma(ot.b.name, OGT[kvh * 256:(kvh + 1) * 256, c0:c0 + 128].rearrange("(g d) q -> d g q", g=4), ot.t[0:64, :, :],
                    [ot.b], [DB("OGT", (kvh, c0))])
        S.barrier()
        A.release(m)

    def ssm_phase(l, ctx_out):
        m = A.mark()
        LC = 512
        par = {}
        for nm in ("lamre", "lamim", "logdt"):
            par[nm] = A.alloc([2, 16], F32, nm)
            dma(nm, par[nm].t, dr[nm][:, l, :, :], (), [par[nm].b])
        Bre = A.alloc([2, 16, 16], F32, "Bre")
        Bim = A.alloc([2, 16, 16], F32, "Bim")
        dma("Bre", Bre.t, dr["Bre"][:, l], (), [Bre.b])
        dma("Bim", Bim.t, dr["Bim"][:, l], (), [Bim.b])
        Cre = A.alloc([2, 16, 128], BF16, "Cre")
        Cim = A.alloc([2, 16, 128], BF16, "Cim")
        for d in range(2):
            if WENG == "pool":
                dma("Cre", Cre.t[:, d], dr["Cxre"][:, l, d], (), [Cre.b], eng="pool", max_dma_last_dim=4096)
                dma("Cim", Cim.t[:, d], dr["Cxim"][:, l, d], (), [Cim.b], eng="pool", max_dma_last_dim=4096)
            else:
                dma("Cre", Cre.t[:, d], dr["Cxre"][:, l, d], (), [Cre.b])
                dma("Cim", Cim.t[:, d], dr["Cxim"][:, l, d], (), [Cim.b])
        sm = {}
        for nm in ("dt", "lrdt", "lidt", "mag", "cs", "sn", "are", "aim", "den", "wre", "wim", "x1", "x2", "x3"):
            sm[nm] = A.alloc([2, 16], F32, "s_" + nm)
        halfpi = A.alloc([1], F32, "halfpi")
        memset(halfpi.t, math.pi / 2, [halfpi.b])
        P = par
        act(sm["dt"].t, P["logdt"].t, AF.Exp, [P["logdt"].b], [sm["dt"].b])
        tt(sm["lrdt"].t, P["lamre"].t, sm["dt"].t, ALU.mult, [P["lamre"].b, sm["dt"].b], [sm["lrdt"].b])
        tt(sm["lidt"].t, P["lamim"].t, sm["dt"].t, ALU.mult, [P["lamim"].b, sm["dt"].b], [sm["lidt"].b])
        act(sm["mag"].t, sm["lrdt"].t, AF.Exp, [sm["lrdt"].b], [sm["mag"].b])
        act(sm["sn"].t, sm["lidt"].t, AF.Sin, [sm["lidt"].b], [sm["sn"].b], scale=1.0 / 16)
        act(sm["cs"].t, sm["lidt"].t, AF.Sin, [sm["lidt"].b, halfpi.b], [sm["cs"].b], scale=1.0 / 16, bias=halfpi.t[:, 0:1])
        for _ in range(4):
            tt(sm["x1"].t, sm["cs"].t, sm["cs"].t, ALU.mult, [sm["cs"].b], [sm["x1"].b])
            tt(sm["x2"].t, sm["sn"].t, sm["sn"].t, ALU.mult, [sm["sn"].b], [sm["x2"].b])
            tt(sm["x3"].t, sm["cs"].t, sm["sn"].t, ALU.mult, [sm["cs"].b, sm["sn"].b], [sm["x3"].b])
            tt(sm["cs"].t, sm["x1"].t, sm["x2"].t, ALU.subtract, [sm["x1"].b, sm["x2"].b], [sm["cs"].b])
            ts(sm["sn"].t, sm["x3"].t, 2.0, None, ALU.mult, None, [sm["x3"].b], [sm["sn"].b])
        tt(sm["are"].t, sm["mag"].t, sm["cs"].t, ALU.mult, [sm["mag"].b, sm["cs"].b], [sm["are"].b])
        tt(sm["aim"].t, sm["mag"].t, sm["sn"].t, ALU.mult, [sm["mag"].b, sm["sn"].b], [sm["aim"].b])
        tt(sm["x1"].t, P["lamre"].t, P["lamre"].t, ALU.mult, [P["lamre"].b], [sm["x1"].b])
        tt(sm["x2"].t, P["lamim"].t, P["lamim"].t, ALU.mult, [P["lamim"].b], [sm["x2"].b])
        tt(sm["den"].t, sm["x1"].t, sm["x2"].t, ALU.add, [sm["x1"].b, sm["x2"].b], [sm["den"].b])
        recip(sm["den"].t, sm["den"].t, [sm["den"].b], [sm["den"].b])
        ts(sm["x3"].t, sm["are"].t, -1.0, None, ALU.add, None, [sm["are"].b], [sm["x3"].b])
        tt(sm["x1"].t, sm["x3"].t, P["lamre"].t, ALU.mult, [sm["x3"].b, P["lamre"].b], [sm["x1"].b])
        tt(sm["x2"].t, sm["aim"].t, P["lamim"].t, ALU.mult, [sm["aim"].b, P["lamim"].b], [sm["x2"].b])
        tt(sm["wre"].t, sm["x1"].t, sm["x2"].t, ALU.add, [sm["x1"].b, sm["x2"].b], [sm["wre"].b])
        tt(sm["wre"].t, sm["wre"].t, sm["den"].t, ALU.mult, [sm["wre"].b, sm["den"].b], [sm["wre"].b])
        tt(sm["x1"].t, sm["aim"].t, P["lamre"].t, ALU.mult, [sm["aim"].b, P["lamre"].b], [sm["x1"].b])
        tt(sm["x2"].t, sm["x3"].t, P["lamim"].t, ALU.mult, [sm["x3"].b, P["lamim"].b], [sm["x2"].b])
        tt(sm["wim"].t, sm["x1"].t, sm["x2"].t, ALU.subtract, [sm["x1"].b, sm["x2"].b], [sm["wim"].b])
        tt(sm["wim"].t, sm["wim"].t, sm["den"].t, ALU.mult, [sm["wim"].b, sm["den"].b], [sm["wim"].b])
        bbx = {}
        for ri in ("re", "im"):
            bbx[ri] = A.alloc([2, 16, 2, 16], F32, "bbx" + ri)
            memset(bbx[ri].t, 0.0, [bbx[ri].b])
        tmpb = A.alloc([16], F32, "tmpb")
        for d in range(2):
            for st in range(16):
                wre = sm["wre"].t[:, d, st:st + 1]
                wim = sm["wim"].t[:, d, st:st + 1]
                for gi in range(2):
                    lo, hi = gi * 64, gi * 64 + 64
                    rd = [sm["wre"].b, sm["wim"].b, Bre.b, Bim.b, tmpb.b]
                    sr_ = [sm["wre"].b, sm["wim"].b]
                    ts(tmpb.t[lo:hi, :], Bim.t[lo:hi, d, st, :], wim[lo:hi], None, ALU.mult, None, rd, [tmpb.b], sreads=sr_)
                    stt(bbx["re"].t[lo:hi, d, st, gi, :], Bre.t[lo:hi, d, st, :], wre[lo:hi], tmpb.t[lo:hi, :], ALU.mult, ALU.subtract,
                        rd, [bbx["re"].b], sreads=sr_)
                    ts(tmpb.t[lo:hi, :], Bre.t[lo:hi, d, st, :], wim[lo:hi], None, ALU.mult, None, rd, [tmpb.b], sreads=sr_)
                    stt(bbx["im"].t[lo:hi, d, st, gi, :], Bim.t[lo:hi, d, st, :], wre[lo:hi], tmpb.t[lo:hi, :], ALU.mult, ALU.add,
                        rd, [bbx["im"].b], sreads=sr_)
        BT = {}
        for ri in ("re", "im"):
            BT[ri] = A.alloc([2, 4, 128], BF16, "BT" + ri)
            for d in range(2):
                for q in range(4):
                    p = PS.next()
                    src = bbx[ri].t[:, d, q * 4:(q + 1) * 4, :, :].rearrange("p a b c -> p (a b c)")
                    S.op("pe", lambda e, p=p, src=src: e.transpose(out=p.t[:, 0:128], in_=src, identity=ident.t),
                         [bbx[ri].b, ident.b], [p.b])
                    cp(BT[ri].t[:, d, q, :], p.t[:, 0:128], [p.b], [BT[ri].b])
        BT3 = {}
        bz = A.alloc([4, 2, 16], F32, "bz")
        memset(bz.t, 0.0, [bz.b])
        for ri in ("re", "im"):
            BT3[ri] = A.alloc([2, 4, 128], BF16, "BT3" + ri)
            for d in range(2):
                for q in range(4):
                    cp(bz.t[:, 3, :, :], bbx[ri].t[:, d, q * 4 + 3, :, :], [bbx[ri].b], [bz.b])
                    p = PS.next()
                    src = bz.t.rearrange("p a b c -> p (a b c)")
                    S.op("pe", lambda e, p=p, src=src: e.transpose(out=p.t[:, 0:128], in_=src, identity=ident.t),
                         [bz.b, ident.b], [p.b])
                    cp(BT3[ri].t[:, d, q, :], p.t[:, 0:128], [p.b], [BT3[ri].b])
        ECOS = A.alloc([16, LC], F32, "ECOS")
        ESIN = A.alloc([16, LC], F32, "ESIN")
        carry = A.alloc([16, 2], F32, "carry")
        uf = A.rot(2, [4, LC], F32, "uf")
        ub = A.rot(2, [4, LC], BF16, "ub")
        zr = A.rot(2, [LC], F32, "zr")
        zi = A.rot(2, [LC], F32, "zi")
        t1r = A.rot(3, [LC], F32, "st1")
        t2r = A.rot(3, [LC], F32, "st2")
        srr = A.rot(2, [LC], F32, "sr")
        sir = A.rot(2, [LC], F32, "si")
        srb = A.rot(8, [LC], BF16, "srb")
        sib = A.rot(8, [LC], BF16, "sib")
        yt = A.rot(2, [LC], F32, "yt")
        yo = A.rot(2, [LC], F32, "yo")
        g1 = A.rot(2, [LC], F32, "g1")
        gyb = A.rot(2, [LC], BF16, "gyb")
        chunks_f = [(0, CTX, 0)] + [(CTX + i * LC, LC, 1 + i) for i in range(T // LC)]
        for d in range(2):
            rd0 = [sm["cs"].b, sm["sn"].b]
            cp(ECOS.t[:, :, 0], sm["cs"].t[:, d, :], rd0, [ECOS.b])
            cp(ESIN.t[:, :, 0], sm["sn"].t[:, d, :], rd0, [ESIN.b])
            w = 1
            while w < LC:
                for st in range(16):
                    c_ = ECOS.t[:, st, w - 1:w]
                    s_ = ESIN.t[:, st, w - 1:w]
                    a1 = t1r.next()
                    a2 = t2r.next()
                    ts(a1.t[:, :w], ESIN.t[:, st, 0:w], s_, None, ALU.mult, None, [ESIN.b], [a1.b], sreads=[ESIN.b, ECOS.b])
                    ts(a2.t[:, :w], ECOS.t[:, st, 0:w], s_, None, ALU.mult, None, [ECOS.b, ESIN.b], [a2.b], sreads=[ESIN.b, ECOS.b])
                    stt(ECOS.t[:, st, w:2 * w], ECOS.t[:, st, 0:w], c_, a1.t[:, :w], ALU.mult, ALU.subtract, [ECOS.b, a1.b], [ECOS.b], sreads=[ESIN.b, ECOS.b])
                    stt(ESIN.t[:, st, w:2 * w], ESIN.t[:, st, 0:w], c_, a2.t[:, :w], ALU.mult, ALU.add, [ESIN.b, ECOS.b, a2.b], [ESIN.b], sreads=[ESIN.b, ECOS.b])
                w *= 2
            if d == 0:
                order = [(c, False) for c in chunks_f]
            else:
                order = [(chunks_f[0], True)] + [(c, True) for c in reversed(chunks_f[1:])]
            memset(carry.t, 0.0, [carry.b])
            for ((c0, N, hb), rev) in order:
                isctx = c0 == 0
                U = uf.next()
                Ub = ub.next()
                dma(U.b.name, U.t[:, :, :N], UT[:, c0:c0 + N].rearrange("(q p) n -> p q n", p=128), DBall("UT"), [U.b])
                cp(Ub.t[:, :, :N], U.t[:, :, :N], [U.b], [Ub.b], eng="pool")

                def rv(ap):
                    return ap[:, N - 1::-1] if rev else ap[:, 0:N]
                for q in range(4):
                    sbs = []
                    for sti in range(4):
                        st = q * 4 + sti
                        lo = sti * 32
                        pr = PS.next()
                        pi = PS.next()
                        if sti < 3:
                            urhs = rv(Ub.t[lo:lo + 32, q, :N]) if rev else Ub.t[lo:lo + 32, q, :N]
                            lre = BT["re"].t[lo:lo + 32, d, q, :]
                            lim = BT["im"].t[lo:lo + 32, d, q, :]
                        else:
                            urhs = rv(Ub.t[64:128, q, :N]) if rev else Ub.t[64:128, q, :N]
                            lre = BT3["re"].t[64:128, d, q, :]
                            lim = BT3["im"].t[64:128, d, q, :]
                        mm(pr.t[:, :N], lre, urhs, True, True, [BT["re"].b, BT3["re"].b, Ub.b], [pr.b])
                        mm(pi.t[:, :N], lim, urhs, True, True, [BT["im"].b, BT3["im"].b, Ub.b], [pi.b])
                        ec = ECOS.t[:, st, :N]
                        es = ESIN.t[:, st, :N]
                        a1 = t1r.next(); a2 = t2r.next(); ZR = zr.next(); ZI = zi.next()
                        tt(a1.t[:, :N], pr.t[:, :N], ec, ALU.mult, [pr.b, ECOS.b], [a1.b])
                        tt(a2.t[:, :N], pi.t[:, :N], es, ALU.mult, [pi.b, ESIN.b], [a2.b])
                        tt(ZR.t[:, :N], a1.t[:, :N], a2.t[:, :N], ALU.add, [a1.b, a2.b], [ZR.b], eng="pool")
                        a1 = t1r.next(); a2 = t2r.next()
                        tt(a1.t[:, :N], pi.t[:, :N], ec, ALU.mult, [pi.b, ECOS.b], [a1.b])
                        tt(a2.t[:, :N], pr.t[:, :N], es, ALU.mult, [pr.b, ESIN.b], [a2.b])
                        tt(ZI.t[:, :N], a1.t[:, :N], a2.t[:, :N], ALU.subtract, [a1.b, a2.b], [ZI.b], eng="pool")
                        rbc = sm["mag"].t[:, d, st:st + 1].to_broadcast([128, N])
                        scan(ZR.t[:, :N], rbc, ZR.t[:, :N], carry.t[:, st, 0:1], [ZR.b, carry.b, sm["mag"].b], [ZR.b])
                        scan(ZI.t[:, :N], rbc, ZI.t[:, :N], carry.t[:, st, 1:2], [ZI.b, carry.b, sm["mag"].b], [ZI.b])
                        SR = srr.next(); SI = sir.next()
                        a1 = t1r.next(); a2 = t2r.next()
                        tt(a1.t[:, :N], ZR.t[:, :N], ec, ALU.mult, [ZR.b, ECOS.b], [a1.b])
                        tt(a2.t[:, :N], ZI.t[:, :N], es, ALU.mult, [ZI.b, ESIN.b], [a2.b], eng="pool")
                        tt(SR.t[:, :N], a1.t[:, :N], a2.t[:, :N], ALU.subtract, [a1.b, a2.b], [SR.b], eng="pool")
                        a1 = t1r.next(); a2 = t2r.next()
                        tt(a1.t[:, :N], ZR.t[:, :N], es, ALU.mult, [ZR.b, ESIN.b], [a1.b])
                        tt(a2.t[:, :N], ZI.t[:, :N], ec, ALU.mult, [ZI.b, ECOS.b], [a2.b], eng="pool")
                        tt(SI.t[:, :N], a1.t[:, :N], a2.t[:, :N], ALU.add, [a1.b, a2.b], [SI.b], eng="pool")
                        act(carry.t[:, st, 0:1], SR.t[:, N - 1:N], AF.Identity, [SR.b], [carry.b])
                        act(carry.t[:, st, 1:2], SI.t[:, N - 1:N], AF.Identity, [SI.b], [carry.b])
                        SRb = srb.next(); SIb = sib.next()
                        act(SRb.t[:, :N], SR.t[:, :N], AF.Identity, [SR.b], [SRb.b])
                        act(SIb.t[:, :N], SI.t[:, :N], AF.Identity, [SI.b], [SIb.b], scale=-1.0)
                        sbs.append((st, SRb, SIb))
                    if isctx and not ctx_out:
                        continue
                    py = PSA.next()
                    for i, (st, SRb, SIb) in enumerate(sbs):
                        mm(py.t[:, :N], Cre.t[:, d, st, :], rv(SRb.t[:, :N]) if rev else SRb.t[:, :N], i == 0, False, [Cre.b, SRb.b], [py.b])
                        mm(py.t[:, :N], Cim.t[:, d, st, :], rv(SIb.t[:, :N]) if rev else SIb.t[:, :N], False, i == 3, [Cim.b, SIb.b], [py.b])
                    Y = yt.next()
                    if d == 0:
                        stt(Y.t[:, :N], U.t[:, q, :N], sdT.t[:, l, q:q + 1], py.t[:, :N], ALU.mult, ALU.add, [U.b, sdT.b, py.b], [Y.b])
                        dma(Y.b.name, YT[q * 128:(q + 1) * 128, c0:c0 + N], Y.t[:, :N], [Y.b], [DB("YT", (q, c0))])
                    else:
                        Yo = yo.next()
                        dma(Yo.b.name, Yo.t[:, :N], YT[q * 128:(q + 1) * 128, c0:c0 + N], [DB("YT", (q, c0))], [Yo.b])
                        tt(Y.t[:, :N], Yo.t[:, :N], py.t[:, :N], ALU.add, [Yo.b, py.b], [Y.b])
                        G = g1.next()
                        tt(G.t[:, :N], Y.t[:, :N], Y.t[:, :N], ALU.mult, [Y.b], [G.b], eng="pool")
                        ts(G.t[:, :N], G.t[:, :N], 0.044715, 1.0, ALU.mult, ALU.add, [G.b], [G.b], eng="pool")
                        tt(G.t[:, :N], G.t[:, :N], Y.t[:, :N], ALU.mult, [G.b, Y.b], [G.b], eng="pool")
                        act(G.t[:, :N], G.t[:, :N], AF.Sigmoid, [G.b], [G.b], scale=1.5957691216057308)
                        Gb = gyb.next()
                        tt(Gb.t[:, :N], G.t[:, :N], Y.t[:, :N], ALU.mult, [G.b, Y.b], [Gb.b], eng="pool")
                        dma(Gb.b.name, GYT[q * 128:(q + 1) * 128, c0:c0 + N], Gb.t[:, :N], [Gb.b], [DB("GYT", (q, c0))])
        for nm in ("are", "aim", "wre", "wim", "mag", "cs", "sn"):
            dbg_dump(nm, sm[nm])
        dbg_dump("ECOS", ECOS)
        dbg_dump("ESIN", ESIN)
        dbg_dump("carry", carry)
        dbg_dump("BTre", BT["re"], BF16)
        dbg_dump("bbxre", bbx["re"])
        dbg_dump("Bre", Bre)
        S.barrier()
        A.release(m)

    def merge_phase(l, ctx_out):
        m = A.mark()
        Wg = A.alloc([8, 3072], BF16, "Wg"); wload(Wg, dr["w_in"][l][:, 1952:5024])
        Wmo = A.alloc([4, D], BF16, "Wmo"); wload(Wmo, dr["mla_w_o"][l])
        Wgl = A.alloc([4, 2 * D], BF16, "Wgl"); wload(Wgl, dr["ssm_w_glu"][l])
        Wgo = A.alloc([4, D], BF16, "Wgo"); wload(Wgo, dr["gqa_w_o"][l])
        Wo = A.alloc([8, D], BF16, "Wo"); wload(Wo, dr["w_out"][l])
        Hr = A.rot(2, [8, 512], F32, "H")
        xmr = A.rot(2, [8, 512], BF16, "xm")
        inr = A.rot(2, [3, 4, 512], BF16, "ins")
        mg = A.alloc([8, 512], BF16, "mg")
        sgr = A.rot(4, [512], BF16, "sg")
        b1r = A.rot(2, [512], F32, "b1")
        accr = A.rot(2, [512], F32, "acc")
        t3r = A.rot(2, [512], F32, "t3")
        for (c0, N, isctx, hb) in tiles512:
            if isctx and not ctx_out:
                continue
            k = 1 if isctx else 0
            H = Hr.next(); xm = xmr.next(); I = inr.next()
            dma(H.b.name, H.t[:, :, :N], hT[:, c0:c0 + N].rearrange("(j p) n -> p j n", p=128), [DB("hT", hb)], [H.b])
            dma(xm.b.name, xm.t[:, :, :N], XMT[:, c0:c0 + N].rearrange("(j p) n -> p j n", p=128), DBall("XMT"), [xm.b])
            for i, (src, nm) in enumerate(((OMT, "OMT"), (GYT, "GYT"), (OGT, "OGT"))):
                dma(I.b.name, I.t[:, i, :, :N], src[:, c0:c0 + N].rearrange("(j p) n -> p j n", p=128), DBall(nm), [I.b])
            for dd in range(8):
                cs = slice(dd * 128, (dd + 1) * 128)
                sg = []
                for gi in range(3):
                    p = PS.next()
                    for kc in range(8):
                        mm(p.t[:, :N], Wg.t[:, kc, gi * D + dd * 128:gi * D + (dd + 1) * 128], xm.t[:, kc, :N], kc == 0, kc == 7, [Wg.b, xm.b], [p.b])
                    s_ = sgr.next()
                    act(s_.t[:, :N], p.t[:, :N], AF.Sigmoid, [p.b], [s_.b])
                    sg.append(s_)
                p0 = PS.next()
                for kc in range(4):
                    mm(p0.t[:, :N], Wmo.t[:, kc, cs], I.t[:, 0, kc, :N], kc == 0, kc == 3, [Wmo.b, I.b], [p0.b])
                acc = accr.next()
                tt(acc.t[:, :N], sg[0].t[:, :N], p0.t[:, :N], ALU.mult, [sg[0].b, p0.b], [acc.b])
                pa = PS.next(); pg = PS.next()
                for kc in range(4):
                    mm(pa.t[:, :N], Wgl.t[:, kc, cs], I.t[:, 1, kc, :N], kc == 0, kc == 3, [Wgl.b, I.b], [pa.b])
                for kc in range(4):
                    mm(pg.t[:, :N], Wgl.t[:, kc, D + dd * 128:D + (dd + 1) * 128], I.t[:, 1, kc, :N], kc == 0, kc == 3, [Wgl.b, I.b], [pg.b])
                s3 = sgr.next()
                act(s3.t[:, :N], pg.t[:, :N], AF.Sigmoid, [pg.b], [s3.b])
                b1 = b1r.next()
                tt(b1.t[:, :N], s3.t[:, :N], pa.t[:, :N], ALU.mult, [s3.b, pa.b], [b1.b])
                t3 = t3r.next()
                tt(t3.t[:, :N], b1.t[:, :N], sg[1].t[:, :N], ALU.mult, [b1.b, sg[1].b], [t3.b], eng="pool")
                p2 = PS.next()
                for kc in range(4):
                    mm(p2.t[:, :N], Wgo.t[:, kc, cs], I.t[:, 2, kc, :N], kc == 0, kc == 3, [Wgo.b, I.b], [p2.b])
                b2 = b1r.next()
                tt(b2.t[:, :N], sg[2].t[:, :N], p2.t[:, :N], ALU.mult, [sg[2].b, p2.b], [b2.b])
                tt(acc.t[:, :N], acc.t[:, :N], t3.t[:, :N], ALU.add, [acc.b, t3.b], [acc.b], eng="pool")
                tt(mg.t[:, dd, :N], acc.t[:, :N], b2.t[:, :N], ALU.add, [acc.b, b2.b], [mg.b], eng="pool")
            for dd in range(8):
                po = PS.next()
                for kc in range(8):
                    mm(po.t[:, :N], Wo.t[:, kc, dd * 128:(dd + 1) * 128], mg.t[:, kc, :N], kc == 0, kc == 7, [Wo.b, mg.b], [po.b])
                stt(H.t[:, dd, :N], po.t[:, :N], hgT.t[:, l, 1, dd, k:k + 1], H.t[:, dd, :N], ALU.mult, ALU.add, [po.b, hgT.b, H.b], [H.b])
            dma(H.b.name, hT[:, c0:c0 + N].rearrange("(j p) n -> p j n", p=128), H.t[:, :, :N], [H.b], [DB("hT", hb)])
        S.barrier()
        A.release(m)

    def final_phase():
        m = A.mark()
        Hr = A.rot(2, [8, 512], F32, "H")
        ot = A.alloc([8, 512], F32, "ot")
        sq = A.alloc([8, 512], BF16, "sq")
        rstd = A.alloc([512], F32, "rstd")
        for (c0, N, isctx, hb) in tiles512:
            if isctx:
                continue
            H = Hr.next()
            dma(H.b.name, H.t[:, :, :N], hT[:, c0:c0 + N].rearrange("(j p) n -> p j n", p=128), [DB("hT", hb)], [H.b])
            norm_mod(H, N, 8, lambda j: fnT.t[:, j:j + 1], None, ot, sq, rstd, None, D, extra_reads=[fnT.b])
            dma(ot.b.name, outT[:, c0 - CTX:c0 - CTX + N].rearrange("(j p) n -> p j n", p=128), ot.t[:, :, :N],
                [ot.b], [DB("outT", 0)])
        S.op("sp", None, reads=DBall("outT"))
        A.release(m)

    PH = os.environ.get("PHASES", "fpmsgeF")

    def layer_body():
        if "f" in PH:
            ffn_phase(0, 0, dr["ffn1_w13"][0], dr["ffn1_w2"][0], False, False)
        if "p" in PH:
            proj_phase(0)
        if "m" in PH:
            mla_phase(0, True)
        if "s" in PH:
            ssm_phase(0, True)
        if "g" in PH:
            gqa_phase(0, True)
        if "e" in PH:
            merge_phase(0, True)
        if "F" in PH:
            ffn_phase(0, 2, dr["ffn2_w13"][0], dr["ffn2_w2"][0], False, False)

    scheds = [S]
    if mode == "final":
        final_phase()
        S.emit()
    elif mode == "layer":
        layer_body()
        S.op("sp", None, reads=DBall("hT"))
        S.emit()
    elif mode == "loop":
        S.emit()
        Buf.reset_all()
        S = Sched(nc)
        scheds.append(S)
        setup_layers()
        layer_body()
        for k in layered:
            for l in range(L - 1):
                dma("shift", lslice(dr[k], k, l), lslice(dr[k], k, l + 1), [DB("wc_" + k, 0)], [DB("wc_" + k, 0)])
        S.barrier()
        with nc.Fori(0, L):
            S.emit()
            S.reset_sems()
        Buf.reset_all()
        S = Sched(nc)
        scheds.append(S)
        final_phase()
        S.emit()
    else:
        for l in range(L):
            ctx_out = l < L - 1
            ffn_phase(l, 0, dr["ffn1_w13"][l], dr["ffn1_w2"][l], False, False)
            proj_phase(l)
            mla_phase(l, ctx_out)
            ssm_phase(l, ctx_out)
            gqa_phase(l, ctx_out)
            merge_phase(l, ctx_out)
            ffn_phase(l, 2, dr["ffn2_w13"][l], dr["ffn2_w2"][l], not ctx_out, l == L - 1)
        S.op("sp", None, reads=DBall("outT"))
        S.emit()
    nc._keep = scheds
    return nc


_CACHE = {}


def run(inputs, T=None, L=None, debug=False, mode="loop"):
    x = np.asarray(inputs["x"])
    B = x.shape[0]
    T = T or x.shape[1]
    L = L or np.asarray(inputs["ada_w"]).shape[0]
    sh, pcs = host_prep(inputs, T, L)
    shapes = {k: (v.shape, v.dtype) for k, v in list(sh.items()) + list(pcs[0].items())}
    key = (T, L, debug, mode)
    if key not in _CACHE:
        _CACHE[key] = build(T, L, shapes, debug, mode=mode)
    nc = _CACHE[key]
    in_maps = []
    for c in range(8):
        mp = dict(sh)
        mp.update(pcs[c % B])
        in_maps.append(mp)
    res = run_bass_kernel_spmd(nc, in_maps, core_ids=list(range(8)))
    out = np.stack([np.ascontiguousarray(res.results[b]["outT"].T) for b in range(B)], 0)
    return out.astype(np.float32), res


def run_layers(inputs):
    x = np.asarray(inputs["x"])
    B, T = x.shape[0], x.shape[1]
    L = np.asarray(inputs["ada_w"]).shape[0]
    h = [np.ascontiguousarray(np.concatenate([np.asarray(inputs["ctx"][b], np.float32).T, np.asarray(x[b], np.float32).T], 1))
         for b in range(B)]
    per_layer_keys = [k for k, v in inputs.items() if k not in ("x", "c", "ctx", "c_ctx", "final_norm")]
    nc_layer = None
    for l in range(L):
        inp_l = dict(inputs)
        for k in per_layer_keys:
            inp_l[k] = np.asarray(inputs[k])[l:l + 1]
        sh, pcs = host_prep(inp_l, T, 1)
        for pc in pcs:
            del pc["xT"], pc["ctxT"]
        del sh["fnT"]
        if nc_layer is None:
            shapes = {k: (v.shape, v.dtype) for k, v in list(sh.items()) + list(pcs[0].items())}
            shapes["h_in"] = ((D, T + CTX), np.float32)
            nc_layer = build(T, 1, shapes, False, mode="layer")
        in_maps = []
        for c in range(8):
            mp = dict(sh)
            mp.update(pcs[c % B])
            mp["h_in"] = h[c % B]
            in_maps.append(mp)
        res = run_bass_kernel_spmd(nc_layer, in_maps, core_ids=list(range(8)))
        h = [np.ascontiguousarray(res.results[b]["hT"]) for b in range(B)]
    shapes = {"fnT": ((128, 8), np.float32), "h_in": ((D, T + CTX), np.float32)}
    nc_fin = build(T, 1, shapes, False, mode="final")
    fn = fm(inputs["final_norm"], 8)
    res = run_bass_kernel_spmd(nc_fin, [{"fnT": fn, "h_in": h[c % B]} for c in range(8)], core_ids=list(range(8)))
    out = np.stack([np.ascontiguousarray(res.results[b]["outT"].T) for b in range(B)], 0)
    return out.astype(np.float32)


def kernel(**inputs):
    out, _ = run(inputs, mode="loop")
    return out
```

```python
import contextlib
import math
import numpy as np
import concourse.bass as bass
import concourse.mybir as mybir
from concourse.bass_utils import run_bass_kernel_spmd

F32 = mybir.dt.float32
BF16 = mybir.dt.bfloat16
AF = mybir.ActivationFunctionType
ALU = mybir.AluOpType

D = 1024
CTX = 256
FF = 2816
NFS = FF // 128
EPS = 1e-6


class Buf:
    __slots__ = ("name", "writers", "readers")

    ALL = []

    def __init__(self, name):
        self.name = name
        self.writers = []
        self.readers = []
        Buf.ALL.append(self)

    @staticmethod
    def reset_all():
        for b in Buf.ALL:
            b.writers = []
            b.readers = []


class Op:
    __slots__ = ("eng", "fn", "deps", "is_dma", "sem", "val", "signal", "idx", "strict")

    def __init__(self, eng, fn, is_dma):
        self.strict = ()
        self.eng = eng
        self.fn = fn
        self.deps = []
        self.is_dma = is_dma
        self.sem = None
        self.val = 0
        self.signal = is_dma
        self.idx = 0


class Sched:
    ENGS = ("pe", "dve", "act", "pool", "sp")

    def __init__(self, nc):
        self.nc = nc
        self.stack = contextlib.ExitStack()
        self.ops = []
        self.dma_sems = {}
        self.last = {}
        self.phase_dep = None

    def sbuf(self, name, shape, dt):
        return self.stack.enter_context(self.nc.sbuf_tensor(name, list(shape), dt))

    def psum(self, name, shape, dt=F32):
        return self.stack.enter_context(self.nc.psum_tensor(name, list(shape), dt))

    def sem(self, name):
        if not hasattr(self, "all_sems"):
            self.all_sems = []
        Sched._uid = getattr(Sched, "_uid", 0) + 1
        h = self.stack.enter_context(self.nc.semaphore("%s_%d" % (name, Sched._uid)))
        self.all_sems.append(h)
        return h

    def op(self, eng, fn, reads=(), writes=(), dma_key=None, extra=(), sreads=()):
        is_dma = dma_key is not None
        o = Op(eng, fn, is_dma)
        o.idx = len(self.ops)
        deps = set(extra)
        if sreads:
            st_ = set()
            for b in sreads:
                st_.update(b.writers)
            o.strict = st_
            deps.update(st_)
        if self.phase_dep is not None:
            deps.add(self.phase_dep)
        for b in reads:
            deps.update(b.writers)
        for b in writes:
            deps.update(b.writers)
            deps.update(b.readers)
        o.deps = sorted(deps)
        if is_dma:
            if dma_key not in self.dma_sems:
                self.dma_sems[dma_key] = [self.sem("d_" + str(dma_key)), 0]
            ent = self.dma_sems[dma_key]
            ent[1] += 16
            o.sem = ent[0]
            o.val = ent[1]
            self.last[("dma", dma_key)] = o.idx
        else:
            self.last[eng] = o.idx
        for b in reads:
            b.readers.append(o.idx)
        for b in writes:
            b.writers = [o.idx]
            b.readers = []
        self.ops.append(o)
        return o

    def barrier(self):
        deps = list(self.last.values())
        self.phase_dep = None
        o = self.op("sp", lambda e: e.nop(), extra=deps)
        self.phase_dep = o.idx
        self.last = {}

    def emit(self):
        nc = self.nc
        ops = self.ops
        esem = {e: self.sem("e_" + e) for e in self.ENGS}
        for o in ops:
            for d in o.deps:
                p = ops[d]
                if p.is_dma or (p.eng == o.eng and o.eng in ("pe", "sp")):
                    continue
                p.signal = True
        cnt = {e: 0 for e in self.ENGS}
        for o in ops:
            if not o.is_dma and o.signal:
                cnt[o.eng] += 1
                o.sem = esem[o.eng]
                o.val = cnt[o.eng]
        dma_issued = {}
        per_eng = {e: [] for e in self.ENGS}
        waited = {e: {} for e in self.ENGS}
        for o in ops:
            waits = {}
            for d in o.deps:
                p = ops[d]
                if p.is_dma:
                    s, v = dma_issued[id(p.sem)]
                else:
                    if p.eng == o.eng and o.eng in ("pe", "sp"):
                        continue
                    s, v = p.sem, p.val
                k = id(s)
                if waited[o.eng].get(k, 0) >= v:
                    continue
                if k not in waits or waits[k][1] < v:
                    waits[k] = (s, v)
            for k, (s, v) in waits.items():
                waited[o.eng][k] = v
            if o.is_dma:
                dma_issued[id(o.sem)] = (o.sem, o.val)
            per_eng[o.eng].append((o, list(waits.values())))

        def run(engname, e):
            for o, waits in per_eng[engname]:
                for s, v in waits:
                    e.wait_ge(s, v)
                if o.fn is None:
                    continue
                ins = o.fn(e)
                if o.signal:
                    ins.then_inc(o.sem, 16 if o.is_dma else 1)

        with nc.Block() as block:
            @block.tensor
            def _(e):
                run("pe", e)

            @block.vector
            def _(e):
                run("dve", e)

            @block.scalar
            def _(e):
                run("act", e)

            @block.gpsimd
            def _(e):
                run("pool", e)

            @block.sync
            def _(e):
                run("sp", e)

    def reset_sems(self):
        nc = self.nc
        for s in self.all_sems:
            nc.gpsimd.sem_clear(s)
        nc.all_engine_barrier()

    def close(self):
        self.stack.close()


class Tl:
    __slots__ = ("t", "b")

    def __init__(self, t, name):
        self.t = t
        self.b = Buf(name)


class Rot:
    def __init__(self, tiles):
        self.tiles = tiles
        self.i = 0

    def next(self):
        t = self.tiles[self.i % len(self.tiles)]
        self.i += 1
        return t


class Arena:
    def __init__(self, S, nbytes):
        self.t = S.sbuf("arena", [128, nbytes // 4], F32)
        self.cap = nbytes
        self.off = 0
        self.n = 0

    def alloc(self, free_shape, dt, name=None):
        esz = 4 if dt == F32 else 2
        n = int(np.prod(free_shape))
        nb = (n * esz + 63) // 64 * 64
        assert self.off + nb <= self.cap, ("arena overflow", name, self.off, nb, self.cap)
        v = self.t[:, self.off // 4:(self.off + nb) // 4]
        if dt != F32:
            v = v.bitcast(dt)
        v = v[:, 0:n]
        if len(free_shape) == 2:
            v = v.rearrange("p (a b) -> p a b", a=free_shape[0])
        elif len(free_shape) == 3:
            v = v.rearrange("p (a b c) -> p a b c", a=free_shape[0], b=free_shape[1])
        elif len(free_shape) == 4:
            v = v.rearrange("p (a b c d) -> p a b c d", a=free_shape[0], b=free_shape[1], c=free_shape[2])
        self.off += nb
        self.n += 1
        return Tl(v, name or ("a%d" % self.n))

    def rot(self, k, free_shape, dt, name):
        return Rot([self.alloc(free_shape, dt, "%s%d" % (name, i)) for i in range(k)])

    def mark(self):
        return self.off

    def release(self, m):
        self.off = m


def _swap_perm(n_axial):
    half = n_axial // 2
    q = half // 2
    perm = np.zeros(n_axial, np.int64)
    for i in range(n_axial):
        base = (i // half) * half
        j = i - base
        perm[i] = base + (j + q if j < q else j - q)
    return perm


def _rope_tables(n_axial, T, grid_w=64):
    half = n_axial // 2
    q = half // 2
    t = np.arange(T)
    row = (t // grid_w).astype(np.float32)
    col = (t % grid_w).astype(np.float32)
    inv = (10000.0 ** (-np.arange(0, half, 2, dtype=np.float32) / np.float32(half))).astype(np.float32)
    cos = np.zeros((n_axial, T), np.float32)
    sin = np.zeros((n_axial, T), np.float32)
    for i in range(n_axial):
        pos = row if i < half else col
        j = (i % half)
        ang = (pos * inv[j % q]).astype(np.float32)
        cos[i] = np.cos(ang)
        sn = np.sin(ang)
        sin[i] = -sn if j < q else sn
    return cos, sin


def fm(v, nch):
    v = np.asarray(v, np.float32)
    lead = v.shape[:-1]
    v = v.reshape(lead + (nch, 128))
    v = np.moveaxis(v, -1, 0)
    return np.ascontiguousarray(v)


def host_prep(inp, T, L):
    f32 = np.float32
    sh = {}
    sh["ada_w"] = np.ascontiguousarray(inp["ada_w"][:L], f32)
    sh["ada_bT"] = fm(inp["ada_b"][:L].reshape(L, 9, D), 8)
    norms = np.stack([inp["norm_ffn1"][:L], inp["norm_mix"][:L], inp["norm_ffn2"][:L]], 1)
    sh["normsT"] = fm(norms, 8)
    sh["fnT"] = fm(inp["final_norm"], 8)
    for k in ("ffn1_w13", "ffn1_w2", "ffn2_w13", "ffn2_w2", "mla_w_o", "ssm_w_glu", "gqa_w_o", "w_out", "mla_w_uq"):
        sh[k] = np.ascontiguousarray(inp[k][:L], f32)
    w_in = np.asarray(inp["w_in"][:L], f32)
    sh["w_in"] = np.ascontiguousarray(w_in)
    pm = _swap_perm(32)
    pg = _swap_perm(64)
    krp = np.zeros((L, D, 96), f32)
    krs = np.zeros((L, D, 96), f32)
    krp[:, :, 64:96] = w_in[:, :, 640:672]
    krs[:, :, 64:96] = w_in[:, :, 640 + pm]
    sh["w_krp"] = krp
    sh["w_krs"] = krs
    gq = w_in[:, :, 1184:1696].reshape(L, D, 8, 64)
    sh["w_gqs"] = np.ascontiguousarray(gq[:, :, :, pg].reshape(L, D, 512))
    gk = w_in[:, :, 1696:1824].reshape(L, D, 2, 64)
    sh["w_gks"] = np.ascontiguousarray(gk[:, :, :, pg].reshape(L, D, 128))
    uq = np.asarray(inp["mla_w_uq"][:L], f32).reshape(L, 384, 8, 96)
    uqs = uq.copy()
    uqs[:, :, :, 64:96] = uq[:, :, :, 64 + pm]
    sh["w_uqs"] = np.ascontiguousarray(uqs.reshape(L, 384, 768))
    ukv = np.asarray(inp["mla_w_ukv"][:L], f32).reshape(L, 256, 8, 128)
    sh["w_uk"] = np.ascontiguousarray(ukv[:, :, :, :64].reshape(L, 256, 512))
    sh["w_uv"] = np.ascontiguousarray(ukv[:, :, :, 64:].reshape(L, 256, 512))
    sh["qnT"] = fm(inp["mla_q_norm"][:L], 3)
    sh["kvnT"] = fm(inp["mla_kv_norm"][:L], 2)
    sh["ssm_dT"] = fm(inp["ssm_d"][:L], 4)

    def st_layout(a):
        a = np.asarray(a, f32)
        rest = a.shape[4:]
        a = a.reshape((L, 2, 16, 2, 64) + rest)
        a = np.moveaxis(a, (3, 4), (0, 1))
        return np.ascontiguousarray(a.reshape((128, L, 2, 16) + rest))
    sh["lamre"] = st_layout(inp["ssm_lambda_re"][:L])
    sh["lamim"] = st_layout(inp["ssm_lambda_im"][:L])
    ldt = np.broadcast_to(np.asarray(inp["ssm_log_dt"][:L], f32)[:, :, :, None], (L, 2, 32, 64))
    sh["logdt"] = st_layout(ldt)
    sh["Bre"] = st_layout(inp["ssm_b_re"][:L])
    sh["Bim"] = st_layout(inp["ssm_b_im"][:L])
    for nm, key in (("Cxre", "ssm_c_re"), ("Cxim", "ssm_c_im")):
        c = np.asarray(inp[key][:L], f32)
        cx = np.zeros((128, L, 2, 16, 128), f32)
        for st in range(16):
            for gi in range(2):
                g = 2 * st + gi
                c0 = (st % 4) * 32 + gi * 16
                cx[gi * 64:(gi + 1) * 64, :, :, st, c0:c0 + 16] = np.transpose(c[:, :, g, :, :], (3, 0, 1, 2))
        sh[nm] = cx
    sk = np.asarray(inp["gqa_sink"][:L], f32)
    sh["sinkb"] = np.ascontiguousarray(np.broadcast_to(sk[None, :, :, None], (1, L, 8, 128)))
    cm, sm = _rope_tables(32, T)
    cg, sg = _rope_tables(64, T)
    sh["cosM"] = cm
    sh["sinM"] = sm
    sh["cosG"] = np.ascontiguousarray(np.concatenate([cg, cg], 0))
    sh["sinG"] = np.ascontiguousarray(np.concatenate([sg, sg], 0))
    kk = np.arange(128)[:, None]
    qq = np.arange(128)[None, :]
    sh["maskL"] = np.ascontiguousarray(np.tile((kk >= qq).astype(f32), (1, 4)))
    sh["maskU"] = np.ascontiguousarray(np.tile((kk <= qq).astype(f32), (1, 4)))
    sh["ident"] = np.eye(128, dtype=f32)
    per_core = []
    for b in range(inp["x"].shape[0]):
        pc = {}
        pc["xT"] = np.ascontiguousarray(np.asarray(inp["x"][b], f32).T)
        pc["ctxT"] = np.ascontiguousarray(np.asarray(inp["ctx"][b], f32).T)
        cc = np.stack([np.asarray(inp["c"][b], f32), np.asarray(inp["c_ctx"], f32)], 0)
        pc["cc"] = np.ascontiguousarray(np.transpose(fm(cc, 8), (0, 2, 1)))
        per_core.append(pc)
    return sh, per_core


def build(T, L, shapes, debug=False, mode="full"):
    import os
    S_ = T + CTX
    NKT = S_ // 128
    nc = bass.Bass("TRN2", target_bir_lowering=False)
    S = Sched(nc)
    dr = {}
    for k, (shp, _) in shapes.items():
        dr[k] = nc.dram_tensor(k, list(shp), F32, kind="ExternalInput").ap()
    LB = L if mode == "full" else 1
    NOTLAYERED = ("xT", "ctxT", "cc", "fnT", "cosM", "sinM", "cosG", "sinG", "maskL", "maskU", "ident")
    LEAD = ("ada_w", "ffn1_w13", "ffn1_w2", "ffn2_w13", "ffn2_w2", "mla_w_o", "ssm_w_glu", "gqa_w_o", "w_out", "mla_w_uq",
            "w_in", "w_krp", "w_krs", "w_gqs", "w_gks", "w_uqs", "w_uk", "w_uv")
    BF16SET = ("ffn1_w13", "ffn1_w2", "ffn2_w13", "ffn2_w2", "w_in", "w_krp", "w_krs", "w_gqs", "w_gks", "mla_w_uq", "w_uqs",
               "w_uk", "w_uv", "mla_w_o", "ssm_w_glu", "gqa_w_o", "w_out", "Cxre", "Cxim")
    ext = dict(dr)
    layered = []
    WENG = "pool"
    if mode == "loop" and os.environ.get("LOOPDBG") != "1":
        WENG = "sp"
    if mode == "loop" and os.environ.get("LOOPDBG") != "1":
        layered = [k for k in shapes if k not in NOTLAYERED]
        for k in layered:
            dr[k] = nc.dram_tensor("wc_" + k, list(shapes[k][0]), BF16 if k in BF16SET else F32, kind="Internal").ap()
        for k in ("maskL", "maskU"):
            dr[k] = nc.dram_tensor("wc_" + k, list(shapes[k][0]), BF16, kind="Internal").ap()

    def flat(ap):
        n = len(ap.shape)
        if n <= 2:
            return ap
        pat = {3: "p a b -> p (a b)", 4: "p a b c -> p (a b c)", 5: "p a b c d -> p (a b c d)"}[n]
        return ap.rearrange(pat)

    def lslice(ap, k, l):
        return flat(ap[l]) if k in LEAD else flat(ap[:, l])
    if mode != "layer":
        outT = nc.dram_tensor("outT", [D, T], F32, kind="ExternalOutput").ap()
    sk = "ExternalOutput" if debug else "Internal"

    def scratch(name, shape, dt):
        return nc.dram_tensor(name, list(shape), dt, kind=sk).ap()
    if mode == "layer":
        hT = nc.dram_tensor("hT", [D, S_], F32, kind="ExternalOutput").ap()
    else:
        hT = scratch("hT", [D, S_], F32)
    XMT = scratch("XMT", [D, S_], BF16)
    QT = scratch("QT", [8, 96, S_], BF16)
    KNT = scratch("KNT", [512, S_], BF16)
    KRT = scratch("KRT", [32, S_], BF16)
    VM = scratch("VM", [S_, 520], BF16)
    UT = scratch("UT", [512, S_], F32)
    GQT = scratch("GQT", [512, S_], BF16)
    GKT = scratch("GKT", [128, S_], BF16)
    GVM = scratch("GVM", [S_, 130], BF16)
    OMT = scratch("OMT", [512, S_], BF16)
    YT = scratch("YT", [512, S_], F32)
    GYT = scratch("GYT", [512, S_], BF16)
    OGT = scratch("OGT", [512, S_], BF16)
    dbuf = {}

    def DB(name, i=0):
        k = (name, i)
        if k not in dbuf:
            dbuf[k] = Buf("%s_%s" % (name, i))
        return dbuf[k]

    def DBall(name):
        return [b for (n, _), b in dbuf.items() if n == name]

    A = Arena(S, 200 * 1024)
    _banks = [Tl(S.psum("ps%d" % i, [128, 512]), "ps%d" % i) for i in range(8)]
    PS = Rot(_banks[0:6])
    PSA = Rot(_banks[6:8])
    dbg_outs = {}

    def dbg_dump(name, tl, dt=F32):
        if not debug:
            return
        shp = list(tl.t.shape)
        d_ = nc.dram_tensor("dbg_" + name, shp, dt, kind="ExternalOutput").ap()
        dma("dbg_" + name, d_, tl.t, [tl.b], [DB("dbg_" + name, 0)])

    def mm(out, lhsT, rhs, start, stop, reads, writes):
        S.op("pe", lambda e: e.matmul(out, lhsT=lhsT, rhs=rhs, start=start, stop=stop), reads, writes)

    def act(out, in_, func, reads, writes, **kw):
        S.op("act", lambda e: e.activation(out=out, in_=in_, func=func, **kw), reads, writes)

    def tt(out, in0, in1, op, reads, writes, eng="dve"):
        S.op(eng, lambda e: e.tensor_tensor(out=out, in0=in0, in1=in1, op=op), reads, writes)

    def ts(out, in0, s1, s2, op0, op1, reads, writes, eng="dve", sreads=()):
        if op1 is None:
            S.op(eng, lambda e: e.tensor_scalar(out=out, in0=in0, scalar1=s1, scalar2=None, op0=op0), reads, writes, sreads=sreads)
        else:
            S.op(eng, lambda e: e.tensor_scalar(out=out, in0=in0, scalar1=s1, scalar2=s2, op0=op0, op1=op1), reads, writes, sreads=sreads)

    def stt(out, in0, scalar, in1, op0, op1, reads, writes, sreads=()):
        S.op("dve", lambda e: e.scalar_tensor_tensor(out=out, in0=in0, scalar=scalar, in1=in1, op0=op0, op1=op1), reads, writes, sreads=sreads)

    def scan(out, d0, d1, init, reads, writes):
        S.op("dve", lambda e: e.tensor_tensor_scan(out=out, data0=d0, data1=d1, initial=init, op0=ALU.mult, op1=ALU.add), reads, writes)

    def cp(out, in_, reads, writes, eng="dve"):
        S.op(eng, lambda e: e.tensor_copy(out=out, in_=in_), reads, writes)

    def recip(out, in_, reads, writes):
        S.op("dve", lambda e: e.reciprocal(out=out, in_=in_), reads, writes)

    def memset(ap, val, writes, eng="dve"):
        S.op(eng, lambda e: e.memset(ap, val), (), writes)

    def dma(key, out, in_, reads, writes, eng="sp", **kw):
        S.op(eng, lambda e: e.dma_start(out=out, in_=in_, **kw), reads, writes, dma_key=key)

    def wload(tl, src, kc_list=None):
        nkc = tl.t.shape[1]
        v = src.rearrange("(kc p) f -> p kc f", p=128)
        for kc in range(nkc):
            if WENG == "pool":
                dma(tl.b.name, tl.t[:, kc, :], v[:, kc, :], (), [tl.b], eng="pool", max_dma_last_dim=4096)
            else:
                dma(tl.b.name, tl.t[:, kc, :], v[:, kc, :], (), [tl.b], eng="sp")

    ones_bf = A.alloc([128], BF16, "ones_bf")
    ones_f = A.alloc([128], F32, "ones_f")
    ident = A.alloc([128], F32, "ident")
    memset(ones_bf.t, 1.0, [ones_bf.b])
    memset(ones_f.t, 1.0, [ones_f.b])
    if "ident" in dr:
        dma("ident", ident.t, dr["ident"], (), [ident.b])
    modT = A.alloc([LB, 9, 8, 2], F32, "modT")
    gsT = A.alloc([LB, 3, 8, 2], F32, "gsT")
    hgT = A.alloc([LB, 3, 8, 2], F32, "hgT")
    normsT = A.alloc([LB, 3, 8], F32, "normsT")
    fnT = A.alloc([8], F32, "fnT")
    qnT = A.alloc([LB, 3], F32, "qnT")
    kvnT = A.alloc([LB, 2], F32, "kvnT")
    sdT = A.alloc([LB, 4], F32, "sdT")
    if "fnT" in dr:
        dma("fnT", fnT.t, dr["fnT"], (), [fnT.b])

    def setup_layers():
        for tl, nm in ((normsT, "normsT"), (qnT, "qnT"), (kvnT, "kvnT"), (sdT, "ssm_dT")):
            dma(tl.b.name, tl.t, dr[nm][:, 0:LB], (), [tl.b])
        m0 = A.mark()
        cc = A.alloc([8, 2], F32, "cc")
        scT = A.alloc([8, 2], F32, "scT")
        adab = A.alloc([LB, 9, 8], F32, "adab")
        dma("cc", cc.t, dr["cc"], (), [cc.b])
        dma("adab", adab.t, dr["ada_bT"][:, 0:LB], (), [adab.b])
        act(scT.t, cc.t, AF.Silu, [cc.b], [scT.b])
        awr = A.rot(2, [8, 1024], F32, "adaw")
        for l in range(LB):
            pst = PSA.next()
            for i in range(9):
                aw = awr.next()
                src = dr["ada_w"][l, :, i * 1024:(i + 1) * 1024].rearrange("(kc p) f -> p kc f", p=128)
                for half in range(2):
                    dma(aw.b.name, aw.t[:, half * 4:(half + 1) * 4, :], src[:, half * 4:(half + 1) * 4, :], (), [aw.b])
                for j in range(8):
                    c0 = (i * 8 + j) * 2
                    for kc in range(8):
                        mm(pst.t[:, c0:c0 + 2], aw.t[:, kc, j * 128:(j + 1) * 128], scT.t[:, kc, :], kc == 0, kc == 7,
                           [aw.b, scT.b], [pst.b])
            pv = pst.t[:, 0:144].rearrange("p (i j k) -> p i j k", i=9, j=8)
            for k in range(2):
                tt(modT.t[:, l, :, :, k], pv[:, :, :, k], adab.t[:, l, :, :], ALU.add, [pst.b, adab.b], [modT.b])
            for n in range(3):
                for k in range(2):
                    stt(gsT.t[:, l, n, :, k], modT.t[:, l, 3 * n + 1, :, k], 1.0, normsT.t[:, l, n, :], ALU.add, ALU.mult,
                        [modT.b, normsT.b], [gsT.b])
                ts(hgT.t[:, l, n, :, :], modT.t[:, l, 3 * n + 2, :, :], 1.0 if n == 1 else 0.5, None, ALU.mult, None,
                   [modT.b], [hgT.b])
        S.barrier()
        A.release(m0)

    NTL = T // 512
    if mode in ("full", "loop"):
        dma("h0", hT[:, 0:CTX], dr["ctxT"], (), [DB("hT", 0)])
        for i in range(NTL):
            dma("h0", hT[:, CTX + i * 512:CTX + (i + 1) * 512], dr["xT"][:, i * 512:(i + 1) * 512], (), [DB("hT", 1 + i)])
    else:
        dma("h0", hT[:, 0:CTX], dr["h_in"][:, 0:CTX], (), [DB("hT", 0)])
        for i in range(NTL):
            dma("h0", hT[:, CTX + i * 512:CTX + (i + 1) * 512], dr["h_in"][:, CTX + i * 512:CTX + (i + 1) * 512], (), [DB("hT", 1 + i)])
    if mode == "loop" and layered:
        for k in layered:
            for l in range(L):
                if k in BF16SET:
                    dma("cpyc", lslice(dr[k], k, l), lslice(ext[k], k, l), (), [DB("wc_" + k, l)], eng="pool", max_dma_last_dim=4096)
                else:
                    dma("cpy", lslice(dr[k], k, l), lslice(ext[k], k, l), (), [DB("wc_" + k, l)])
        for k in ("maskL", "maskU"):
            dma("cpyc", dr[k], ext[k], (), [DB("wc_" + k, 0)], eng="pool", max_dma_last_dim=4096)
    if mode in ("full", "layer"):
        setup_layers()
    else:
        S.barrier()

    tiles512 = [(0, CTX, True, 0)] + [(CTX + i * 512, 512, False, 1 + i) for i in range(NTL)]
    tiles256 = [(0, CTX, True, [0])] + [(CTX + i * 256, 256, False, [1 + i // 2]) for i in range(T // 256)]

    def norm_mod(Ht, N, nchunks, gs_ap, sh_ap, out_t, sq, rstd, tmpr, dim, extra_reads=()):
        act(sq.t[:, 0:nchunks, :N], Ht.t[:, 0:nchunks, :N], AF.Square, [Ht.b], [sq.b])
        pss = PS.next()
        for j in range(nchunks):
            mm(pss.t[:, :N], ones_bf.t, sq.t[:, j, :N], j == 0, j == nchunks - 1, [sq.b, ones_bf.b], [pss.b])
        ts(rstd.t[:, :N], pss.t[:, :N], 1.0 / dim, EPS, ALU.mult, ALU.add, [pss.b], [rstd.b])
        act(rstd.t[:, :N], rstd.t[:, :N], AF.Sqrt, [rstd.b], [rstd.b])
        recip(rstd.t[:, :N], rstd.t[:, :N], [rstd.b], [rstd.b])
        for j in range(nchunks):
            if sh_ap is None:
                stt(out_t.t[:, j, :N], Ht.t[:, j, :N], gs_ap(j), rstd.t[:, :N], ALU.mult, ALU.mult,
                    [Ht.b, rstd.b] + list(extra_reads), [out_t.b])
            else:
                tm = tmpr.next()
                stt(tm.t[:, :N], Ht.t[:, j, :N], gs_ap(j), rstd.t[:, :N], ALU.mult, ALU.mult,
                    [Ht.b, rstd.b] + list(extra_reads), [tm.b])
                act(out_t.t[:, j, :N], tm.t[:, :N], AF.Identity, [tm.b] + list(extra_reads), [out_t.b], bias=sh_ap(j))

    def ffn_phase(l, nidx, w13_dr, w2_dr, skip_ctx, final):
        m = A.mark()
        W13 = A.alloc([8, 2 * FF], BF16, "W13")
        W2 = A.alloc([NFS, D], BF16, "W2")
        wload(W13, w13_dr)
        wload(W2, w2_dr)
        Hr = A.rot(2, [8, 256], F32, "H")
        xnr = A.rot(2, [8, 256], BF16, "xn")
        sq = A.alloc([8, 256], BF16, "sq")
        g = A.alloc([NFS, 256], BF16, "g")
        rstd = A.alloc([256], F32, "rstd")
        tmpr = A.rot(2, [256], F32, "tmp")
        sir = A.rot(2, [256], BF16, "si")
        if final:
            ot = A.alloc([8, 256], F32, "ot")
        tl_ = [t_ for t_ in tiles256 if not (t_[2] and skip_ctx)]

        def stage_a(t_):
            (c0, N, isctx, hb) = t_
            k = 1 if isctx else 0
            H = Hr.next()
            xn = xnr.next()
            dma(H.b.name, H.t[:, :, :N], hT[:, c0:c0 + N].rearrange("(j p) n -> p j n", p=128),
                [DB("hT", i) for i in hb], [H.b])
            norm_mod(H, N, 8, lambda j: gsT.t[:, l, nidx, j, k:k + 1], lambda j: modT.t[:, l, 3 * nidx, j, k:k + 1],
                     xn, sq, rstd, tmpr, D, extra_reads=[gsT.b, modT.b])
            return H, xn
        cur = stage_a(tl_[0])
        for ti_, (c0, N, isctx, hb) in enumerate(tl_):
            k = 1 if isctx else 0
            H, xn = cur
            for s in range(NFS):
                pa = PS.next()
                pb = PS.next()
                for kc in range(8):
                    mm(pa.t[:, :N], W13.t[:, kc, s * 128:(s + 1) * 128], xn.t[:, kc, :N], kc == 0, kc == 7, [W13.b, xn.b], [pa.b])
                for kc in range(8):
                    mm(pb.t[:, :N], W13.t[:, kc, FF + s * 128:FF + (s + 1) * 128], xn.t[:, kc, :N], kc == 0, kc == 7, [W13.b, xn.b], [pb.b])
                si = sir.next()
                act(si.t[:, :N], pa.t[:, :N], AF.Silu, [pa.b], [si.b])
                tt(g.t[:, s, :N], si.t[:, :N], pb.t[:, :N], ALU.mult, [si.b, pb.b], [g.b])
            if ti_ + 1 < len(tl_):
                cur = stage_a(tl_[ti_ + 1])
            for d in range(8):
                po = PS.next()
                for fc in range(NFS):
                    mm(po.t[:, :N], W2.t[:, fc, d * 128:(d + 1) * 128], g.t[:, fc, :N], fc == 0, fc == NFS - 1, [W2.b, g.b], [po.b])
                stt(H.t[:, d, :N], po.t[:, :N], hgT.t[:, l, nidx, d, k:k + 1], H.t[:, d, :N], ALU.mult, ALU.add,
                    [po.b, hgT.b, H.b], [H.b])
            if final:
                if not isctx:
                    norm_mod(H, N, 8, lambda j: fnT.t[:, j:j + 1], None, ot, sq, rstd, tmpr, D, extra_reads=[fnT.b])
                    dma(ot.b.name, outT[:, c0 - CTX:c0 - CTX + N].rearrange("(j p) n -> p j n", p=128), ot.t[:, :, :N],
                        [ot.b], [DB("outT", 0)])
            else:
                dma(H.b.name, hT[:, c0:c0 + N].rearrange("(j p) n -> p j n", p=128), H.t[:, :, :N], [H.b],
                    [DB("hT", i) for i in hb])
        S.barrier()
        A.release(m)

    def proj_phase(l):
        m = A.mark()
        win = dr["w_in"][l]
        Wq1 = A.alloc([8, 384], BF16, "Wq1"); wload(Wq1, win[:, 0:384])
        Wkv1 = A.alloc([8, 256], BF16, "Wkv1"); wload(Wkv1, win[:, 384:640])
        Wkrp = A.alloc([8, 96], BF16, "Wkrp"); wload(Wkrp, dr["w_krp"][l])
        Wkrs = A.alloc([8, 96], BF16, "Wkrs"); wload(Wkrs, dr["w_krs"][l])
        Wu = A.alloc([8, 512], BF16, "Wu"); wload(Wu, win[:, 672:1184])
        Wgq = A.alloc([8, 512], BF16, "Wgq"); wload(Wgq, win[:, 1184:1696])
        Wgqs = A.alloc([8, 512], BF16, "Wgqs"); wload(Wgqs, dr["w_gqs"][l])
        Wgk = A.alloc([8, 128], BF16, "Wgk"); wload(Wgk, win[:, 1696:1824])
        Wgks = A.alloc([8, 128], BF16, "Wgks"); wload(Wgks, dr["w_gks"][l])
        Wgv = A.alloc([8, 128], BF16, "Wgv"); wload(Wgv, win[:, 1824:1952])
        Wuq = A.alloc([3, 768], BF16, "Wuq"); wload(Wuq, dr["mla_w_uq"][l])
        Wuqs = A.alloc([3, 768], BF16, "Wuqs"); wload(Wuqs, dr["w_uqs"][l])
        Wuk = A.alloc([2, 512], BF16, "Wuk"); wload(Wuk, dr["w_uk"][l])
        Wuv = A.alloc([2, 512], BF16, "Wuv"); wload(Wuv, dr["w_uv"][l])
        Hr = A.rot(2, [8, 512], F32, "H")
        xm = A.alloc([8, 512], BF16, "xm")
        sq = A.alloc([8, 512], BF16, "sq")
        rstd = A.alloc([512], F32, "rstd")
        tmpr = A.rot(2, [512], F32, "tmp")
        cq = A.alloc([3, 512], F32, "cq")
        cqn = A.alloc([3, 512], BF16, "cqn")
        ckvn = A.alloc([2, 512], BF16, "ckvn")
        rt = A.rot(2, [4, 512], F32, "ropet")
        qtr = A.rot(3, [512], BF16, "qt")
        t1r = A.rot(2, [512], F32, "t1")
        t2r = A.rot(2, [512], F32, "t2")
        utr = A.rot(2, [512], F32, "ut")
        vst = A.rot(2, [8, 65], BF16, "vst")
        vst2 = A.rot(2, [2, 65], BF16, "vst2")
        for v in vst.tiles + vst2.tiles:
            memset(v.t, 1.0, [v.b])

        def rope_out(dst, p, psw, lo, hi, cosr, sinr, rtb, N, isctx):
            if isctx:
                cp(dst.t[lo:hi, :N], p.t[lo:hi, :N], [p.b], [dst.b])
                return
            t1 = t1r.next()
            t2 = t2r.next()
            tt(t1.t[lo:hi, :N], p.t[lo:hi, :N], cosr[lo:hi, :N], ALU.mult, [p.b, rtb], [t1.b])
            tt(t2.t[lo:hi, :N], psw.t[lo:hi, :N], sinr[lo:hi, :N], ALU.mult, [psw.b, rtb], [t2.b])
            tt(dst.t[lo:hi, :N], t1.t[lo:hi, :N], t2.t[lo:hi, :N], ALU.add, [t1.b, t2.b], [dst.b], eng="pool")

        for (c0, N, isctx, hb) in tiles512:
            k = 1 if isctx else 0
            H = Hr.next()
            dma(H.b.name, H.t[:, :, :N], hT[:, c0:c0 + N].rearrange("(j p) n -> p j n", p=128), [DB("hT", hb)], [H.b])
            R = rt.next()
            if not isctx:
                t0 = c0 - CTX
                dma(R.b.name, R.t[64:96, 0, :N], dr["cosM"][:, t0:t0 + N], (), [R.b])
                dma(R.b.name, R.t[64:96, 1, :N], dr["sinM"][:, t0:t0 + N], (), [R.b])
                dma(R.b.name, R.t[:, 2, :N], dr["cosG"][:, t0:t0 + N], (), [R.b])
                dma(R.b.name, R.t[:, 3, :N], dr["sinG"][:, t0:t0 + N], (), [R.b])
            norm_mod(H, N, 8, lambda j: gsT.t[:, l, 1, j, k:k + 1], lambda j: modT.t[:, l, 3, j, k:k + 1],
                     xm, sq, rstd, tmpr, D, extra_reads=[gsT.b, modT.b])
            dma(xm.b.name, XMT[:, c0:c0 + N].rearrange("(j p) n -> p j n", p=128), xm.t[:, :, :N], [xm.b], [DB("XMT", hb)])
            for j in range(3):
                p = PS.next()
                for kc in range(8):
                    mm(p.t[:, :N], Wq1.t[:, kc, j * 128:(j + 1) * 128], xm.t[:, kc, :N], kc == 0, kc == 7, [Wq1.b, xm.b], [p.b])
                act(cq.t[:, j, :N], p.t[:, :N], AF.Identity, [p.b], [cq.b])
            norm_mod(cq, N, 3, lambda j: qnT.t[:, l, j:j + 1], None, cqn, sq, rstd, tmpr, 384, extra_reads=[qnT.b])
            for h in range(8):
                p = PS.next()
                psw = PS.next()
                for kc in range(3):
                    mm(p.t[0:96, :N], Wuq.t[:, kc, h * 96:(h + 1) * 96], cqn.t[:, kc, :N], kc == 0, kc == 2, [Wuq.b, cqn.b], [p.b])
                if not isctx:
                    for kc in range(3):
                        mm(psw.t[0:96, :N], Wuqs.t[:, kc, h * 96:(h + 1) * 96], cqn.t[:, kc, :N], kc == 0, kc == 2, [Wuqs.b, cqn.b], [psw.b])
                qt = qtr.next()
                act(qt.t[0:64, :N], p.t[0:64, :N], AF.Identity, [p.b], [qt.b])
                rope_out(qt, p, psw, 64, 96, R.t[:, 0, :], R.t[:, 1, :], R.b, N, isctx)
                dma(qt.b.name, QT[h, :, c0:c0 + N], qt.t[0:96, :N], [qt.b], [DB("QT", hb)])
            for j in range(2):
                p = PS.next()
                for kc in range(8):
                    mm(p.t[:, :N], Wkv1.t[:, kc, j * 128:(j + 1) * 128], xm.t[:, kc, :N], kc == 0, kc == 7, [Wkv1.b, xm.b], [p.b])
                act(cq.t[:, j, :N], p.t[:, :N], AF.Identity, [p.b], [cq.b])
            norm_mod(cq, N, 2, lambda j: kvnT.t[:, l, j:j + 1], None, ckvn, sq, rstd, tmpr, 256, extra_reads=[kvnT.b])
            for j in range(4):
                p = PS.next()
                for kc in range(2):
                    mm(p.t[:, :N], Wuk.t[:, kc, j * 128:(j + 1) * 128], ckvn.t[:, kc, :N], kc == 0, kc == 1, [Wuk.b, ckvn.b], [p.b])
                qt = qtr.next()
                act(qt.t[:, :N], p.t[:, :N], AF.Identity, [p.b], [qt.b])
                dma(qt.b.name, KNT[j * 128:(j + 1) * 128, c0:c0 + N], qt.t[:, :N], [qt.b], [DB("KNT", hb)])
            for tb in range(N // 128):
                p = PS.next()
                for kc in range(2):
                    mm(p.t[:, :], ckvn.t[:, kc, tb * 128:(tb + 1) * 128], Wuv.t[:, kc, :], kc == 0, kc == 1, [Wuv.b, ckvn.b], [p.b])
                v = vst.next()
                cp(v.t[:, :, 0:64], p.t[:, :].rearrange("p (h d) -> p h d", h=8), [p.b], [v.b])
                dma(v.b.name, VM[c0 + tb * 128:c0 + (tb + 1) * 128, :], v.t.rearrange("p h d -> p (h d)"), [v.b], [DB("VM", hb)])
            p = PS.next()
            psw = PS.next()
            for kc in range(8):
                mm(p.t[0:96, :N], Wkrp.t[:, kc, :], xm.t[:, kc, :N], kc == 0, kc == 7, [Wkrp.b, xm.b], [p.b])
            if not isctx:
                for kc in range(8):
                    mm(psw.t[0:96, :N], Wkrs.t[:, kc, :], xm.t[:, kc, :N], kc == 0, kc == 7, [Wkrs.b, xm.b], [psw.b])
            qt = qtr.next()
            rope_out(qt, p, psw, 64, 96, R.t[:, 0, :], R.t[:, 1, :], R.b, N, isctx)
            dma(qt.b.name, KRT[:, c0:c0 + N], qt.t[64:96, :N], [qt.b], [DB("KRT", hb)])
            for j in range(4):
                p = PS.next()
                for kc in range(8):
                    mm(p.t[:, :N], Wu.t[:, kc, j * 128:(j + 1) * 128], xm.t[:, kc, :N], kc == 0, kc == 7, [Wu.b, xm.b], [p.b])
                ut = utr.next()
                act(ut.t[:, :N], p.t[:, :N], AF.Identity, [p.b], [ut.b])
                dma(ut.b.name, UT[j * 128:(j + 1) * 128, c0:c0 + N], ut.t[:, :N], [ut.b], [DB("UT", hb)])
            for j in range(5):
                Wa, Ws, cs = (Wgq, Wgqs, slice(j * 128, (j + 1) * 128)) if j < 4 else (Wgk, Wgks, slice(0, 128))
                p = PS.next()
                psw = PS.next()
                for kc in range(8):
                    mm(p.t[:, :N], Wa.t[:, kc, cs], xm.t[:, kc, :N], kc == 0, kc == 7, [Wa.b, xm.b], [p.b])
                if not isctx:
                    for kc in range(8):
                        mm(psw.t[:, :N], Ws.t[:, kc, cs], xm.t[:, kc, :N], kc == 0, kc == 7, [Ws.b, xm.b], [psw.b])
                qt = qtr.next()
                rope_out(qt, p, psw, 0, 128, R.t[:, 2, :], R.t[:, 3, :], R.b, N, isctx)
                if j < 4:
                    dma(qt.b.name, GQT[j * 128:(j + 1) * 128, c0:c0 + N], qt.t[:, :N], [qt.b], [DB("GQT", hb)])
                else:
                    dma(qt.b.name, GKT[:, c0:c0 + N], qt.t[:, :N], [qt.b], [DB("GKT", hb)])
            for tb in range(N // 128):
                p = PS.next()
                for kc in range(8):
                    mm(p.t[:, 0:128], xm.t[:, kc, tb * 128:(tb + 1) * 128], Wgv.t[:, kc, :], kc == 0, kc == 7, [Wgv.b, xm.b], [p.b])
                v = vst2.next()
                cp(v.t[:, :, 0:64], p.t[:, 0:128].rearrange("p (h d) -> p h d", h=2), [p.b], [v.b])
                dma(v.b.name, GVM[c0 + tb * 128:c0 + (tb + 1) * 128, :], v.t.rearrange("p h d -> p (h d)"), [v.b], [DB("GVM", hb)])
        S.barrier()
        A.release(m)

    def finalize_attn(po, N, osb, rec, extra_den, dst_dram_ap, dst_key, otile):
        act(osb.t[0:65, :N], po.t[0:65, :N], AF.Identity, [po.b], [osb.b])
        if extra_den is not None:
            ap_, b_ = extra_den
            tt(osb.t[64:65, :N], osb.t[64:65, :N], ap_, ALU.add, [osb.b, b_], [osb.b])
        recip(rec.t[64:65, :N], osb.t[64:65, :N], [osb.b], [rec.b])
        pb = PS.next()
        mm(pb.t[0:64, :N], ones_f.t[64:65, 0:64], rec.t[64:65, :N], True, True, [ones_f.b, rec.b], [pb.b])
        tt(otile.t[0:64, :N], osb.t[0:64, :N], pb.t[0:64, :N], ALU.mult, [osb.b, pb.b], [otile.b])
        dma(otile.b.name, dst_dram_ap, otile.t[0:64, :N], [otile.b], [dst_key])

    def mla_phase(l, ctx_out):
        m = A.mark()
        Vall = A.alloc([NKT, 520], BF16, "Vall")
        dma("Vall", Vall.t, VM.rearrange("(kt p) c -> p kt c", p=128), DBall("VM"), [Vall.b])
        Kr = A.rot(2, [S_], BF16, "Kh")
        Qr = A.rot(2, [S_], BF16, "Qh")
        pTr = A.rot(4, [512], BF16, "pT")
        osbr = A.rot(2, [512], F32, "osb")
        recr = A.rot(2, [512], F32, "rec")
        otr = A.rot(2, [512], BF16, "ot")
        scale = 96 ** -0.5
        qblocks = ([(0, CTX, 2)] if ctx_out else []) + [(CTX + i * 512, 512, NKT) for i in range(NTL)]
        for h in range(8):
            Kh = Kr.next()
            Qh = Qr.next()
            dma(Kh.b.name, Kh.t[0:64, :], KNT[h * 64:(h + 1) * 64, :], DBall("KNT"), [Kh.b])
            dma(Kh.b.name, Kh.t[64:96, :], KRT[:, :], DBall("KRT"), [Kh.b])
            dma(Qh.b.name, Qh.t[0:96, :], QT[h, :, :], DBall("QT"), [Qh.b])
            for (c0, N, nk) in qblocks:
                po = PSA.next()
                for kt in range(nk):
                    ps = PS.next()
                    mm(ps.t[:, :N], Kh.t[0:96, kt * 128:(kt + 1) * 128], Qh.t[0:96, c0:c0 + N], True, True, [Kh.b, Qh.b], [ps.b])
                    pT = pTr.next()
                    act(pT.t[:, :N], ps.t[:, :N], AF.Exp, [ps.b], [pT.b], scale=scale)
                    mm(po.t[0:65, :N], Vall.t[:, kt, h * 65:(h + 1) * 65], pT.t[:, :N], kt == 0, kt == nk - 1, [Vall.b, pT.b], [po.b])
                finalize_attn(po, N, osbr.next(), recr.next(), None, OMT[h * 64:(h + 1) * 64, c0:c0 + N],
                              DB("OMT", (h, c0)), otr.next())
        S.barrier()
        A.release(m)

    def gqa_phase(l, ctx_out):
        m = A.mark()
        V2 = A.alloc([NKT, 130], BF16, "V2")
        dma("V2", V2.t, GVM.rearrange("(kt p) c -> p kt c", p=128), DBall("GVM"), [V2.b])
        mL = A.alloc([512], BF16, "mL")
        mU = A.alloc([512], BF16, "mU")
        dma("mL", mL.t, dr["maskL"], (), [mL.b], eng=WENG)
        dma("mU", mU.t, dr["maskU"], (), [mU.b], eng=WENG)
        skx = A.alloc([8, 128], F32, "skx")
        dma("skx", skx.t[64:65, :, :], dr["sinkb"][:, l, :, :], (), [skx.b])
        act(skx.t[64:65, :, :], skx.t[64:65, :, :], AF.Exp, [skx.b], [skx.b])
        K2r = A.rot(1, [S_], BF16, "K2")
        Q2r = A.rot(1, [4, S_], BF16, "Q2")
        pTr = A.rot(4, [512], BF16, "pT")
        osbr = A.rot(2, [512], F32, "osb")
        recr = A.rot(2, [512], F32, "rec")
        otr = A.rot(2, [4, 128], BF16, "ot")
        nb = T // 128
        for kvh in range(2):
            K2 = K2r.next()
            Q2 = Q2r.next()
            dma(K2.b.name, K2.t[0:64, :], GKT[kvh * 64:(kvh + 1) * 64, :], DBall("GKT"), [K2.b])
            dma(Q2.b.name, Q2.t[0:64, :, :], GQT[kvh * 256:(kvh + 1) * 256, :].rearrange("(g d) s -> d g s", g=4),
                DBall("GQT"), [Q2.b])
            blocks = []
            if ctx_out:
                for cb in range(2):
                    blocks.append((cb * 128, [(0, None), (1, None)]))
            for n in range(nb):
                kts = [(0, None), (1, None)]
                if n > 0:
                    kts.append((2 + n - 1, mL))
                kts.append((2 + n, None))
                if n < nb - 1:
                    kts.append((2 + n + 1, mU))
                blocks.append((CTX + n * 128, kts))
            for (c0, kts) in blocks:
                po = PSA.next()
                for i, (kt, msk) in enumerate(kts):
                    ps = PS.next()
                    mm(ps.t[:, :], K2.t[0:64, kt * 128:(kt + 1) * 128], Q2.t[0:64, :, c0:c0 + 128], True, True, [K2.b, Q2.b], [ps.b])
                    pT = pTr.next()
                    act(pT.t[:, :], ps.t[:, :], AF.Exp, [ps.b], [pT.b], scale=0.125)
                    if msk is not None:
                        tt(pT.t[:, :], pT.t[:, :], msk.t, ALU.mult, [pT.b, msk.b], [pT.b])
                    mm(po.t[0:65, :], V2.t[:, kt, kvh * 65:(kvh + 1) * 65], pT.t[:, :], i == 0, i == len(kts) - 1, [V2.b, pT.b], [po.b])
                ot = otr.next()
                osb = osbr.next()
                rec = recr.next()
                act(osb.t[0:65, :], po.t[0:65, :], AF.Identity, [po.b], [osb.b])
                tt(osb.t[64:65, :], osb.t[64:65, :], skx.t[64:65, kvh * 4:(kvh + 1) * 4, :].rearrange("p g q -> p (g q)"),
                   ALU.add, [osb.b, skx.b], [osb.b])
                recip(rec.t[64:65, :], osb.t[64:65, :], [osb.b], [rec.b])
                pb = PS.next()
                mm(pb.t[0:64, :], ones_f.t[64:65, 0:64], rec.t[64:65, :], True, True, [ones_f.b, rec.b], [pb.b])
                tt(ot.t[0:64, :, :].rearrange("p g q -> p (g q)"), osb.t[0:64, :], pb.t[0:64, :], ALU.mult, [osb.b, pb.b], [ot.b])
                dma(ot.b.name, OGT[kvh * 256:(kvh + 1) * 256, c0:c0 + 128].rearrange("(g d) q -> d g q", g=4), ot.t[0:64, :, :],
                    [ot.b], [DB("OGT", (kvh, c0))])
        S.barrier()
        A.release(m)

    def ssm_phase(l, ctx_out):
        m = A.mark()
        LC = 512
        par = {}
        for nm in ("lamre", "lamim", "logdt"):
            par[nm] = A.alloc([2, 16], F32, nm)
            dma(nm, par[nm].t, dr[nm][:, l, :, :], (), [par[nm].b])
        Bre = A.alloc([2, 16, 16], F32, "Bre")
        Bim = A.alloc([2, 16, 16], F32, "Bim")
        dma("Bre", Bre.t, dr["Bre"][:, l], (), [Bre.b])
        dma("Bim", Bim.t, dr["Bim"][:, l], (), [Bim.b])
        Cre = A.alloc([2, 16, 128], BF16, "Cre")
        Cim = A.alloc([2, 16, 128], BF16, "Cim")
        for d in range(2):
            if WENG == "pool":
                dma("Cre", Cre.t[:, d], dr["Cxre"][:, l, d], (), [Cre.b], eng="pool", max_dma_last_dim=4096)
                dma("Cim", Cim.t[:, d], dr["Cxim"][:, l, d], (), [Cim.b], eng="pool", max_dma_last_dim=4096)
            else:
                dma("Cre", Cre.t[:, d], dr["Cxre"][:, l, d], (), [Cre.b])
                dma("Cim", Cim.t[:, d], dr["Cxim"][:, l, d], (), [Cim.b])
        sm = {}
        for nm in ("dt", "lrdt", "lidt", "mag", "cs", "sn", "are", "aim", "den", "wre", "wim", "x1", "x2", "x3"):
            sm[nm] = A.alloc([2, 16], F32, "s_" + nm)
        halfpi = A.alloc([1], F32, "halfpi")
        memset(halfpi.t, math.pi / 2, [halfpi.b])
        P = par
        act(sm["dt"].t, P["logdt"].t, AF.Exp, [P["logdt"].b], [sm["dt"].b])
        tt(sm["lrdt"].t, P["lamre"].t, sm["dt"].t, ALU.mult, [P["lamre"].b, sm["dt"].b], [sm["lrdt"].b])
        tt(sm["lidt"].t, P["lamim"].t, sm["dt"].t, ALU.mult, [P["lamim"].b, sm["dt"].b], [sm["lidt"].b])
        act(sm["mag"].t, sm["lrdt"].t, AF.Exp, [sm["lrdt"].b], [sm["mag"].b])
        act(sm["sn"].t, sm["lidt"].t, AF.Sin, [sm["lidt"].b], [sm["sn"].b], scale=1.0 / 16)
        act(sm["cs"].t, sm["lidt"].t, AF.Sin, [sm["lidt"].b, halfpi.b], [sm["cs"].b], scale=1.0 / 16, bias=halfpi.t[:, 0:1])
        for _ in range(4):
            tt(sm["x1"].t, sm["cs"].t, sm["cs"].t, ALU.mult, [sm["cs"].b], [sm["x1"].b])
            tt(sm["x2"].t, sm["sn"].t, sm["sn"].t, ALU.mult, [sm["sn"].b], [sm["x2"].b])
            tt(sm["x3"].t, sm["cs"].t, sm["sn"].t, ALU.mult, [sm["cs"].b, sm["sn"].b], [sm["x3"].b])
            tt(sm["cs"].t, sm["x1"].t, sm["x2"].t, ALU.subtract, [sm["x1"].b, sm["x2"].b], [sm["cs"].b])
            ts(sm["sn"].t, sm["x3"].t, 2.0, None, ALU.mult, None, [sm["x3"].b], [sm["sn"].b])
        tt(sm["are"].t, sm["mag"].t, sm["cs"].t, ALU.mult, [sm["mag"].b, sm["cs"].b], [sm["are"].b])
        tt(sm["aim"].t, sm["mag"].t, sm["sn"].t, ALU.mult, [sm["mag"].b, sm["sn"].b], [sm["aim"].b])
        tt(sm["x1"].t, P["lamre"].t, P["lamre"].t, ALU.mult, [P["lamre"].b], [sm["x1"].b])
        tt(sm["x2"].t, P["lamim"].t, P["lamim"].t, ALU.mult, [P["lamim"].b], [sm["x2"].b])
        tt(sm["den"].t, sm["x1"].t, sm["x2"].t, ALU.add, [sm["x1"].b, sm["x2"].b], [sm["den"].b])
        recip(sm["den"].t, sm["den"].t, [sm["den"].b], [sm["den"].b])
        ts(sm["x3"].t, sm["are"].t, -1.0, None, ALU.add, None, [sm["are"].b], [sm["x3"].b])
        tt(sm["x1"].t, sm["x3"].t, P["lamre"].t, ALU.mult, [sm["x3"].b, P["lamre"].b], [sm["x1"].b])
        tt(sm["x2"].t, sm["aim"].t, P["lamim"].t, ALU.mult, [sm["aim"].b, P["lamim"].b], [sm["x2"].b])
        tt(sm["wre"].t, sm["x1"].t, sm["x2"].t, ALU.add, [sm["x1"].b, sm["x2"].b], [sm["wre"].b])
        tt(sm["wre"].t, sm["wre"].t, sm["den"].t, ALU.mult, [sm["wre"].b, sm["den"].b], [sm["wre"].b])
        tt(sm["x1"].t, sm["aim"].t, P["lamre"].t, ALU.mult, [sm["aim"].b, P["lamre"].b], [sm["x1"].b])
        tt(sm["x2"].t, sm["x3"].t, P["lamim"].t, ALU.mult, [sm["x3"].b, P["lamim"].b], [sm["x2"].b])
        tt(sm["wim"].t, sm["x1"].t, sm["x2"].t, ALU.subtract, [sm["x1"].b, sm["x2"].b], [sm["wim"].b])
        tt(sm["wim"].t, sm["wim"].t, sm["den"].t, ALU.mult, [sm["wim"].b, sm["den"].b], [sm["wim"].b])
        bbx = {}
        for ri in ("re", "im"):
            bbx[ri] = A.alloc([2, 16, 2, 16], F32, "bbx" + ri)
            memset(bbx[ri].t, 0.0, [bbx[ri].b])
        tmpb = A.alloc([16], F32, "tmpb")
        for d in range(2):
            for st in range(16):
                wre = sm["wre"].t[:, d, st:st + 1]
                wim = sm["wim"].t[:, d, st:st + 1]
                for gi in range(2):
                    lo, hi = gi * 64, gi * 64 + 64
                    rd = [sm["wre"].b, sm["wim"].b, Bre.b, Bim.b, tmpb.b]
                    sr_ = [sm["wre"].b, sm["wim"].b]
                    ts(tmpb.t[lo:hi, :], Bim.t[lo:hi, d, st, :], wim[lo:hi], None, ALU.mult, None, rd, [tmpb.b], sreads=sr_)
                    stt(bbx["re"].t[lo:hi, d, st, gi, :], Bre.t[lo:hi, d, st, :], wre[lo:hi], tmpb.t[lo:hi, :], ALU.mult, ALU.subtract,
                        rd, [bbx["re"].b], sreads=sr_)
                    ts(tmpb.t[lo:hi, :], Bre.t[lo:hi, d, st, :], wim[lo:hi], None, ALU.mult, None, rd, [tmpb.b], sreads=sr_)
                    stt(bbx["im"].t[lo:hi, d, st, gi, :], Bim.t[lo:hi, d, st, :], wre[lo:hi], tmpb.t[lo:hi, :], ALU.mult, ALU.add,
                        rd, [bbx["im"].b], sreads=sr_)
        BT = {}
        for ri in ("re", "im"):
            BT[ri] = A.alloc([2, 4, 128], BF16, "BT" + ri)
            for d in range(2):
                for q in range(4):
                    p = PS.next()
                    src = bbx[ri].t[:, d, q * 4:(q + 1) * 4, :, :].rearrange("p a b c -> p (a b c)")
                    S.op("pe", lambda e, p=p, src=src: e.transpose(out=p.t[:, 0:128], in_=src, identity=ident.t),
                         [bbx[ri].b, ident.b], [p.b])
                    cp(BT[ri].t[:, d, q, :], p.t[:, 0:128], [p.b], [BT[ri].b])
        BT3 = {}
        bz = A.alloc([4, 2, 16], F32, "bz")
        memset(bz.t, 0.0, [bz.b])
        for ri in ("re", "im"):
            BT3[ri] = A.alloc([2, 4, 128], BF16, "BT3" + ri)
            for d in range(2):
                for q in range(4):
                    cp(bz.t[:, 3, :, :], bbx[ri].t[:, d, q * 4 + 3, :, :], [bbx[ri].b], [bz.b])
                    p = PS.next()
                    src = bz.t.rearrange("p a b c -> p (a b c)")
                    S.op("pe", lambda e, p=p, src=src: e.transpose(out=p.t[:, 0:128], in_=src, identity=ident.t),
                         [bz.b, ident.b], [p.b])
                    cp(BT3[ri].t[:, d, q, :], p.t[:, 0:128], [p.b], [BT3[ri].b])
        ECOS = A.alloc([16, LC], F32, "ECOS")
        ESIN = A.alloc([16, LC], F32, "ESIN")
        carry = A.alloc([16, 2], F32, "carry")
        cb = [Buf("carry%d" % i_) for i_ in range(16)]
        uf = A.rot(2, [4, LC], F32, "uf")
        ub = A.rot(2, [4, LC], BF16, "ub")
        zr = A.rot(4, [LC], F32, "zr")
        zi = A.rot(4, [LC], F32, "zi")
        t1r = A.rot(4, [LC], F32, "st1")
        t2r = A.rot(4, [LC], F32, "st2")
        srr = A.rot(2, [LC], F32, "sr")
        sir = A.rot(2, [LC], F32, "si")
        srb = A.rot(8, [LC], BF16, "srb")
        sib = A.rot(8, [LC], BF16, "sib")
        yt = A.rot(2, [LC], F32, "yt")
        yo = A.rot(2, [LC], F32, "yo")
        g1 = A.rot(2, [LC], F32, "g1")
        gyb = A.rot(2, [LC], BF16, "gyb")
        chunks_f = [(0, CTX, 0)] + [(CTX + i * LC, LC, 1 + i) for i in range(T // LC)]
        for d in range(2):
            rd0 = [sm["cs"].b, sm["sn"].b]
            cp(ECOS.t[:, :, 0], sm["cs"].t[:, d, :], rd0, [ECOS.b])
            cp(ESIN.t[:, :, 0], sm["sn"].t[:, d, :], rd0, [ESIN.b])
            w = 1
            while w < LC:
                for st in range(16):
                    c_ = ECOS.t[:, st, w - 1:w]
                    s_ = ESIN.t[:, st, w - 1:w]
                    a1 = t1r.next()
                    a2 = t2r.next()
                    ts(a1.t[:, :w], ESIN.t[:, st, 0:w], s_, None, ALU.mult, None, [ESIN.b], [a1.b], sreads=[ESIN.b, ECOS.b])
                    ts(a2.t[:, :w], ECOS.t[:, st, 0:w], s_, None, ALU.mult, None, [ECOS.b, ESIN.b], [a2.b], sreads=[ESIN.b, ECOS.b])
                    stt(ECOS.t[:, st, w:2 * w], ECOS.t[:, st, 0:w], c_, a1.t[:, :w], ALU.mult, ALU.subtract, [ECOS.b, a1.b], [ECOS.b], sreads=[ESIN.b, ECOS.b])
                    stt(ESIN.t[:, st, w:2 * w], ESIN.t[:, st, 0:w], c_, a2.t[:, :w], ALU.mult, ALU.add, [ESIN.b, ECOS.b, a2.b], [ESIN.b], sreads=[ESIN.b, ECOS.b])
                w *= 2
            if d == 0:
                order = [(c, False) for c in chunks_f]
            else:
                order = [(chunks_f[0], True)] + [(c, True) for c in reversed(chunks_f[1:])]
            memset(carry.t, 0.0, [carry.b] + cb)
            for ((c0, N, hb), rev) in order:
                isctx = c0 == 0
                U = uf.next()
                Ub = ub.next()
                dma(U.b.name, U.t[:, :, :N], UT[:, c0:c0 + N].rearrange("(q p) n -> p q n", p=128), DBall("UT"), [U.b])
                cp(Ub.t[:, :, :N], U.t[:, :, :N], [U.b], [Ub.b], eng="pool")

                def rv(ap):
                    return ap[:, N - 1::-1] if rev else ap[:, 0:N]
                zbuf = {}
                sbs_all = {q_: [] for q_ in range(4)}

                def stage_p(st):
                    q, sti = st // 4, st % 4
                    lo = sti * 32
                    pr = PS.next()
                    pi = PS.next()
                    if sti < 3:
                        urhs = rv(Ub.t[lo:lo + 32, q, :N]) if rev else Ub.t[lo:lo + 32, q, :N]
                        lre = BT["re"].t[lo:lo + 32, d, q, :]
                        lim = BT["im"].t[lo:lo + 32, d, q, :]
                    else:
                        urhs = rv(Ub.t[64:128, q, :N]) if rev else Ub.t[64:128, q, :N]
                        lre = BT3["re"].t[64:128, d, q, :]
                        lim = BT3["im"].t[64:128, d, q, :]
                    mm(pr.t[:, :N], lre, urhs, True, True, [BT["re"].b, BT3["re"].b, Ub.b], [pr.b])
                    mm(pi.t[:, :N], lim, urhs, True, True, [BT["im"].b, BT3["im"].b, Ub.b], [pi.b])
                    ec = ECOS.t[:, st, :N]
                    es = ESIN.t[:, st, :N]
                    a1 = t1r.next(); a2 = t2r.next(); ZR = zr.next(); ZI = zi.next()
                    tt(a1.t[:, :N], pr.t[:, :N], ec, ALU.mult, [pr.b, ECOS.b], [a1.b])
                    tt(a2.t[:, :N], pi.t[:, :N], es, ALU.mult, [pi.b, ESIN.b], [a2.b])
                    tt(ZR.t[:, :N], a1.t[:, :N], a2.t[:, :N], ALU.add, [a1.b, a2.b], [ZR.b], eng="pool")
                    a1 = t1r.next(); a2 = t2r.next()
                    tt(a1.t[:, :N], pi.t[:, :N], ec, ALU.mult, [pi.b, ECOS.b], [a1.b])
                    tt(a2.t[:, :N], pr.t[:, :N], es, ALU.mult, [pr.b, ESIN.b], [a2.b])
                    tt(ZI.t[:, :N], a1.t[:, :N], a2.t[:, :N], ALU.subtract, [a1.b, a2.b], [ZI.b], eng="pool")
                    zbuf[st] = (ZR, ZI)

                def stage_s(st):
                    ZR, ZI = zbuf[st]
                    rbc = sm["mag"].t[:, d, st:st + 1].to_broadcast([128, N])
                    scan(ZR.t[:, :N], rbc, ZR.t[:, :N], carry.t[:, st, 0:1], [ZR.b, cb[st], sm["mag"].b], [ZR.b])
                    scan(ZI.t[:, :N], rbc, ZI.t[:, :N], carry.t[:, st, 1:2], [ZI.b, cb[st], sm["mag"].b], [ZI.b])

                def stage_r(st):
                    q = st // 4
                    ZR, ZI = zbuf.pop(st)
                    ec = ECOS.t[:, st, :N]
                    es = ESIN.t[:, st, :N]
                    SR = srr.next(); SI = sir.next()
                    a1 = t1r.next(); a2 = t2r.next()
                    tt(a1.t[:, :N], ZR.t[:, :N], ec, ALU.mult, [ZR.b, ECOS.b], [a1.b])
                    tt(a2.t[:, :N], ZI.t[:, :N], es, ALU.mult, [ZI.b, ESIN.b], [a2.b], eng="pool")
                    tt(SR.t[:, :N], a1.t[:, :N], a2.t[:, :N], ALU.subtract, [a1.b, a2.b], [SR.b], eng="pool")
                    a1 = t1r.next(); a2 = t2r.next()
                    tt(a1.t[:, :N], ZR.t[:, :N], es, ALU.mult, [ZR.b, ESIN.b], [a1.b])
                    tt(a2.t[:, :N], ZI.t[:, :N], ec, ALU.mult, [ZI.b, ECOS.b], [a2.b], eng="pool")
                    tt(SI.t[:, :N], a1.t[:, :N], a2.t[:, :N], ALU.add, [a1.b, a2.b], [SI.b], eng="pool")
                    act(carry.t[:, st, 0:1], SR.t[:, N - 1:N], AF.Identity, [SR.b], [cb[st]])
                    act(carry.t[:, st, 1:2], SI.t[:, N - 1:N], AF.Identity, [SI.b], [cb[st]])
                    SRb = srb.next(); SIb = sib.next()
                    act(SRb.t[:, :N], SR.t[:, :N], AF.Identity, [SR.b], [SRb.b])
                    act(SIb.t[:, :N], SI.t[:, :N], AF.Identity, [SI.b], [SIb.b], scale=-1.0)
                    sbs_all[q].append((st, SRb, SIb))
                    if st % 4 == 3:
                        readout(q, sbs_all[q])

                def readout(q, sbs):
                    if isctx and not ctx_out:
                        return
                    py = PSA.next()
                    for i, (st, SRb, SIb) in enumerate(sbs):
                        mm(py.t[:, :N], Cre.t[:, d, st, :], rv(SRb.t[:, :N]) if rev else SRb.t[:, :N], i == 0, False, [Cre.b, SRb.b], [py.b])
                        mm(py.t[:, :N], Cim.t[:, d, st, :], rv(SIb.t[:, :N]) if rev else SIb.t[:, :N], False, i == 3, [Cim.b, SIb.b], [py.b])
                    Y = yt.next()
                    if d == 0:
                        stt(Y.t[:, :N], U.t[:, q, :N], sdT.t[:, l, q:q + 1], py.t[:, :N], ALU.mult, ALU.add, [U.b, sdT.b, py.b], [Y.b])
                        dma(Y.b.name, YT[q * 128:(q + 1) * 128, c0:c0 + N], Y.t[:, :N], [Y.b], [DB("YT", (q, c0))])
                    else:
                        Yo = yo.next()
                        dma(Yo.b.name, Yo.t[:, :N], YT[q * 128:(q + 1) * 128, c0:c0 + N], [DB("YT", (q, c0))], [Yo.b])
                        tt(Y.t[:, :N], Yo.t[:, :N], py.t[:, :N], ALU.add, [Yo.b, py.b], [Y.b])
                        G = g1.next()
                        tt(G.t[:, :N], Y.t[:, :N], Y.t[:, :N], ALU.mult, [Y.b], [G.b], eng="pool")
                        ts(G.t[:, :N], G.t[:, :N], 0.044715, 1.0, ALU.mult, ALU.add, [G.b], [G.b], eng="pool")
                        tt(G.t[:, :N], G.t[:, :N], Y.t[:, :N], ALU.mult, [G.b, Y.b], [G.b], eng="pool")
                        act(G.t[:, :N], G.t[:, :N], AF.Sigmoid, [G.b], [G.b], scale=1.5957691216057308)
                        Gb = gyb.next()
                        tt(Gb.t[:, :N], G.t[:, :N], Y.t[:, :N], ALU.mult, [G.b, Y.b], [Gb.b], eng="pool")
                        dma(Gb.b.name, GYT[q * 128:(q + 1) * 128, c0:c0 + N], Gb.t[:, :N], [Gb.b], [DB("GYT", (q, c0))])
                for kk in range(16 + 2):
                    if kk < 16:
                        stage_p(kk)
                    if 1 <= kk <= 16:
                        stage_s(kk - 1)
                    if kk >= 2:
                        stage_r(kk - 2)
        for nm in ("are", "aim", "wre", "wim", "mag", "cs", "sn"):
            dbg_dump(nm, sm[nm])
        dbg_dump("ECOS", ECOS)
        dbg_dump("ESIN", ESIN)

        dbg_dump("BTre", BT["re"], BF16)
        dbg_dump("bbxre", bbx["re"])
        dbg_dump("Bre", Bre)
        S.barrier()
        A.release(m)

    def merge_phase(l, ctx_out):
        m = A.mark()
        Wg = A.alloc([8, 3072], BF16, "Wg"); wload(Wg, dr["w_in"][l][:, 1952:5024])
        Wmo = A.alloc([4, D], BF16, "Wmo"); wload(Wmo, dr["mla_w_o"][l])
        Wgl = A.alloc([4, 2 * D], BF16, "Wgl"); wload(Wgl, dr["ssm_w_glu"][l])
        Wgo = A.alloc([4, D], BF16, "Wgo"); wload(Wgo, dr["gqa_w_o"][l])
        Wo = A.alloc([8, D], BF16, "Wo"); wload(Wo, dr["w_out"][l])
        Hr = A.rot(2, [8, 512], F32, "H")
        xmr = A.rot(2, [8, 512], BF16, "xm")
        inr = A.rot(2, [3, 4, 512], BF16, "ins")
        mg = A.alloc([8, 512], BF16, "mg")
        sgr = A.rot(4, [512], BF16, "sg")
        b1r = A.rot(2, [512], F32, "b1")
        accr = A.rot(2, [512], F32, "acc")
        t3r = A.rot(2, [512], F32, "t3")
        for (c0, N, isctx, hb) in tiles512:
            if isctx and not ctx_out:
                continue
            k = 1 if isctx else 0
            H = Hr.next(); xm = xmr.next(); I = inr.next()
            dma(H.b.name, H.t[:, :, :N], hT[:, c0:c0 + N].rearrange("(j p) n -> p j n", p=128), [DB("hT", hb)], [H.b])
            dma(xm.b.name, xm.t[:, :, :N], XMT[:, c0:c0 + N].rearrange("(j p) n -> p j n", p=128), DBall("XMT"), [xm.b])
            for i, (src, nm) in enumerate(((OMT, "OMT"), (GYT, "GYT"), (OGT, "OGT"))):
                dma(I.b.name, I.t[:, i, :, :N], src[:, c0:c0 + N].rearrange("(j p) n -> p j n", p=128), DBall(nm), [I.b])
            for dd in range(8):
                cs = slice(dd * 128, (dd + 1) * 128)
                sg = []
                for gi in range(3):
                    p = PS.next()
                    for kc in range(8):
                        mm(p.t[:, :N], Wg.t[:, kc, gi * D + dd * 128:gi * D + (dd + 1) * 128], xm.t[:, kc, :N], kc == 0, kc == 7, [Wg.b, xm.b], [p.b])
                    s_ = sgr.next()
                    act(s_.t[:, :N], p.t[:, :N], AF.Sigmoid, [p.b], [s_.b])
                    sg.append(s_)
                p0 = PS.next()
                for kc in range(4):
                    mm(p0.t[:, :N], Wmo.t[:, kc, cs], I.t[:, 0, kc, :N], kc == 0, kc == 3, [Wmo.b, I.b], [p0.b])
                acc = accr.next()
                tt(acc.t[:, :N], sg[0].t[:, :N], p0.t[:, :N], ALU.mult, [sg[0].b, p0.b], [acc.b])
                pa = PS.next(); pg = PS.next()
                for kc in range(4):
                    mm(pa.t[:, :N], Wgl.t[:, kc, cs], I.t[:, 1, kc, :N], kc == 0, kc == 3, [Wgl.b, I.b], [pa.b])
                for kc in range(4):
                    mm(pg.t[:, :N], Wgl.t[:, kc, D + dd * 128:D + (dd + 1) * 128], I.t[:, 1, kc, :N], kc == 0, kc == 3, [Wgl.b, I.b], [pg.b])
                s3 = sgr.next()
                act(s3.t[:, :N], pg.t[:, :N], AF.Sigmoid, [pg.b], [s3.b])
                b1 = b1r.next()
                tt(b1.t[:, :N], s3.t[:, :N], pa.t[:, :N], ALU.mult, [s3.b, pa.b], [b1.b])
                t3 = t3r.next()
                tt(t3.t[:, :N], b1.t[:, :N], sg[1].t[:, :N], ALU.mult, [b1.b, sg[1].b], [t3.b], eng="pool")
                p2 = PS.next()
                for kc in range(4):
                    mm(p2.t[:, :N], Wgo.t[:, kc, cs], I.t[:, 2, kc, :N], kc == 0, kc == 3, [Wgo.b, I.b], [p2.b])
                b2 = b1r.next()
                tt(b2.t[:, :N], sg[2].t[:, :N], p2.t[:, :N], ALU.mult, [sg[2].b, p2.b], [b2.b])
                tt(acc.t[:, :N], acc.t[:, :N], t3.t[:, :N], ALU.add, [acc.b, t3.b], [acc.b], eng="pool")
                tt(mg.t[:, dd, :N], acc.t[:, :N], b2.t[:, :N], ALU.add, [acc.b, b2.b], [mg.b], eng="pool")
            for dd in range(8):
                po = PS.next()
                for kc in range(8):
                    mm(po.t[:, :N], Wo.t[:, kc, dd * 128:(dd + 1) * 128], mg.t[:, kc, :N], kc == 0, kc == 7, [Wo.b, mg.b], [po.b])
                stt(H.t[:, dd, :N], po.t[:, :N], hgT.t[:, l, 1, dd, k:k + 1], H.t[:, dd, :N], ALU.mult, ALU.add, [po.b, hgT.b, H.b], [H.b])
            dma(H.b.name, hT[:, c0:c0 + N].rearrange("(j p) n -> p j n", p=128), H.t[:, :, :N], [H.b], [DB("hT", hb)])
        S.barrier()
        A.release(m)

    def final_phase():
        m = A.mark()
        Hr = A.rot(2, [8, 512], F32, "H")
        ot = A.alloc([8, 512], F32, "ot")
        sq = A.alloc([8, 512], BF16, "sq")
        rstd = A.alloc([512], F32, "rstd")
        for (c0, N, isctx, hb) in tiles512:
            if isctx:
                continue
            H = Hr.next()
            dma(H.b.name, H.t[:, :, :N], hT[:, c0:c0 + N].rearrange("(j p) n -> p j n", p=128), [DB("hT", hb)], [H.b])
            norm_mod(H, N, 8, lambda j: fnT.t[:, j:j + 1], None, ot, sq, rstd, None, D, extra_reads=[fnT.b])
            dma(ot.b.name, outT[:, c0 - CTX:c0 - CTX + N].rearrange("(j p) n -> p j n", p=128), ot.t[:, :, :N],
                [ot.b], [DB("outT", 0)])
        S.op("sp", None, reads=DBall("outT"))
        A.release(m)

    PH = os.environ.get("PHASES", "fpmsgeF")

    def layer_body():
        if "f" in PH:
            ffn_phase(0, 0, dr["ffn1_w13"][0], dr["ffn1_w2"][0], False, False)
        if "p" in PH:
            proj_phase(0)
        if "m" in PH:
            mla_phase(0, True)
        if "s" in PH:
            ssm_phase(0, True)
        if "g" in PH:
            gqa_phase(0, True)
        if "e" in PH:
            merge_phase(0, True)
        if "F" in PH:
            ffn_phase(0, 2, dr["ffn2_w13"][0], dr["ffn2_w2"][0], False, False)

    scheds = [S]
    if mode == "final":
        final_phase()
        S.emit()
    elif mode == "layer":
        layer_body()
        S.op("sp", None, reads=DBall("hT"))
        S.emit()
    elif mode == "loop":
        S.emit()
        Buf.reset_all()
        S = Sched(nc)
        scheds.append(S)
        setup_layers()
        layer_body()
        for k in layered:
            for l in range(L - 1):
                dma("shift", lslice(dr[k], k, l), lslice(dr[k], k, l + 1), [DB("wc_" + k, 0)], [DB("wc_" + k, 0)])
        S.barrier()
        with nc.Fori(0, L):
            S.emit()
            S.reset_sems()
        Buf.reset_all()
        S = Sched(nc)
        scheds.append(S)
        final_phase()
        S.emit()
    else:
        for l in range(L):
            ctx_out = l < L - 1
            ffn_phase(l, 0, dr["ffn1_w13"][l], dr["ffn1_w2"][l], False, False)
            proj_phase(l)
            mla_phase(l, ctx_out)
            ssm_phase(l, ctx_out)
            gqa_phase(l, ctx_out)
            merge_phase(l, ctx_out)
            ffn_phase(l, 2, dr["ffn2_w13"][l], dr["ffn2_w2"][l], not ctx_out, l == L - 1)
        S.op("sp", None, reads=DBall("outT"))
        S.emit()
    nc._keep = scheds
    return nc


_CACHE = {}


def run(inputs, T=None, L=None, debug=False, mode="loop"):
    x = np.asarray(inputs["x"])
    B = x.shape[0]
    T = T or x.shape[1]
    L = L or np.asarray(inputs["ada_w"]).shape[0]
    sh, pcs = host_prep(inputs, T, L)
    shapes = {k: (v.shape, v.dtype) for k, v in list(sh.items()) + list(pcs[0].items())}
    key = (T, L, debug, mode)
    if key not in _CACHE:
        _CACHE[key] = build(T, L, shapes, debug, mode=mode)
    nc = _CACHE[key]
    in_maps = []
    for c in range(8):
        mp = dict(sh)
        mp.update(pcs[c % B])
        in_maps.append(mp)
    res = run_bass_kernel_spmd(nc, in_maps, core_ids=list(range(8)))
    out = np.stack([np.ascontiguousarray(res.results[b]["outT"].T) for b in range(B)], 0)
    return out.astype(np.float32), res


def run_layers(inputs):
    x = np.asarray(inputs["x"])
    B, T = x.shape[0], x.shape[1]
    L = np.asarray(inputs["ada_w"]).shape[0]
    h = [np.ascontiguousarray(np.concatenate([np.asarray(inputs["ctx"][b], np.float32).T, np.asarray(x[b], np.float32).T], 1))
         for b in range(B)]
    per_layer_keys = [k for k, v in inputs.items() if k not in ("x", "c", "ctx", "c_ctx", "final_norm")]
    nc_layer = None
    for l in range(L):
        inp_l = dict(inputs)
        for k in per_layer_keys:
            inp_l[k] = np.asarray(inputs[k])[l:l + 1]
        sh, pcs = host_prep(inp_l, T, 1)
        for pc in pcs:
            del pc["xT"], pc["ctxT"]
        del sh["fnT"]
        if nc_layer is None:
            shapes = {k: (v.shape, v.dtype) for k, v in list(sh.items()) + list(pcs[0].items())}
            shapes["h_in"] = ((D, T + CTX), np.float32)
            nc_layer = build(T, 1, shapes, False, mode="layer")
        in_maps = []
        for c in range(8):
            mp = dict(sh)
            mp.update(pcs[c % B])
            mp["h_in"] = h[c % B]
            in_maps.append(mp)
        res = run_bass_kernel_spmd(nc_layer, in_maps, core_ids=list(range(8)))
        h = [np.ascontiguousarray(res.results[b]["hT"]) for b in range(B)]
    shapes = {"fnT": ((128, 8), np.float32), "h_in": ((D, T + CTX), np.float32)}
    nc_fin = build(T, 1, shapes, False, mode="final")
    fn = fm(inputs["final_norm"], 8)
    res = run_bass_kernel_spmd(nc_fin, [{"fnT": fn, "h_in": h[c % B]} for c in range(8)], core_ids=list(range(8)))
    out = np.stack([np.ascontiguousarray(res.results[b]["outT"].T) for b in range(B)], 0)
    return out.astype(np.float32)


def kernel(**inputs):
    out, _ = run(inputs, mode="loop")
    return out
```
